# Optimizing a Trainium2 kernel written in Bass

```python
import jax, jax.numpy as jnp
from jax import lax
import numpy as np

D_MODEL = 1024
BATCH = 2
SEQ = 8192
DEPTH = 4
DEC_BATCH = 32
DEC_SEQ = 64
PAST_LEN = 4096

CHUNK = 64
N_EVEN = DEPTH - DEPTH // 2
N_ODD = DEPTH // 2
A_HEADS = 8
A_HEAD_DIM = D_MODEL // 16
A_WIDTH = A_HEADS * A_HEAD_DIM
A_BLOCK = 128
POOL_WINDOWS = (2, 4, 8, 16)
B_GROUPS = len(POOL_WINDOWS)
B_GROUP_DIM = D_MODEL // 8
B_WIDTH = B_GROUPS * B_GROUP_DIM
POOL_HIST = max(POOL_WINDOWS) - 1
C_WIDTH = D_MODEL // 2
C_KERNEL = 31
D_WIDTH = D_MODEL // 2
D_KERNEL = 3
EVEN_IN = 2 * A_WIDTH + B_WIDTH
ODD_IN = 2 * C_WIDTH + 3 * D_WIDTH
MIX_WIDTH = A_WIDTH + B_WIDTH
D_FF = 4 * D_MODEL
EPS = 1e-6

kernel_name = "hybrid_chunkmlp_pool_conformer_shortconv_step"


def rms_norm(x, g):
    xf = x.astype(jnp.float32)
    y = xf * lax.rsqrt(jnp.mean(xf * xf, axis=-1, keepdims=True) + EPS)
    return (y * g.astype(jnp.float32)).astype(x.dtype)


def layer_norm(x, g, b):
    xf = x.astype(jnp.float32)
    mu = jnp.mean(xf, axis=-1, keepdims=True)
    xc = xf - mu
    y = xc * lax.rsqrt(jnp.mean(xc * xc, axis=-1, keepdims=True) + EPS)
    return (y * g.astype(jnp.float32) + b.astype(jnp.float32)).astype(x.dtype)


def causal_dwconv(x_ext, w):
    c = w.shape[1]
    return lax.conv_general_dilated(
        x_ext, w[:, None, :].astype(x_ext.dtype), window_strides=(1,), padding="VALID",
        dimension_numbers=("NWC", "WIO", "NWC"), feature_group_count=c)


def chunk_gating(u, v, w_s, b_s):
    n, t = v.shape[0], v.shape[1]
    nb = -(-t // A_BLOCK)
    pad = nb * A_BLOCK - t
    vp = jnp.pad(v, ((0, 0), (0, pad), (0, 0), (0, 0))).reshape(n, nb, A_BLOCK, A_HEADS, A_HEAD_DIM)
    cidx = jnp.arange(A_BLOCK) // CHUNK
    mask = (cidx[None, :] <= cidx[:, None]).astype(w_s.dtype)
    mixed = jnp.einsum("hij,nbjhc->nbihc", w_s * mask[None], vp)
    mixed = mixed + b_s.T[None, None, :, :, None]
    mixed = mixed.reshape(n, nb * A_BLOCK, A_HEADS, A_HEAD_DIM)[:, :t]
    return u * mixed


def multiscale_pool(xb_ext, pos0):
    n, te, _ = xb_ext.shape
    t = te - POOL_HIST
    xf = xb_ext.astype(jnp.float32)
    csum = jnp.concatenate([jnp.zeros((n, 1, B_WIDTH), jnp.float32), jnp.cumsum(xf, axis=1)], axis=1)
    pos = pos0 + jnp.arange(t)
    outs = []
    for g, w in enumerate(POOL_WINDOWS):
        cg = csum[..., g * B_GROUP_DIM:(g + 1) * B_GROUP_DIM]
        end = cg[:, POOL_HIST + 1:POOL_HIST + 1 + t]
        start = cg[:, POOL_HIST + 1 - w:POOL_HIST + 1 - w + t]
        cnt = jnp.minimum(w, pos + 1).astype(jnp.float32)[None, :, None]
        outs.append((end - start) / cnt)
    return jnp.concatenate(outs, axis=-1) - xf[:, POOL_HIST:]


def even_mixer(h, pos0, hist_pool, w_in, w_out, a_w_s, a_b_s, a_ln_g, a_ln_b, b_pool_w, b_scale):
    n, t, _ = h.shape
    z = h @ w_in
    a = jax.nn.gelu(z[..., :2 * A_WIDTH])
    u = a[..., :A_WIDTH]
    v = layer_norm(a[..., A_WIDTH:], a_ln_g, a_ln_b)
    ya = chunk_gating(u.reshape(n, t, A_HEADS, A_HEAD_DIM), v.reshape(n, t, A_HEADS, A_HEAD_DIM),
                      a_w_s, a_b_s).reshape(n, t, A_WIDTH)
    xb_ext = jnp.concatenate([hist_pool.astype(z.dtype), z[..., 2 * A_WIDTH:]], axis=1)
    pooled = multiscale_pool(xb_ext, pos0).astype(h.dtype).reshape(n, t, B_GROUPS, B_GROUP_DIM)
    yb = jnp.einsum("ntgc,gcd->ntgd", pooled, b_pool_w).reshape(n, t, B_WIDTH) * b_scale
    y = jnp.concatenate([ya, yb], axis=-1) @ w_out
    return y, v, xb_ext[:, -POOL_HIST:]


def odd_mixer(h, hist_c, hist_d, w_in, w_out, c_conv_w, c_conv_b, c_ln_g, c_ln_b, d_conv_w):
    z = h @ w_in
    c_in = z[..., :C_WIDTH] * jax.nn.sigmoid(z[..., C_WIDTH:2 * C_WIDTH])
    c_ext = jnp.concatenate([hist_c.astype(z.dtype), c_in], axis=1)
    c = causal_dwconv(c_ext, c_conv_w) + c_conv_b
    c = jax.nn.silu(layer_norm(c, c_ln_g, c_ln_b))
    o = 2 * C_WIDTH
    gate_b = z[..., o:o + D_WIDTH]
    gate_c = z[..., o + D_WIDTH:o + 2 * D_WIDTH]
    xt = z[..., o + 2 * D_WIDTH:]
    d_ext = jnp.concatenate([hist_d.astype(z.dtype), gate_c * xt], axis=1)
    d = gate_b * causal_dwconv(d_ext, d_conv_w)
    y = jnp.concatenate([c, d], axis=-1) @ w_out
    return y, c_ext[:, -(C_KERNEL - 1):], d_ext[:, -(D_KERNEL - 1):]


def trunk(x, pos0, hist_pool, hist_c, hist_d, norm_mix_g, norm_ffn_g, final_norm_g,
          w_in_even, w_out_even, a_w_s, a_b_s, a_ln_g, a_ln_b, b_pool_w, b_scale,
          w_in_odd, w_out_odd, c_conv_w, c_conv_b, c_ln_g, c_ln_b, d_conv_w, w_ff1, w_ff2):
    vs, pools, cs, ds = [], [], [], []
    for layer in range(DEPTH):
        i = layer // 2
        h = rms_norm(x, norm_mix_g[layer])
        if layer % 2 == 0:
            y, v_rows, pool_rows = even_mixer(h, pos0, hist_pool[i], w_in_even[i], w_out_even[i],
                                              a_w_s[i], a_b_s[i], a_ln_g[i], a_ln_b[i],
                                              b_pool_w[i], b_scale[i])
            vs.append(v_rows)
            pools.append(pool_rows)
        else:
            y, c_rows, d_rows = odd_mixer(h, hist_c[i], hist_d[i], w_in_odd[i], w_out_odd[i],
                                          c_conv_w[i], c_conv_b[i], c_ln_g[i], c_ln_b[i], d_conv_w[i])
            cs.append(c_rows)
            ds.append(d_rows)
        x = x + y
        h = rms_norm(x, norm_ffn_g[layer])
        x = x + jnp.square(jax.nn.relu(h @ w_ff1[layer])) @ w_ff2[layer]
    return rms_norm(x, final_norm_g), jnp.stack(vs), jnp.stack(pools), jnp.stack(cs), jnp.stack(ds)


def setup_inputs(seed: int = 0) -> dict:
    key = jax.random.key(seed)
    ks = jax.random.split(key, 32)
    f32 = jnp.float32

    def nrm(k, shape, scale):
        return jax.random.normal(k, shape, f32) * scale

    return {
        "x_prompt": nrm(ks[0], (BATCH, SEQ, D_MODEL), 1.0),
        "x_sample": nrm(ks[1], (DEC_BATCH, DEC_SEQ, D_MODEL), 1.0),
        "state_pool": nrm(ks[2], (N_EVEN, DEC_BATCH, POOL_HIST, B_WIDTH), 1.0),
        "state_conv_c": nrm(ks[3], (N_ODD, DEC_BATCH, C_KERNEL - 1, C_WIDTH), 0.5),
        "state_conv_d": nrm(ks[4], (N_ODD, DEC_BATCH, D_KERNEL - 1, D_WIDTH), 1.0),
        "norm_mix_g": 1.0 + nrm(ks[5], (DEPTH, D_MODEL), 0.05),
        "norm_ffn_g": 1.0 + nrm(ks[6], (DEPTH, D_MODEL), 0.05),
        "final_norm_g": 1.0 + nrm(ks[7], (D_MODEL,), 0.05),
        "w_in_even": nrm(ks[8], (N_EVEN, D_MODEL, EVEN_IN), D_MODEL ** -0.5),
        "w_out_even": nrm(ks[9], (N_EVEN, MIX_WIDTH, D_MODEL), 0.5 * MIX_WIDTH ** -0.5),
        "a_w_s": nrm(ks[10], (N_EVEN, A_HEADS, A_BLOCK, A_BLOCK), A_BLOCK ** -0.5),
        "a_b_s": 1.0 + nrm(ks[11], (N_EVEN, A_HEADS, A_BLOCK), 0.05),
        "a_ln_g": 1.0 + nrm(ks[12], (N_EVEN, A_WIDTH), 0.05),
        "a_ln_b": nrm(ks[13], (N_EVEN, A_WIDTH), 0.02),
        "b_pool_w": nrm(ks[14], (N_EVEN, B_GROUPS, B_GROUP_DIM, B_GROUP_DIM), B_GROUP_DIM ** -0.5),
        "b_scale": 1.0 + nrm(ks[15], (N_EVEN, B_WIDTH), 0.1),
        "w_in_odd": nrm(ks[16], (N_ODD, D_MODEL, ODD_IN), D_MODEL ** -0.5),
        "w_out_odd": nrm(ks[17], (N_ODD, MIX_WIDTH, D_MODEL), 0.5 * MIX_WIDTH ** -0.5),
        "c_conv_w": nrm(ks[18], (N_ODD, C_KERNEL, C_WIDTH), C_KERNEL ** -0.5),
        "c_conv_b": nrm(ks[19], (N_ODD, C_WIDTH), 0.02),
        "c_ln_g": 1.0 + nrm(ks[20], (N_ODD, C_WIDTH), 0.05),
        "c_ln_b": nrm(ks[21], (N_ODD, C_WIDTH), 0.02),
        "d_conv_w": nrm(ks[22], (N_ODD, D_KERNEL, D_WIDTH), D_KERNEL ** -0.5),
        "w_ff1": nrm(ks[23], (DEPTH, D_MODEL, D_FF), D_MODEL ** -0.5),
        "w_ff2": nrm(ks[24], (DEPTH, D_FF, D_MODEL), 0.5 * D_FF ** -0.5),
    }


def reference(x_prompt, x_sample, state_pool, state_conv_c, state_conv_d,
              norm_mix_g, norm_ffn_g, final_norm_g,
              w_in_even, w_out_even, a_w_s, a_b_s, a_ln_g, a_ln_b, b_pool_w, b_scale,
              w_in_odd, w_out_odd, c_conv_w, c_conv_b, c_ln_g, c_ln_b, d_conv_w, w_ff1, w_ff2):
    nb = x_prompt.shape[0]
    dt = x_prompt.dtype
    zero_pool = jnp.zeros((N_EVEN, nb, POOL_HIST, B_WIDTH), dt)
    zero_c = jnp.zeros((N_ODD, nb, C_KERNEL - 1, C_WIDTH), dt)
    zero_d = jnp.zeros((N_ODD, nb, D_KERNEL - 1, D_WIDTH), dt)
    y_prompt, _, new_pool_p, new_conv_c_p, new_conv_d_p = trunk(
        x_prompt, 0, zero_pool, zero_c, zero_d, norm_mix_g, norm_ffn_g, final_norm_g,
        w_in_even, w_out_even, a_w_s, a_b_s, a_ln_g, a_ln_b, b_pool_w, b_scale,
        w_in_odd, w_out_odd, c_conv_w, c_conv_b, c_ln_g, c_ln_b, d_conv_w, w_ff1, w_ff2)
    y_sample, new_a_v_s, new_pool_s, new_conv_c_s, new_conv_d_s = trunk(
        x_sample, PAST_LEN, state_pool, state_conv_c, state_conv_d, norm_mix_g, norm_ffn_g, final_norm_g,
        w_in_even, w_out_even, a_w_s, a_b_s, a_ln_g, a_ln_b, b_pool_w, b_scale,
        w_in_odd, w_out_odd, c_conv_w, c_conv_b, c_ln_g, c_ln_b, d_conv_w, w_ff1, w_ff2)
    return (y_prompt, y_sample, new_pool_p, new_conv_c_p, new_conv_d_p,
            new_a_v_s, new_pool_s, new_conv_c_s, new_conv_d_s)
```

```python
import numpy as np
from contextlib import ExitStack
import concourse.bass as bass
import concourse.mybir as mybir
from concourse.bass_utils import run_bass_kernel_spmd

F32 = mybir.dt.float32
BF16 = mybir.dt.bfloat16
AF = mybir.ActivationFunctionType
ALU = mybir.AluOpType

P = 128
D = 1024
KC = 8
NCORE = 8
TS, TH, TM = 256, 128, 2048
T = TS + TH + TM
TILES = [(0, 384)] + [(384 + 512 * i, 512) for i in range(4)]
NT = len(TILES)
DEPTH = 4
EPS = 1e-6
NSLOT = 5
DG_ENG = ("dve", "pool", "dve", "act")
A_OFF = "dve"
SLOTW = 4096
EXTW = 576

C_MIXG, C_FFNG, C_FING, C_BSC, C_CCB, C_CLG, C_CLB = 0, 32, 64, 72, 80, 88, 96
C_CCW, C_DCW, C_MASK, C_ICNT = 104, 352, 376, 377
C_ALG, C_ALB = 448, 456
NPT = 464


class Sched:
    ENG = ("pe", "act", "dve", "pool", "sp")

    def __init__(self):
        self.ops = []
        self.lastw = {}
        self.readers = {}
        self.lastdma = {}

    def _add(self, eng, fn, r, w, kind, key=None):
        i = len(self.ops)
        deps = set()
        for u in r:
            j = self.lastw.get(u)
            if j is not None:
                deps.add(j)
        for u in w:
            j = self.lastw.get(u)
            if j is not None:
                deps.add(j)
            deps.update(self.readers.get(u, ()))
        if kind == "d":
            j = self.lastdma.get(key)
            if j is not None:
                deps.add(j)
            self.lastdma[key] = i
        for u in r:
            self.readers.setdefault(u, []).append(i)
        for u in w:
            self.lastw[u] = i
            self.readers[u] = []
        self.ops.append(dict(eng=eng, fn=fn, deps=deps, kind=kind, key=key, ev=None))
        return i

    def op(self, eng, fn, r=(), w=()):
        return self._add(eng, fn, r, w, "c")

    def dma(self, eng, fn, key, r=(), w=()):
        return self._add(eng, fn, r, w, "d", key)

    def emit(self, nc, stack):
        ops = self.ops
        n = len(ops)
        need = [False] * n
        for op in ops:
            for d in op["deps"]:
                od = ops[d]
                if od["kind"] == "c":
                    if od["eng"] == "pe" and op["eng"] == "pe" and op["kind"] == "c":
                        continue
                    need[d] = True
        cnt = {e: 0 for e in self.ENG}
        dcnt = {}
        for i, op in enumerate(ops):
            if op["kind"] == "c":
                if need[i]:
                    cnt[op["eng"]] += 1
                    op["ev"] = (("eng", op["eng"]), cnt[op["eng"]])
            else:
                k = op["key"]
                dcnt[k] = dcnt.get(k, 0) + 16
                op["ev"] = (("dma", k), dcnt[k])
        sems = {}
        for e in self.ENG:
            sems[("eng", e)] = stack.enter_context(nc.semaphore("s_" + e))
        for idx, k in enumerate(dcnt):
            sems[("dma", k)] = stack.enter_context(nc.semaphore("d_%d" % idx))
        block = stack.enter_context(nc.Block())
        by_eng = {e: [] for e in self.ENG}
        for i, op in enumerate(ops):
            by_eng[op["eng"]].append(i)

        def run(eng_name, eh):
            waited = {}
            for i in by_eng[eng_name]:
                op = ops[i]
                w = {}
                for d in op["deps"]:
                    od = ops[d]
                    if od["kind"] == "c" and od["eng"] == "pe" and eng_name == "pe" and op["kind"] == "c":
                        continue
                    s, v = od["ev"]
                    if v > w.get(s, 0):
                        w[s] = v
                for s, v in w.items():
                    if waited.get(s, 0) < v:
                        eh.wait_ge(sems[s], v)
                        waited[s] = v
                nm, a_, kw_ = op["fn"]
                ins = getattr(eh, nm)(*a_, **kw_)
                if op["kind"] == "c":
                    if need[i]:
                        ins.then_inc(sems[("eng", eng_name)], 1)
                else:
                    ins.then_inc(sems[("dma", op["key"])], 16)
            if eng_name == "sp":
                for k, v in dcnt.items():
                    eh.wait_ge(sems[("dma", k)], v)

        @block.tensor
        def _(e):
            run("pe", e)

        @block.scalar
        def _(e):
            run("act", e)

        @block.vector
        def _(e):
            run("dve", e)

        @block.gpsimd
        def _(e):
            run("pool", e)

        @block.sync
        def _(e):
            run("sp", e)


def build_program(nlayers=DEPTH):
    nc = bass.Bass("TRN2", target_bir_lowering=False)
    stack = ExitStack()
    S = Sched()

    def OP(eng, name, r, w, *a, **kw):
        S._add(eng, (name, a, kw), r, w, "c")

    def DMA(eng, key, r, w, out, in_):
        S._add(eng, ("dma_start", (), dict(out=out, in_=in_)), r, w, "d", key)

    def din(name, shape, dt=F32):
        return nc.dram_tensor(name, shape, dt, kind="ExternalInput").ap()

    def dout(name, shape, dt=F32):
        return nc.dram_tensor(name, shape, dt, kind="ExternalOutput").ap()

    xin = din("xin", [P, KC * T])
    ptab_d = din("ptab", [P, NPT])
    stp_d = din("stp", [P, 2 * 4 * 4 * 15])
    stc_d = din("stc", [P, 2 * 4 * 4 * 30])
    std_d = din("std", [P, 2 * 4 * 4 * 2])
    wTm_d = din("wTm", [P, 2 * 8 * 128])
    wTs_d = din("wTs", [P, 2 * 8 * 128])
    gmask_d = din("gmask", [P, 3 * 128])
    btab_d = din("btab", [P, 2 * 4 * 128])
    lntab_d = din("lntab", [P, 2 * 2 * 512])
    poolw_d = din("poolw", [P, 2 * 4 * 128])
    w_in_even = din("w_in_even", [2, 1024, 1536])
    w_out_even = din("w_out_even", [2, 1024, 1024])
    w_in_odd = din("w_in_odd", [2, 1024, 2560])
    w_out_odd = din("w_out_odd", [2, 1024, 1024])
    w_ff1 = din("w_ff1", [4, 1024, 4096])
    w_ff2 = din("w_ff2", [4, 4096, 1024])
    yT = dout("yT", [P, KC * T])
    o_pool = dout("o_pool", [P, 2 * 5 * 4 * 15])
    o_c = dout("o_c", [P, 2 * 5 * 4 * 30])
    o_d = dout("o_d", [P, 2 * 5 * 4 * 2])
    o_v = dout("o_v", [P, 2 * 2 * 512])

    def sb(name, shape, dt):
        return stack.enter_context(nc.sbuf_tensor(name, shape, dt))

    x_sb = sb("x_sb", [P, KC * T], F32)
    h_sb = sb("h_sb", [P, KC * T], BF16)
    ring = sb("ring", [P, NSLOT * SLOTW], BF16)
    ext = sb("ext", [P, 4 * EXTW], F32)
    wmt = sb("wmt", [P, 2 * 8 * 128], BF16)
    gmask = sb("gmask_sb", [P, 3 * 128], BF16)
    ident = gmask[:, 256:384]
    btab = sb("btab_sb", [P, 4 * 128], F32)
    lntab = sb("lntab_sb", [P, 2 * 512], F32)
    poolw = sb("poolw_sb", [P, 4 * 128], BF16)
    ptab = sb("ptab_sb", [P, NPT], F32)
    ones = sb("ones", [P, P], BF16)
    small = sb("small", [P, 64], F32)
    nsq = sb("nsq", [P, 2 * 512], BF16)
    nrs = sb("nrs", [P, 512], F32)
    A32W = 544
    ar32 = sb("ar32", [P, 8 * A32W], F32)
    ar16 = sb("ar16", [P, 8 * 512], BF16)
    ps = stack.enter_context(nc.psum_tensor("ps", [P, 8 * 512], F32))

    def xs(k, c0, n):
        return x_sb[:, k * T + c0:k * T + c0 + n]

    def hs(k, c0, n):
        return h_sb[:, k * T + c0:k * T + c0 + n]

    def A32(i, n=512):
        return ar32[:, i * A32W:i * A32W + n]

    def A16(i, n=512):
        return ar16[:, i * 512:i * 512 + n]

    def pt(c, n=1):
        return ptab[:, c:c + n]

    bank_pool = [list(range(8))]
    bank_ctr = [0]

    def set_pool(lst):
        bank_pool[0] = list(lst)
        bank_ctr[0] = 0

    def newbank():
        pl = bank_pool[0]
        b = pl[bank_ctr[0] % len(pl)]
        bank_ctr[0] += 1
        return b

    def PS(b, n=512, p0=0, p1=P, c0=0):
        return ps[p0:p1, b * 512 + c0:b * 512 + c0 + n]

    def v3(ap, nseg):
        if nseg == 1:
            return ap.unsqueeze(1)
        return ap.rearrange("p (s l) -> p s l", s=nseg)

    def extv(g, j, H):
        nseg, L = (6, 64) if j == 0 else (1, 512)
        W = H + L
        a = ext[:, g * EXTW:g * EXTW + nseg * W]
        return v3(a, nseg), nseg, L

    U = lambda *a: tuple(a)

    slab_ctr = [0]

    def load_slab(src_ap, kk, deps=()):
        i = slab_ctr[0]
        slab_ctr[0] += 1
        s = i % NSLOT
        dst = ring[:, s * SLOTW:(s + 1) * SLOTW].rearrange("p (k n) -> p k n", k=kk)
        DMA("pool", U("ring", s), list(deps), [U("ring", s)], dst, src_ap)
        return s

    def slab_in(wd, l, n0):
        return wd[l].rearrange("(k p) n -> p k n", p=P)[:, :, n0:n0 + 512]

    def slab_rows(wd, l, r0):
        return wd[l, r0:r0 + 512, :].rearrange("(k p) n -> p k n", p=P)

    def RA(s, k, m):
        o = s * SLOTW + k * 512 + m * 128
        return ring[:, o:o + 128]

    def RAfull(s, k):
        o = s * SLOTW + k * 512
        return ring[:, o:o + 512]

    def RB(s, k, m):
        o = s * SLOTW + k * 1024 + m * 128
        return ring[:, o:o + 128]

    DMA("sp", "ptab", [], ["ptab"], ptab[:], ptab_d)
    def xload(j, deps=()):
        c0, n = TILES[j]
        src = xin.rearrange("p (k t) -> p k t", k=KC)[:, :, c0:c0 + n]
        dst = x_sb[:].rearrange("p (k t) -> p k t", k=KC)[:, :, c0:c0 + n]
        DMA("sp", U("xin", j), list(deps), [U("x", j, k) for k in range(KC)], dst, src)

    xload(0)
    xload(1)
    OP("pool", "memset", [], ["ones"], ones[:], 1.0)
    DMA("pool", "gmask", [], ["gmask", "ident"], gmask[:], gmask_d)

    nsq_ctr = [0]

    def rmsnorm_tile(j, out_fn):
        c0, n = TILES[j]
        b = newbank()
        for k in range(KC):
            q = nsq_ctr[0] % 2
            nsq_ctr[0] += 1
            sq = nsq[:, q * 512:q * 512 + n]
            OP("act", "activation", [U("x", j, k)], [U("nsq", q)], out=sq, in_=xs(k, c0, n), func=AF.Square)
            OP("pe", "matmul", [U("nsq", q), "ones"], [U("ps", b)], PS(b, n), lhsT=ones[:], rhs=sq,
               start=(k == 0), stop=(k == KC - 1))
        OP("act", "activation", [U("ps", b), "eps"], ["nrs"], out=nrs[:, 0:n], in_=PS(b, n), func=AF.Ln, bias=EPS_AP[0],
           scale=1.0 / D)
        OP("act", "activation", ["nrs"], ["nrs"], out=nrs[:, 0:n], in_=nrs[:, 0:n], func=AF.Exp, scale=-0.5)
        for k in range(KC):
            out_fn(k, nrs[:, 0:n])

    EPS_AP = [None]

    def norm_to_h(j, gbase):
        c0, n = TILES[j]

        def f(k, rstd):
            OP("dve", "scalar_tensor_tensor", [U("x", j, k), "nrs", "ptab"], [U("h", j, k)],
               out=hs(k, c0, n), in0=xs(k, c0, n), scalar=pt(gbase + k), in1=rstd, op0=ALU.mult, op1=ALU.mult)
        rmsnorm_tile(j, f)

    def add_to_x(j, o, b):
        c0, n = TILES[j]
        OP("dve", "tensor_tensor", [U("ps", b), U("x", j, o)], [U("x", j, o)],
           out=xs(o, c0, n), in0=PS(b, n), in1=xs(o, c0, n), op=ALU.add)

    def wout_stage(j, slot, rhs_list, rhs_units, stage=None):
        c0, n = TILES[j]
        for o in range(KC):
            b = newbank()
            for k in range(4):
                OP("pe", "matmul", [U("ring", slot), rhs_units[k]], [U("ps", b)], PS(b, n), lhsT=RB(slot, k, o),
                   rhs=rhs_list[k], start=(k == 0), stop=(k == 3))
            if stage is None or o % 2 == 1:
                add_to_x(j, o, b)
            else:
                sidx = stage[(o // 2) % len(stage)]
                sa = A32(sidx, n)
                OP("act", "activation", [U("ps", b)], [U("a32", sidx)], out=sa, in_=PS(b, n), func=AF.Copy)
                OP("pool", "tensor_tensor", [U("a32", sidx), U("x", j, o)], [U("x", j, o)], out=xs(o, c0, n), in0=sa,
                   in1=xs(o, c0, n), op=ALU.add)

    def hist_setup(j, H, st_d, i):
        for g in range(4):
            ev, nseg, L = extv(g, j, H)
            if j == 0:
                src = st_d.rearrange("p (i s g r) -> p i s g r", i=2, s=4, g=4)[:, i, :, g, :]
                DMA("sp", U("hist", g), [], [U("ext", g)], ev[:, 0:4, 0:H], src)
                OP("dve", "memset", [], [U("ext", g)], ev[:, 4:5, 0:H], 0.0)
            elif j == 1:
                pv, _, _ = extv(g, 0, H)
                OP("dve", "tensor_scalar", [U("ext", g), "ptab"], [U("ext", g)], out=ev[:, 0:1, 0:H],
                   in0=pv[:, 5:6, 64:64 + H], scalar1=pt(C_MASK), scalar2=None, op0=ALU.mult)
            else:
                OP("dve", "tensor_copy", [U("ext", g)], [U("ext", g)], out=ev[:, 0:1, 0:H], in_=ev[:, 0:1, 512:512 + H])

    def hist_mid(j, g, H):
        if j == 0:
            ev, nseg, L = extv(g, 0, H)
            OP("dve", "tensor_copy", [U("ext", g)], [U("ext", g)], out=ev[:, 5:6, 0:H], in_=ev[:, 4:5, 64:64 + H])

    def state_out(j, g, H, o_ap, i):
        ov = o_ap.rearrange("p (i s g r) -> p i s g r", i=2, s=5, g=4)
        if j == 0:
            ev, _, _ = extv(g, 0, H)
            DMA("sp", U("so", g), [U("ext", g)], [], ov[:, i, 0:4, g, :], ev[:, 0:4, 64:64 + H])
        elif j == NT - 1:
            ev, _, _ = extv(g, j, H)
            DMA("sp", U("so", g), [U("ext", g)], [], ov[:, i, 4:5, g, :], ev[:, 0:1, 512:512 + H])

    def even_layer(l, skip_norm0=False):
        i = l // 2
        set_pool(range(8))
        DMA("pool", "wmt0", [], ["wmt0"], wmt[:, 0:1024], wTm_d[:, i * 1024:(i + 1) * 1024])
        DMA("pool", "wmt1", [], ["wmt1"], wmt[:, 1024:2048], wTs_d[:, i * 1024:(i + 1) * 1024])
        for v in range(2):
            wv = wmt[:, v * 1024:(v + 1) * 1024].rearrange("p (h i) -> p h i", h=8)
            mk = gmask[:, v * 128:(v + 1) * 128].unsqueeze(1).broadcast_to([P, 8, 128])
            OP("dve", "tensor_tensor", ["wmt%d" % v, "gmask"], ["wmt%d" % v], out=wv, in0=wv, in1=mk, op=ALU.mult)
        DMA("sp", "btab", [], ["btab"], btab[:], btab_d[:, i * 512:(i + 1) * 512])
        DMA("sp", "lntab", [], ["lntab"], lntab[:], lntab_d[:, i * 1024:(i + 1) * 1024])
        DMA("pool", "poolw", [], ["poolw"], poolw[:], poolw_d[:, i * 512:(i + 1) * 512])
        rsb = [newbank(), newbank()]
        for hb in range(2):
            OP("pe", "matmul", ["wmt0", "ones"], [U("ps", rsb[hb])], PS(rsb[hb], 512), lhsT=ones[:],
               rhs=wmt[:, hb * 512:(hb + 1) * 512], start=True, stop=True)
        tab2 = ext[:, 0:512]
        for c in range(4):
            for half in range(2):
                hh = 2 * c + half
                p0 = 64 * half
                OP("dve", "scalar_tensor_tensor", [U("ps", rsb[hh // 4]), "ptab", "btab"], [U("ext", 0)],
                   out=ext[p0:p0 + 64, c * 128:(c + 1) * 128],
                   in0=PS(rsb[hh // 4], 128, p0, p0 + 64, (hh % 4) * 128),
                   scalar=ptab[p0:p0 + 64, C_ALB + i * 4 + c:C_ALB + i * 4 + c + 1], in1=btab[p0:p0 + 64, c * 128:(c + 1) * 128],
                   op0=ALU.mult, op1=ALU.add)

        sV = load_slab(slab_in(w_in_even, i, 512), 8)
        sU = load_slab(slab_in(w_in_even, i, 0), 8)
        sOA = load_slab(slab_rows(w_out_even, i, 0), 4)
        gel_ctr = [0]
        mixb = [0, 1, 2, 3]
        set_pool([4, 5, 6, 7])
        VB = [0, 1, 6, 7]
        GEL = [4, 5, 6, 7]

        def a_vmm(j):
            c0, n = TILES[j]
            ngrp = n // 128
            for gi in range(ngrp):
                g0 = c0 + gi * 128
                b = newbank()
                for k in range(KC):
                    OP("pe", "matmul", [U("ring", sV), U("h", j, k)], [U("ps", b)], PS(b, 512), lhsT=hs(k, g0, 128),
                       rhs=RAfull(sV, k), start=(k == 0), stop=(k == KC - 1))
                gel = A32(GEL[gi])
                gu = U("a32", GEL[gi])
                OP("act", "activation", [U("ps", b)], [gu], out=gel, in_=PS(b, 512), func=AF.Gelu_apprx_tanh)
                st = small[:, 24 + gi * 6:24 + gi * 6 + 6]
                OP("dve", "bn_stats", [gu], [U("bst", gi)], out=st, in_=gel)
                OP("dve", "bn_aggr", [U("bst", gi)], ["mvall"], out=small[:, 2 * gi:2 * gi + 2], in_=st)

        def a_chain(j):
            c0, n = TILES[j]
            ngrp = n // 128
            mvall = small[:, 0:2 * ngrp].rearrange("p (g t) -> p g t", t=2)
            rsall = small[:, 16:16 + ngrp]
            OP("act", "activation", ["mvall", "eps"], ["rsall"], out=rsall, in_=mvall[:, :, 1], func=AF.Ln, bias=EPS_AP[0], scale=1.0)
            OP("act", "activation", ["rsall"], ["rsall"], out=rsall, in_=rsall, func=AF.Exp, scale=-0.5)
            for gi in range(ngrp):
                samp = (j == 0 and gi < 2)
                gel = A32(GEL[gi])
                gu = U("a32", GEL[gi])
                vb = A16(VB[gi])
                vu = U("a16", VB[gi])
                if not samp:
                    OP("dve", "tensor_scalar", [gu, "mvall", "rsall"], [vu], out=vb, in0=gel, scalar1=small[:, 2 * gi:2 * gi + 1],
                       scalar2=small[:, 16 + gi:17 + gi], op0=ALU.subtract, op1=ALU.mult)
                    continue
                OP("dve", "tensor_scalar", [gu, "mvall", "rsall"], [gu], out=gel, in0=gel, scalar1=small[:, 2 * gi:2 * gi + 1],
                   scalar2=small[:, 16 + gi:17 + gi], op0=ALU.subtract, op1=ALU.mult)
                OP(A_OFF, "tensor_tensor", [gu, "lntab"], [gu], out=gel, in0=gel, in1=lntab[:, 0:512], op=ALU.mult)
                OP(A_OFF, "tensor_tensor", [gu, "lntab"], [vu], out=vb, in0=gel, in1=lntab[:, 512:1024], op=ALU.add)
                if samp:
                    OP(A_OFF, "tensor_tensor", [gu, "lntab"], [U("a32", 7)], out=A32(7), in0=gel, in1=lntab[:, 512:1024],
                       op=ALU.add)
                    dstv = o_v.rearrange("p (g i f) -> p g i f", g=2, i=2)[:, gi, i, :]
                    DMA("sp", "ov", [U("a32", 7)], [], dstv, A32(7))

        def a_umm(j):
            c0, n = TILES[j]
            for c in range(4):
                b = newbank()
                for k in range(KC):
                    OP("pe", "matmul", [U("ring", sU), U("h", j, k)], [U("ps", b)], PS(b, n), lhsT=RA(sU, k, c),
                       rhs=hs(k, c0, n), start=(k == 0), stop=(k == KC - 1))
                OP("act", "activation", [U("ps", b)], [U("a32", c)], out=A32(c, n), in_=PS(b, n), func=AF.Gelu_apprx_tanh)

        def a_gating(j):
            c0, n = TILES[j]
            ngrp = n // 128
            for gi in range(ngrp):
                samp = (j == 0 and gi < 2)
                var = 1 if samp else 0
                vb = A16(VB[gi])
                for hh in range(8):
                    c = hh // 2
                    pp = (hh % 2) * 64
                    OP("pe", "matmul", [U("a16", VB[gi]), "wmt%d" % var], [U("ps", mixb[c])],
                       PS(mixb[c], 128, pp, pp + 64, gi * 128), lhsT=vb[:, hh * 64:(hh + 1) * 64],
                       rhs=wmt[:, var * 1024 + hh * 128:var * 1024 + (hh + 1) * 128], start=True, stop=True)

        def a_ya(j):
            c0, n = TILES[j]
            for c in range(4):
                tb = A32(4 + c, n)
                tu = U("a32", 4 + c)
                bt = btab[:, c * 128:(c + 1) * 128]
                t2 = ext[:, c * 128:(c + 1) * 128]
                if j == 0:
                    OP("dve", "tensor_tensor", [U("ps", mixb[c]), "btab"], [tu], out=v3(tb[:, 0:256], 4),
                       in0=v3(PS(mixb[c], 256), 4), in1=bt[:, 0:64].unsqueeze(1).broadcast_to([P, 4, 64]), op=ALU.add)
                    OP("dve", "scalar_tensor_tensor", [U("ps", mixb[c]), U("ext", 0), "ptab"], [tu], out=tb[:, 256:384],
                       in0=PS(mixb[c], 128, 0, P, 256), scalar=pt(C_ALG + i * 4 + c), in1=t2, op0=ALU.mult, op1=ALU.add)
                else:
                    OP("dve", "scalar_tensor_tensor", [U("ps", mixb[c]), U("ext", 0), "ptab"], [tu], out=v3(tb, 4),
                       in0=v3(PS(mixb[c], 512), 4), scalar=pt(C_ALG + i * 4 + c),
                       in1=t2.unsqueeze(1).broadcast_to([P, 4, 128]), op0=ALU.mult, op1=ALU.add)
                OP(A_OFF, "tensor_tensor", [tu, U("a32", c)], [U("a16", 2 + c)], out=A16(2 + c, n), in0=tb,
                   in1=A32(c, n), op=ALU.mult)

        bst_ = {}

        def zp_mm(j):
            c0, n = TILES[j]
            sPz_ = bst_["sPz"]
            for g in range(4):
                for k in range(KC):
                    OP("pe", "matmul", [U("ring", sPz_), U("h", j, k)], [U("ps", g)], PS(g, n), lhsT=RA(sPz_, k, g),
                       rhs=hs(k, c0, n), start=(k == 0), stop=(k == KC - 1))

        if not skip_norm0:
            norm_to_h(0, C_MIXG + l * 8)
        if l == 0:
            xload(2, [U("h", 0, KC - 1)])
        a_vmm(0)
        a_chain(0)
        a_umm(0)
        norm_to_h(1, C_MIXG + l * 8)
        if l == 0:
            xload(3, [U("h", 1, KC - 1)])
        for j, (c0, n) in enumerate(TILES):
            a_gating(j)
            a_ya(j)
            if j + 1 < NT:
                a_vmm(j + 1)
            else:
                dl = [U("h", 2, KC - 1)] if l == 0 else []
                bst_["sPz"] = load_slab(slab_in(w_in_even, i, 1024), 8, deps=dl)
                bst_["sOB"] = load_slab(slab_rows(w_out_even, i, 512), 4, deps=dl)
                zp_mm(0)
            wout_stage(j, sOA, [A16(2 + c, n) for c in range(4)], [U("a16", 2 + c) for c in range(4)])
            if j + 1 < NT:
                a_chain(j + 1)
                a_umm(j + 1)
                if j + 2 < NT:
                    norm_to_h(j + 2, C_MIXG + l * 8)
                    if l == 0 and j + 4 < NT:
                        xload(j + 4, [U("h", j + 2, KC - 1)])

        sPz = bst_["sPz"]
        sOB = bst_["sOB"]
        H = 15
        set_pool([4, 5, 6, 7])
        def b_evac(j):
            c0, n = TILES[j]
            hist_setup(j, H, stp_d, i)
            for g in range(4):
                ev, nseg, L = extv(g, j, H)
                eu = U("ext", g)
                OP("act", "activation", [U("ps", g)], [eu], out=ev[:, :, H:H + L], in_=v3(PS(g, n), nseg), func=AF.Copy)
                hist_mid(j, g, H)
                state_out(j, g, H, o_pool, i)

        def b_sums(j):
            c0, n = TILES[j]
            for g in range(4):
                ev, nseg, L = extv(g, j, H)
                eu = U("ext", g)
                wwin = 2 << g
                W = H + L
                bufs = [v3(A32(2 * (g % 2), nseg * W), nseg), v3(A32(2 * (g % 2) + 1, nseg * W), nseg)]
                bunits = [U("a32", 2 * (g % 2)), U("a32", 2 * (g % 2) + 1)]
                cur, cur_u = ev, eu
                sh = 1
                step = 0
                while sh < wwin:
                    lo = H - (wwin - 2 * sh)
                    dst, dst_u = bufs[step % 2], bunits[step % 2]
                    OP("dve", "tensor_tensor", [cur_u], [dst_u], out=dst[:, :, lo:W], in0=cur[:, :, lo:W],
                       in1=cur[:, :, lo - sh:W - sh], op=ALU.add)
                    cur, cur_u = dst, dst_u
                    sh *= 2
                    step += 1
                pooled = A16(g, n)
                OP("dve", "scalar_tensor_tensor", [cur_u, eu], [U("a16", g)], out=v3(pooled, nseg), in0=cur[:, :, H:H + L],
                   scalar=1.0 / wwin, in1=ev[:, :, H:H + L], op0=ALU.mult, op1=ALU.subtract)
                if j == 1:
                    tmp = small[:, 48:64]
                    OP("dve", "tensor_tensor", [cur_u, "ptab"], ["small32"], out=tmp, in0=cur[:, 0, H:H + 16],
                       in1=pt(C_ICNT + g * 16, 16), op=ALU.mult)
                    OP("dve", "tensor_tensor", ["small32", eu], [U("a16", g)], out=pooled[:, 0:16], in0=tmp,
                       in1=ev[:, 0, H:H + 16], op=ALU.subtract)

        def b_poolw(j):
            c0, n = TILES[j]
            for g in range(4):
                b2 = newbank()
                OP("pe", "matmul", ["poolw", U("a16", g)], [U("ps", b2)], PS(b2, n), lhsT=poolw[:, g * 128:(g + 1) * 128],
                   rhs=A16(g, n), start=True, stop=True)
                OP("act", "activation", [U("ps", b2), "ptab"], [U("a16", 4 + g)], out=A16(4 + g, n), in_=PS(b2, n),
                   func=AF.Copy, scale=pt(C_BSC + i * 4 + g))

        def b_wout(j):
            c0, n = TILES[j]
            wout_stage(j, sOB, [A16(4 + g, n) for g in range(4)], [U("a16", 4 + g) for g in range(4)])

        for j in range(NT):
            b_evac(j)
            if j + 1 < NT:
                zp_mm(j + 1)
            b_sums(j)
            if j >= 1:
                b_wout(j - 1)
            b_poolw(j)
        b_wout(NT - 1)

    dg_ctr = [0]

    def odd_layer(l, skip_norm0=False):
        i = l // 2
        sCv = load_slab(slab_in(w_in_odd, i, 0), 8)
        sCg = load_slab(slab_in(w_in_odd, i, 512), 8)
        sOA = load_slab(slab_rows(w_out_odd, i, 0), 4)
        H = 30
        sig_ctr = [0]
        set_pool([0, 1])
        bm, bq = 6, 7
        ext16 = ar32[:, 2 * A32W:2 * A32W + 2 * EXTW].bitcast(BF16)
        dgb = ar32[:, 5 * A32W:5 * A32W + 512].bitcast(BF16)
        dgb2 = lntab[:].bitcast(BF16)
        NDG = 16
        E16U = [U("a32", 2), U("a32", 3), U("a32", 4)]

        def e16v(g, j):
            nseg, L = (6, 64) if j == 0 else (1, 512)
            W = H + L
            return v3(ext16[:, g * EXTW:g * EXTW + nseg * W], nseg), nseg, L

        def vg(j, g, banks=None):
            c0, n = TILES[j]
            ev, nseg, L = extv(g, j, H)
            eu = U("ext", g)
            bv = banks[0] if banks else newbank()
            for k in range(KC):
                OP("pe", "matmul", [U("ring", sCv), U("h", j, k)], [U("ps", bv)], PS(bv, n), lhsT=RA(sCv, k, g),
                   rhs=hs(k, c0, n), start=(k == 0), stop=(k == KC - 1))
            bg = banks[1] if banks else newbank()
            for k in range(KC):
                OP("pe", "matmul", [U("ring", sCg), U("h", j, k)], [U("ps", bg)], PS(bg, n), lhsT=RA(sCg, k, g),
                   rhs=hs(k, c0, n), start=(k == 0), stop=(k == KC - 1))
            q = sig_ctr[0] % 2
            sig_ctr[0] += 1
            sg = A32(q, n)
            OP("act", "activation", [U("ps", bg)], [U("a32", q)], out=sg, in_=PS(bg, n), func=AF.Sigmoid)
            OP("dve", "tensor_tensor", [U("ps", bv), U("a32", q)], [eu], out=ev[:, :, H:H + L], in0=v3(PS(bv, n), nseg),
               in1=v3(sg, nseg), op=ALU.mult)
            hist_mid(j, g, H)
            state_out(j, g, H, o_c, i)
            e16, _, _ = e16v(g, j)
            OP("dve", "tensor_copy", [eu], (E16U if j == 0 else []) + [U("e16", g)], out=e16, in_=ev)

        def conv(j, g):
            c0, n = TILES[j]
            e16, nseg, L = e16v(g, j)
            wb = C_CCW + (i * 4 + g) * 31
            for k0 in range(0, 31, 4):
                nk = min(4, 31 - k0)
                gq = dg_ctr[0] % (NDG // 4)
                dg_ctr[0] += 1
                dgw = [U("dg", gq)] + (["lntab"] if first_dg[0] > 0 else [])
                first_dg[0] -= 4
                dg4 = dgb2[:, gq * 512:gq * 512 + nk * 128].rearrange("p (t m) -> p t m", t=nk)
                OP("pool", "tensor_tensor", ["ident", "ptab"], dgw, out=dg4,
                   in0=ident.unsqueeze(1).broadcast_to([P, nk, 128]),
                   in1=pt(wb + k0, nk).unsqueeze(2).broadcast_to([P, nk, 128]), op=ALU.mult)
                for t in range(nk):
                    k = k0 + t
                    OP("pe", "matmul", [U("dg", gq), U("e16", g)], [U("ps", 2 + g)], v3(PS(2 + g, n), nseg),
                       lhsT=dgb2[:, gq * 512 + t * 128:gq * 512 + (t + 1) * 128],
                       rhs=e16[:, :, k:k + L], start=(k == 0), stop=(k == 30))
            q = sig_ctr[0] % 2
            cb = A16(q, n)
            sq = A16(2 + q, n)
            OP("act", "activation", [U("ps", 2 + g), "ptab"], [U("a16", q)], out=cb, in_=PS(2 + g, n), func=AF.Identity,
               bias=pt(C_CCB + i * 4 + g), scale=1.0)
            OP("act", "activation", [U("ps", 2 + g), "ptab"], [U("a16", 2 + q)], out=sq, in_=PS(2 + g, n), func=AF.Square,
               bias=pt(C_CCB + i * 4 + g), scale=1.0)
            OP("pe", "matmul", [U("a16", q), "ones"], [U("ps", bm)], PS(bm, n), lhsT=ones[:], rhs=cb,
               start=(g == 0), stop=(g == 3))
            OP("pe", "matmul", [U("a16", 2 + q), "ones"], [U("ps", bq)], PS(bq, n), lhsT=ones[:], rhs=sq,
               start=(g == 0), stop=(g == 3))
            sig_ctr[0] += 1

        ABUF = [(btab[:, 0:512], "btab"), (wmt[:, 0:1024].bitcast(F32), "wmt0"), (wmt[:, 1024:2048].bitcast(F32), "wmt1"),
                (A32(5), U("a32", 5))]

        def ln_chain(j):
            c0, n = TILES[j]
            m2 = A32(6, n)
            mu = U("a32", 6)
            nm = A32(7, n)
            nu = U("a32", 7)
            OP("act", "activation", [U("ps", bm)], [nu], out=nm, in_=PS(bm, n), func=AF.Copy, scale=-1.0 / 512)
            OP("act", "activation", [U("ps", bm)], [mu], out=m2, in_=PS(bm, n), func=AF.Square, scale=1.0 / 512)
            for g in range(4):
                a, au = ABUF[g][0][:, 0:n], ABUF[g][1]
                OP("dve", "scalar_tensor_tensor", [U("ps", 2 + g), nu, "ptab"], [au], out=a, in0=PS(2 + g, n),
                   scalar=pt(C_CCB + i * 4 + g), in1=nm, op0=ALU.add, op1=ALU.add)
            OP("dve", "scalar_tensor_tensor", [U("ps", bq), mu], [mu], out=m2, in0=PS(bq, n), scalar=1.0 / 512, in1=m2,
               op0=ALU.mult, op1=ALU.subtract)
            OP("act", "activation", [mu, "eps"], [mu], out=m2, in_=m2, func=AF.Ln, bias=EPS_AP[0], scale=1.0)
            OP("act", "activation", [mu], [mu], out=m2, in_=m2, func=AF.Exp, scale=-0.5)

        def ln_post(j):
            c0, n = TILES[j]
            m2 = A32(6, n)
            mu = U("a32", 6)
            for g in range(4):
                a, au = ABUF[g][0][:, 0:n], ABUF[g][1]
                OP("dve", "tensor_tensor", [au, mu], [au], out=a, in0=a, in1=m2, op=ALU.mult)
                OP("act", "activation", [au, "ptab"], [U("a16", 4 + g)], out=A16(4 + g, n), in_=a, func=AF.Silu,
                   scale=pt(C_CLG + i * 4 + g), bias=pt(C_CLB + i * 4 + g))

        first_dg = [NDG]
        if not skip_norm0:
            norm_to_h(0, C_MIXG + l * 8)
        hist_setup(0, H, stc_d, i)
        vg(0, 0)
        vg(0, 1)
        conv(0, 0)
        for j, (c0, n) in enumerate(TILES):
            vg(j, 2)
            conv(j, 1)
            if j + 1 < NT:
                norm_to_h(j + 1, C_MIXG + l * 8)
            vg(j, 3)
            conv(j, 2)
            conv(j, 3)
            ln_chain(j)
            if j + 1 < NT:
                hist_setup(j + 1, H, stc_d, i)
                vg(j + 1, 0)
                vg(j + 1, 1, banks=(4, 5))
            ln_post(j)
            if j + 1 < NT:
                conv(j + 1, 0)
            wout_stage(j, sOA, [A16(4 + g, n) for g in range(4)], [U("a16", 4 + g) for g in range(4)])

        OP("dve", "memset", [U("e16", g) for g in range(4)] + [U("dg", q) for q in range(4)],
           E16U + ["lntab"], small[:, 20:21], 0.0)
        set_pool(range(8))
        sDc = load_slab(slab_in(w_in_odd, i, 1536), 8)
        sDx = load_slab(slab_in(w_in_odd, i, 2048), 8)
        sDb = load_slab(slab_in(w_in_odd, i, 1024), 8)
        sOB = load_slab(slab_rows(w_out_odd, i, 512), 4)
        H = 2
        ctr = [0]
        def d_chunk(j, g):
            c0, n = TILES[j]
            ev, nseg, L = extv(g, j, H)
            eu = U("ext", g)
            b1 = newbank()
            for k in range(KC):
                OP("pe", "matmul", [U("ring", sDc), U("h", j, k)], [U("ps", b1)], PS(b1, n), lhsT=RA(sDc, k, g),
                   rhs=hs(k, c0, n), start=(k == 0), stop=(k == KC - 1))
            b2 = newbank()
            for k in range(KC):
                OP("pe", "matmul", [U("ring", sDx), U("h", j, k)], [U("ps", b2)], PS(b2, n), lhsT=RA(sDx, k, g),
                   rhs=hs(k, c0, n), start=(k == 0), stop=(k == KC - 1))
            b3 = newbank()
            for k in range(KC):
                OP("pe", "matmul", [U("ring", sDb), U("h", j, k)], [U("ps", b3)], PS(b3, n), lhsT=RA(sDb, k, g),
                   rhs=hs(k, c0, n), start=(k == 0), stop=(k == KC - 1))
            q = ctr[0] % 2
            ctr[0] += 1
            gcs = A32(q, n)
            OP("act", "activation", [U("ps", b1)], [U("a32", q)], out=gcs, in_=PS(b1, n), func=AF.Copy)
            OP("dve", "tensor_tensor", [U("ps", b2), U("a32", q)], [eu], out=ev[:, :, H:H + L], in0=v3(PS(b2, n), nseg),
               in1=v3(gcs, nseg), op=ALU.mult)
            hist_mid(j, g, H)
            state_out(j, g, H, o_d, i)
            acc = v3(A32(2 + q, n), nseg)
            au = U("a32", 2 + q)
            wb = C_DCW + (i * 4 + g) * 3
            OP("dve", "tensor_scalar", [eu, "ptab"], [au], out=acc, in0=ev[:, :, 0:L], scalar1=pt(wb), scalar2=None,
               op0=ALU.mult)
            for k in range(1, 3):
                OP("dve", "scalar_tensor_tensor", [eu, "ptab", au], [au], out=acc, in0=ev[:, :, k:k + L],
                   scalar=pt(wb + k), in1=acc, op0=ALU.mult, op1=ALU.add)
            di = (j % 2) * 4 + g
            OP("dve", "tensor_tensor", [U("ps", b3), au], [U("a16", di)], out=A16(di, n), in0=PS(b3, n), in1=A32(2 + q, n),
               op=ALU.mult)

        def d_wout(j):
            c0, n = TILES[j]
            par = (j % 2) * 4
            wout_stage(j, sOB, [A16(par + g, n) for g in range(4)], [U("a16", par + g) for g in range(4)])

        for j in range(NT):
            hist_setup(j, H, std_d, i)
            for g in range(4):
                d_chunk(j, g)
                if g == 0 and j >= 1:
                    d_wout(j - 1)
                if g == 2 and j >= 1:
                    norm_to_h(j - 1, C_FFNG + l * 8)
        d_wout(NT - 1)
        norm_to_h(NT - 1, C_FFNG + l * 8)

    def ffn(l, after_last=None, after_tile0=None):
        set_pool(range(8))
        rctr = [0]
        items = [(p, j) for p in range(8) for j in range(NT)]
        slots = {}

        def stage1(t):
            p, j = items[t]
            c0, n = TILES[j]
            if j == 0:
                s1 = load_slab(slab_in(w_ff1, l, 512 * p), 8)
                s2 = load_slab(slab_rows(w_ff2, l, 512 * p), 4)
                slots[p] = (s1, s2)
            s1, s2 = slots[p]
            if l % 2 == 0 and p == 0 and j == 0:
                norm_to_h(0, C_FFNG + l * 8)
            aset = t % 2
            for c in range(4):
                b = newbank()
                for k in range(KC):
                    OP("pe", "matmul", [U("ring", s1), U("h", j, k)], [U("ps", b)], PS(b, n), lhsT=RA(s1, k, c),
                       rhs=hs(k, c0, n), start=(k == 0), stop=(k == KC - 1))
                q = rctr[0] % 4
                rctr[0] += 1
                r_ = A32(q, n)
                OP("act", "activation", [U("ps", b)], [U("a32", q)], out=r_, in_=PS(b, n), func=AF.Relu)
                OP("pool" if (l % 2 == 0 and p == 0 and c % 2 == 0) else "dve", "tensor_tensor", [U("a32", q)],
                   [U("a16", aset * 4 + c)], out=A16(aset * 4 + c, n), in0=r_, in1=r_, op=ALU.mult)
            if l % 2 == 0 and p == 0 and j + 1 < NT:
                norm_to_h(j + 1, C_FFNG + l * 8)

        def stage2(t):
            p, j = items[t]
            c0, n = TILES[j]
            s1, s2 = slots[p]
            aset = t % 2
            wout_stage(j, s2, [A16(aset * 4 + c, n) for c in range(4)], [U("a16", aset * 4 + c) for c in range(4)])

        for t in range(len(items) + 1):
            if t < len(items):
                stage1(t)
            if t >= 1:
                stage2(t - 1)
                pp_, jj_ = items[t - 1]
                if pp_ == 7 and after_last is not None:
                    after_last(jj_)
                if pp_ == 7 and jj_ == 0 and after_tile0 is not None:
                    after_tile0()

    epsb = sb("epsb", [P, 1], F32)
    OP("dve", "memset", [], ["eps"], epsb[:], EPS)
    EPS_AP[0] = epsb[:]
    yctr = [0]

    def final_norm(j):
        c0, n = TILES[j]

        def f(k, rstd):
            q = 4 + yctr[0] % 4
            yctr[0] += 1
            yb = A32(q, n)
            OP("dve", "scalar_tensor_tensor", [U("x", j, k), "nrs", "ptab"], [U("a32", q)], out=yb, in0=xs(k, c0, n),
               scalar=pt(C_FING + k), in1=rstd, op0=ALU.mult, op1=ALU.mult)
            DMA("sp", U("yo", q), [U("a32", q)], [], yT[:, k * T + c0:k * T + c0 + n], yb)
        rmsnorm_tile(j, f)

    for l in range(nlayers):
        if l % 2 == 0:
            even_layer(l, skip_norm0=(l > 0))
        else:
            odd_layer(l, skip_norm0=(l > 0))
        last = (l == nlayers - 1)
        ffn(l, after_last=(final_norm if last else None),
            after_tile0=(None if last else (lambda l=l: norm_to_h(0, C_MIXG + (l + 1) * 8))))

    S.emit(nc, stack)
    stack.close()
    return nc


_NC_CACHE = {}


def _prep_inputs(inp):
    f = np.float32
    xp = np.asarray(inp["x_prompt"], f)
    xsm = np.asarray(inp["x_sample"], f)
    ptab_common = np.zeros((P, NPT), f)

    def pp(v):
        v = np.asarray(v, f)
        ch = v.shape[-1] // P
        v = v.reshape(v.shape[:-1] + (ch, P))
        return np.moveaxis(v, -1, 0)

    ptab_common[:, C_MIXG:C_MIXG + 32] = pp(inp["norm_mix_g"]).reshape(P, 32)
    ptab_common[:, C_FFNG:C_FFNG + 32] = pp(inp["norm_ffn_g"]).reshape(P, 32)
    ptab_common[:, C_FING:C_FING + 8] = pp(inp["final_norm_g"]).reshape(P, 8)
    ptab_common[:, C_BSC:C_BSC + 8] = pp(inp["b_scale"]).reshape(P, 8)
    ptab_common[:, C_CCB:C_CCB + 8] = pp(inp["c_conv_b"]).reshape(P, 8)
    ptab_common[:, C_CLG:C_CLG + 8] = pp(inp["c_ln_g"]).reshape(P, 8)
    ptab_common[:, C_CLB:C_CLB + 8] = pp(inp["c_ln_b"]).reshape(P, 8)
    ccw = pp(inp["c_conv_w"])
    ptab_common[:, C_CCW:C_CCW + 248] = np.transpose(ccw, (0, 1, 3, 2)).reshape(P, 248)
    dcw = pp(inp["d_conv_w"])
    ptab_common[:, C_DCW:C_DCW + 24] = np.transpose(dcw, (0, 1, 3, 2)).reshape(P, 24)
    ptab_common[:, C_ALG:C_ALG + 8] = pp(inp["a_ln_g"]).reshape(P, 8)
    ptab_common[:, C_ALB:C_ALB + 8] = pp(inp["a_ln_b"]).reshape(P, 8)

    aws = np.asarray(inp["a_w_s"], f)
    wTm = np.transpose(aws, (3, 0, 1, 2)).reshape(P, 2 * 8 * 128)
    idx = np.arange(128) % 64
    aws_s = aws[:, :, idx][:, :, :, idx]
    wTs = np.transpose(aws_s, (3, 0, 1, 2)).reshape(P, 2 * 8 * 128)
    jj = np.arange(128)[:, None] // 64
    ii = np.arange(128)[None, :] // 64
    gmask = np.concatenate([(jj <= ii).astype(f), (jj == ii).astype(f), np.eye(128, dtype=f)], axis=1)
    abs_ = np.asarray(inp["a_b_s"], f)
    hp = (np.arange(P) // 64)
    btab = np.zeros((P, 2, 4, 128), f)
    for c in range(4):
        btab[:, :, c, :] = np.transpose(abs_[:, 2 * c + hp, :], (1, 0, 2))
    btab = btab.reshape(P, 2 * 4 * 128)
    lng = np.asarray(inp["a_ln_g"], f)
    lnb = np.asarray(inp["a_ln_b"], f)
    lntab = np.stack([lng, lnb], axis=1)[None].repeat(P, axis=0).reshape(P, 2 * 2 * 512)
    poolw = np.transpose(np.asarray(inp["b_pool_w"], f), (2, 0, 1, 3)).reshape(P, 2 * 4 * 128)

    def st(v, H):
        v = np.asarray(v, f).reshape(2, 8, 4, H, 4, P)
        v = np.transpose(v, (1, 5, 0, 2, 4, 3))
        return v.reshape(8, P, -1)

    stp = st(inp["state_pool"], 15)
    stc = st(inp["state_conv_c"], 30)
    std = st(inp["state_conv_d"], 2)

    shared = dict(wTm=np.ascontiguousarray(wTm), wTs=np.ascontiguousarray(wTs), gmask=np.ascontiguousarray(gmask),
                  btab=btab, lntab=np.ascontiguousarray(lntab), poolw=np.ascontiguousarray(poolw))
    for nme in ("w_in_even", "w_out_even", "w_in_odd", "w_out_odd", "w_ff1", "w_ff2"):
        shared[nme] = np.ascontiguousarray(np.asarray(inp[nme], f))
    in_maps = []
    for c in range(NCORE):
        b, q = c // 4, c % 4
        tok = np.zeros((T, D), f)
        tok[0:TS] = xsm[4 * c:4 * c + 4].reshape(TS, D)
        if q > 0:
            tok[TS:TS + TH] = xp[b, q * TM - TH:q * TM]
        tok[TS + TH:] = xp[b, q * TM:(q + 1) * TM]
        xin = np.ascontiguousarray(tok.reshape(T, KC, P).transpose(2, 1, 0)).reshape(P, KC * T)
        pt_ = ptab_common.copy()
        pt_[:, C_MASK] = 0.0 if q == 0 else 1.0
        for g, w in enumerate((2, 4, 8, 16)):
            if q == 0:
                cntv = np.minimum(w, np.arange(16) + 1).astype(f)
            else:
                cntv = np.full(16, w, f)
            pt_[:, C_ICNT + g * 16:C_ICNT + (g + 1) * 16] = (1.0 / cntv)[None, :]
        m = dict(shared)
        m.update(xin=xin, ptab=pt_, stp=np.ascontiguousarray(stp[c]), stc=np.ascontiguousarray(stc[c]),
                 std=np.ascontiguousarray(std[c]))
        in_maps.append(m)
    return in_maps


def kernel(**inputs):
    return _run(inputs, DEPTH)


def _run(inputs, nlayers, trace=False):
    if nlayers not in _NC_CACHE:
        _NC_CACHE[nlayers] = build_program(nlayers)
    nc = _NC_CACHE[nlayers]
    in_maps = _prep_inputs(inputs)
    if trace:
        res = run_bass_kernel_spmd(nc, in_maps, core_ids=list(range(NCORE)), trace=True)
        _NC_CACHE["last_res"] = res
    else:
        res = run_bass_kernel_spmd(nc, in_maps, core_ids=list(range(NCORE)))
    R = res.results
    f = np.float32
    y_prompt = np.zeros((2, 8192, D), f)
    y_sample = np.zeros((32, 64, D), f)
    new_pool_p = np.zeros((2, 2, 15, 512), f)
    new_c_p = np.zeros((2, 2, 30, 512), f)
    new_d_p = np.zeros((2, 2, 2, 512), f)
    new_v_s = np.zeros((2, 32, 64, 512), f)
    new_pool_s = np.zeros((2, 32, 15, 512), f)
    new_c_s = np.zeros((2, 32, 30, 512), f)
    new_d_s = np.zeros((2, 32, 2, 512), f)
    for c in range(NCORE):
        b, q = c // 4, c % 4
        y = np.asarray(R[c]["yT"]).reshape(P, KC, T).transpose(2, 1, 0).reshape(T, D)
        y_sample[4 * c:4 * c + 4] = y[0:TS].reshape(4, 64, D)
        y_prompt[b, q * TM:(q + 1) * TM] = y[TS + TH:]
        ov = np.asarray(R[c]["o_v"]).reshape(P, 2, 2, 512)
        vv = ov.transpose(2, 1, 0, 3).reshape(2, 256, 512).reshape(2, 4, 64, 512)
        new_v_s[:, 4 * c:4 * c + 4] = vv
        for arr_s, arr_p, nme, H in ((new_pool_s, new_pool_p, "o_pool", 15), (new_c_s, new_c_p, "o_c", 30),
                                     (new_d_s, new_d_p, "o_d", 2)):
            o = np.asarray(R[c][nme]).reshape(P, 2, 5, 4, H)
            o = o.transpose(1, 2, 4, 3, 0).reshape(2, 5, H, 512)
            arr_s[:, 4 * c:4 * c + 4] = o[:, 0:4]
            if q == 3:
                arr_p[:, b] = o[:, 4]
    return (y_prompt, y_sample, new_pool_p, new_c_p, new_d_p, new_v_s, new_pool_s, new_c_s, new_d_s)
```

```python
import numpy as np
from contextlib import ExitStack
import concourse.bass as bass
import concourse.mybir as mybir
from concourse.bass_utils import run_bass_kernel_spmd

F32 = mybir.dt.float32
BF16 = mybir.dt.bfloat16
AF = mybir.ActivationFunctionType
ALU = mybir.AluOpType

P = 128
D = 1024
KC = 8
NCORE = 8
TS, TH, TM = 256, 128, 2048
T = TS + TH + TM
TILES = [(0, 384)] + [(384 + 512 * i, 512) for i in range(4)]
NT = len(TILES)
DEPTH = 4
EPS = 1e-6
NSLOT = 5
DG_ENG = ("dve", "pool", "dve", "act")
A_OFF = "dve"
SLOTW = 4096
EXTW = 576

C_MIXG, C_FFNG, C_FING, C_BSC, C_CCB, C_CLG, C_CLB = 0, 32, 64, 72, 80, 88, 96
C_CCW, C_DCW, C_MASK, C_ICNT = 104, 352, 376, 377
C_ALG, C_ALB = 448, 456
NPT = 464


class Sched:
    ENG = ("pe", "act", "dve", "pool", "sp")

    def __init__(self):
        self.ops = []
        self.lastw = {}
        self.readers = {}
        self.lastdma = {}

    def _add(self, eng, fn, r, w, kind, key=None):
        i = len(self.ops)
        deps = set()
        for u in r:
            j = self.lastw.get(u)
            if j is not None:
                deps.add(j)
        for u in w:
            j = self.lastw.get(u)
            if j is not None:
                deps.add(j)
            deps.update(self.readers.get(u, ()))
        if kind == "d":
            j = self.lastdma.get(key)
            if j is not None:
                deps.add(j)
            self.lastdma[key] = i
        for u in r:
            self.readers.setdefault(u, []).append(i)
        for u in w:
            self.lastw[u] = i
            self.readers[u] = []
        self.ops.append(dict(eng=eng, fn=fn, deps=deps, kind=kind, key=key, ev=None))
        return i

    def op(self, eng, fn, r=(), w=()):
        return self._add(eng, fn, r, w, "c")

    def dma(self, eng, fn, key, r=(), w=()):
        return self._add(eng, fn, r, w, "d", key)

    def emit(self, nc, stack):
        ops = self.ops
        n = len(ops)
        need = [False] * n
        for op in ops:
            for d in op["deps"]:
                od = ops[d]
                if od["kind"] == "c":
                    if od["eng"] == "pe" and op["eng"] == "pe" and op["kind"] == "c":
                        continue
                    need[d] = True
        cnt = {e: 0 for e in self.ENG}
        dcnt = {}
        for i, op in enumerate(ops):
            if op["kind"] == "c":
                if need[i]:
                    cnt[op["eng"]] += 1
                    op["ev"] = (("eng", op["eng"]), cnt[op["eng"]])
            else:
                k = op["key"]
                dcnt[k] = dcnt.get(k, 0) + 16
                op["ev"] = (("dma", k), dcnt[k])
        sems = {}
        for e in self.ENG:
            sems[("eng", e)] = stack.enter_context(nc.semaphore("s_" + e))
        for idx, k in enumerate(dcnt):
            sems[("dma", k)] = stack.enter_context(nc.semaphore("d_%d" % idx))
        block = stack.enter_context(nc.Block())
        by_eng = {e: [] for e in self.ENG}
        for i, op in enumerate(ops):
            by_eng[op["eng"]].append(i)

        def run(eng_name, eh):
            waited = {}
            for i in by_eng[eng_name]:
                op = ops[i]
                w = {}
                for d in op["deps"]:
                    od = ops[d]
                    if od["kind"] == "c" and od["eng"] == "pe" and eng_name == "pe" and op["kind"] == "c":
                        continue
                    s, v = od["ev"]
                    if v > w.get(s, 0):
                        w[s] = v
                for s, v in w.items():
                    if waited.get(s, 0) < v:
                        eh.wait_ge(sems[s], v)
                        waited[s] = v
                nm, a_, kw_ = op["fn"]
                ins = getattr(eh, nm)(*a_, **kw_)
                if op["kind"] == "c":
                    if need[i]:
                        ins.then_inc(sems[("eng", eng_name)], 1)
                else:
                    ins.then_inc(sems[("dma", op["key"])], 16)
            if eng_name == "sp":
                for k, v in dcnt.items():
                    eh.wait_ge(sems[("dma", k)], v)

        @block.tensor
        def _(e):
            run("pe", e)

        @block.scalar
        def _(e):
            run("act", e)

        @block.vector
        def _(e):
            run("dve", e)

        @block.gpsimd
        def _(e):
            run("pool", e)

        @block.sync
        def _(e):
            run("sp", e)


def build_program(nlayers=DEPTH):
    nc = bass.Bass("TRN2", target_bir_lowering=False)
    stack = ExitStack()
    S = Sched()

    def OP(eng, name, r, w, *a, **kw):
        S._add(eng, (name, a, kw), r, w, "c")

    def DMA(eng, key, r, w, out, in_):
        S._add(eng, ("dma_start", (), dict(out=out, in_=in_)), r, w, "d", key)

    def din(name, shape, dt=F32):
        return nc.dram_tensor(name, shape, dt, kind="ExternalInput").ap()

    def dout(name, shape, dt=F32):
        return nc.dram_tensor(name, shape, dt, kind="ExternalOutput").ap()

    xin = din("xin", [P, KC * T])
    ptab_d = din("ptab", [P, NPT])
    stp_d = din("stp", [P, 2 * 4 * 4 * 15])
    stc_d = din("stc", [P, 2 * 4 * 4 * 30])
    std_d = din("std", [P, 2 * 4 * 4 * 2])
    wTm_d = din("wTm", [P, 2 * 8 * 128])
    wTs_d = din("wTs", [P, 2 * 8 * 128])
    gmask_d = din("gmask", [P, 3 * 128])
    btab_d = din("btab", [P, 2 * 4 * 128])
    lntab_d = din("lntab", [P, 2 * 2 * 512])
    poolw_d = din("poolw", [P, 2 * 4 * 128])
    w_in_even = din("w_in_even", [2, 1024, 1536])
    w_out_even = din("w_out_even", [2, 1024, 1024])
    w_in_odd = din("w_in_odd", [2, 1024, 2560])
    w_out_odd = din("w_out_odd", [2, 1024, 1024])
    w_ff1 = din("w_ff1", [4, 1024, 4096])
    w_ff2 = din("w_ff2", [4, 4096, 1024])
    yT = dout("yT", [P, KC * T])
    o_pool = dout("o_pool", [P, 2 * 5 * 4 * 15])
    o_c = dout("o_c", [P, 2 * 5 * 4 * 30])
    o_d = dout("o_d", [P, 2 * 5 * 4 * 2])
    o_v = dout("o_v", [P, 2 * 2 * 512])

    def sb(name, shape, dt):
        return stack.enter_context(nc.sbuf_tensor(name, shape, dt))

    x_sb = sb("x_sb", [P, KC * T], F32)
    h_sb = sb("h_sb", [P, KC * T], BF16)
    ring = sb("ring", [P, NSLOT * SLOTW], BF16)
    ext = sb("ext", [P, 4 * EXTW], F32)
    wmt = sb("wmt", [P, 2 * 8 * 128], BF16)
    gmask = sb("gmask_sb", [P, 3 * 128], BF16)
    ident = gmask[:, 256:384]
    btab = sb("btab_sb", [P, 4 * 128], F32)
    lntab = sb("lntab_sb", [P, 2 * 512], F32)
    poolw = sb("poolw_sb", [P, 4 * 128], BF16)
    ptab = sb("ptab_sb", [P, NPT], F32)
    ones = sb("ones", [P, P], BF16)
    small = sb("small", [P, 64], F32)
    nsq = sb("nsq", [P, 2 * 512], BF16)
    nrs = sb("nrs", [P, 512], F32)
    A32W = 544
    ar32 = sb("ar32", [P, 8 * A32W], F32)
    ar16 = sb("ar16", [P, 8 * 512], BF16)
    ps = stack.enter_context(nc.psum_tensor("ps", [P, 8 * 512], F32))

    def xs(k, c0, n):
        return x_sb[:, k * T + c0:k * T + c0 + n]

    def hs(k, c0, n):
        return h_sb[:, k * T + c0:k * T + c0 + n]

    def A32(i, n=512):
        return ar32[:, i * A32W:i * A32W + n]

    def A16(i, n=512):
        return ar16[:, i * 512:i * 512 + n]

    def pt(c, n=1):
        return ptab[:, c:c + n]

    bank_pool = [list(range(8))]
    bank_ctr = [0]

    def set_pool(lst):
        bank_pool[0] = list(lst)
        bank_ctr[0] = 0

    def newbank():
        pl = bank_pool[0]
        b = pl[bank_ctr[0] % len(pl)]
        bank_ctr[0] += 1
        return b

    def PS(b, n=512, p0=0, p1=P, c0=0):
        return ps[p0:p1, b * 512 + c0:b * 512 + c0 + n]

    def v3(ap, nseg):
        if nseg == 1:
            return ap.unsqueeze(1)
        return ap.rearrange("p (s l) -> p s l", s=nseg)

    def extv(g, j, H):
        nseg, L = (6, 64) if j == 0 else (1, 512)
        W = H + L
        a = ext[:, g * EXTW:g * EXTW + nseg * W]
        return v3(a, nseg), nseg, L

    U = lambda *a: tuple(a)

    slab_ctr = [0]

    def load_slab(src_ap, kk, deps=()):
        i = slab_ctr[0]
        slab_ctr[0] += 1
        s = i % NSLOT
        dst = ring[:, s * SLOTW:(s + 1) * SLOTW].rearrange("p (k n) -> p k n", k=kk)
        DMA("pool", U("ring", s), list(deps), [U("ring", s)], dst, src_ap)
        return s

    def slab_in(wd, l, n0):
        return wd[l].rearrange("(k p) n -> p k n", p=P)[:, :, n0:n0 + 512]

    def slab_rows(wd, l, r0):
        return wd[l, r0:r0 + 512, :].rearrange("(k p) n -> p k n", p=P)

    def RA(s, k, m):
        o = s * SLOTW + k * 512 + m * 128
        return ring[:, o:o + 128]

    def RAfull(s, k):
        o = s * SLOTW + k * 512
        return ring[:, o:o + 512]

    def RB(s, k, m):
        o = s * SLOTW + k * 1024 + m * 128
        return ring[:, o:o + 128]

    DMA("sp", "ptab", [], ["ptab"], ptab[:], ptab_d)
    def xload(j, deps=()):
        c0, n = TILES[j]
        src = xin.rearrange("p (k t) -> p k t", k=KC)[:, :, c0:c0 + n]
        dst = x_sb[:].rearrange("p (k t) -> p k t", k=KC)[:, :, c0:c0 + n]
        DMA("sp", U("xin", j), list(deps), [U("x", j, k) for k in range(KC)], dst, src)

    xload(0)
    xload(1)
    OP("pool", "memset", [], ["ones"], ones[:], 1.0)
    DMA("pool", "gmask", [], ["gmask", "ident"], gmask[:], gmask_d)

    nsq_ctr = [0]

    def rmsnorm_tile(j, out_fn):
        c0, n = TILES[j]
        b = newbank()
        for k in range(KC):
            q = nsq_ctr[0] % 2
            nsq_ctr[0] += 1
            sq = nsq[:, q * 512:q * 512 + n]
            OP("act", "activation", [U("x", j, k)], [U("nsq", q)], out=sq, in_=xs(k, c0, n), func=AF.Square)
            OP("pe", "matmul", [U("nsq", q), "ones"], [U("ps", b)], PS(b, n), lhsT=ones[:], rhs=sq,
               start=(k == 0), stop=(k == KC - 1))
        OP("act", "activation", [U("ps", b), "eps"], ["nrs"], out=nrs[:, 0:n], in_=PS(b, n), func=AF.Ln, bias=EPS_AP[0],
           scale=1.0 / D)
        OP("act", "activation", ["nrs"], ["nrs"], out=nrs[:, 0:n], in_=nrs[:, 0:n], func=AF.Exp, scale=-0.5)
        for k in range(KC):
            out_fn(k, nrs[:, 0:n])

    EPS_AP = [None]

    def norm_to_h(j, gbase):
        c0, n = TILES[j]

        def f(k, rstd):
            OP("dve", "scalar_tensor_tensor", [U("x", j, k), "nrs", "ptab"], [U("h", j, k)],
               out=hs(k, c0, n), in0=xs(k, c0, n), scalar=pt(gbase + k), in1=rstd, op0=ALU.mult, op1=ALU.mult)
        rmsnorm_tile(j, f)

    def add_to_x(j, o, b):
        c0, n = TILES[j]
        OP("dve", "tensor_tensor", [U("ps", b), U("x", j, o)], [U("x", j, o)],
           out=xs(o, c0, n), in0=PS(b, n), in1=xs(o, c0, n), op=ALU.add)

    def wout_stage(j, slot, rhs_list, rhs_units, stage=None):
        c0, n = TILES[j]
        for o in range(KC):
            b = newbank()
            for k in range(4):
                OP("pe", "matmul", [U("ring", slot), rhs_units[k]], [U("ps", b)], PS(b, n), lhsT=RB(slot, k, o),
                   rhs=rhs_list[k], start=(k == 0), stop=(k == 3))
            if stage is None or o % 2 == 1:
                add_to_x(j, o, b)
            else:
                sidx = stage[(o // 2) % len(stage)]
                sa = A32(sidx, n)
                OP("act", "activation", [U("ps", b)], [U("a32", sidx)], out=sa, in_=PS(b, n), func=AF.Copy)
                OP("pool", "tensor_tensor", [U("a32", sidx), U("x", j, o)], [U("x", j, o)], out=xs(o, c0, n), in0=sa,
                   in1=xs(o, c0, n), op=ALU.add)

    def hist_setup(j, H, st_d, i):
        for g in range(4):
            ev, nseg, L = extv(g, j, H)
            if j == 0:
                src = st_d.rearrange("p (i s g r) -> p i s g r", i=2, s=4, g=4)[:, i, :, g, :]
                DMA("sp", U("hist", g), [], [U("ext", g)], ev[:, 0:4, 0:H], src)
                OP("dve", "memset", [], [U("ext", g)], ev[:, 4:5, 0:H], 0.0)
            elif j == 1:
                pv, _, _ = extv(g, 0, H)
                OP("dve", "tensor_scalar", [U("ext", g), "ptab"], [U("ext", g)], out=ev[:, 0:1, 0:H],
                   in0=pv[:, 5:6, 64:64 + H], scalar1=pt(C_MASK), scalar2=None, op0=ALU.mult)
            else:
                OP("dve", "tensor_copy", [U("ext", g)], [U("ext", g)], out=ev[:, 0:1, 0:H], in_=ev[:, 0:1, 512:512 + H])

    def hist_mid(j, g, H):
        if j == 0:
            ev, nseg, L = extv(g, 0, H)
            OP("dve", "tensor_copy", [U("ext", g)], [U("ext", g)], out=ev[:, 5:6, 0:H], in_=ev[:, 4:5, 64:64 + H])

    def state_out(j, g, H, o_ap, i):
        ov = o_ap.rearrange("p (i s g r) -> p i s g r", i=2, s=5, g=4)
        if j == 0:
            ev, _, _ = extv(g, 0, H)
            DMA("sp", U("so", g), [U("ext", g)], [], ov[:, i, 0:4, g, :], ev[:, 0:4, 64:64 + H])
        elif j == NT - 1:
            ev, _, _ = extv(g, j, H)
            DMA("sp", U("so", g), [U("ext", g)], [], ov[:, i, 4:5, g, :], ev[:, 0:1, 512:512 + H])

    def even_layer(l, skip_norm0=False):
        i = l // 2
        set_pool(range(8))
        DMA("pool", "wmt0", [], ["wmt0"], wmt[:, 0:1024], wTm_d[:, i * 1024:(i + 1) * 1024])
        DMA("pool", "wmt1", [], ["wmt1"], wmt[:, 1024:2048], wTs_d[:, i * 1024:(i + 1) * 1024])
        for v in range(2):
            wv = wmt[:, v * 1024:(v + 1) * 1024].rearrange("p (h i) -> p h i", h=8)
            mk = gmask[:, v * 128:(v + 1) * 128].unsqueeze(1).broadcast_to([P, 8, 128])
            OP("dve", "tensor_tensor", ["wmt%d" % v, "gmask"], ["wmt%d" % v], out=wv, in0=wv, in1=mk, op=ALU.mult)
        DMA("sp", "btab", [], ["btab"], btab[:], btab_d[:, i * 512:(i + 1) * 512])
        DMA("sp", "lntab", [], ["lntab"], lntab[:], lntab_d[:, i * 1024:(i + 1) * 1024])
        DMA("pool", "poolw", [], ["poolw"], poolw[:], poolw_d[:, i * 512:(i + 1) * 512])
        rsb = [newbank(), newbank()]
        for hb in range(2):
            OP("pe", "matmul", ["wmt0", "ones"], [U("ps", rsb[hb])], PS(rsb[hb], 512), lhsT=ones[:],
               rhs=wmt[:, hb * 512:(hb + 1) * 512], start=True, stop=True)
        tab2 = ext[:, 0:512]
        for c in range(4):
            for half in range(2):
                hh = 2 * c + half
                p0 = 64 * half
                OP("dve", "scalar_tensor_tensor", [U("ps", rsb[hh // 4]), "ptab", "btab"], [U("ext", 0)],
                   out=ext[p0:p0 + 64, c * 128:(c + 1) * 128],
                   in0=PS(rsb[hh // 4], 128, p0, p0 + 64, (hh % 4) * 128),
                   scalar=ptab[p0:p0 + 64, C_ALB + i * 4 + c:C_ALB + i * 4 + c + 1], in1=btab[p0:p0 + 64, c * 128:(c + 1) * 128],
                   op0=ALU.mult, op1=ALU.add)

        sV = load_slab(slab_in(w_in_even, i, 512), 8)
        sU = load_slab(slab_in(w_in_even, i, 0), 8)
        sOA = load_slab(slab_rows(w_out_even, i, 0), 4)
        gel_ctr = [0]
        mixb = [0, 1, 2, 3]
        set_pool([4, 5, 6, 7])
        VB = [0, 1, 6, 7]
        GEL = [4, 5, 6, 7]

        def a_vmm(j):
            c0, n = TILES[j]
            ngrp = n // 128
            for gi in range(ngrp):
                g0 = c0 + gi * 128
                b = newbank()
                for k in range(KC):
                    OP("pe", "matmul", [U("ring", sV), U("h", j, k)], [U("ps", b)], PS(b, 512), lhsT=hs(k, g0, 128),
                       rhs=RAfull(sV, k), start=(k == 0), stop=(k == KC - 1))
                gel = A32(GEL[gi])
                gu = U("a32", GEL[gi])
                OP("act", "activation", [U("ps", b)], [gu], out=gel, in_=PS(b, 512), func=AF.Gelu_apprx_tanh)
                st = small[:, 24 + gi * 6:24 + gi * 6 + 6]
                OP("dve", "bn_stats", [gu], [U("bst", gi)], out=st, in_=gel)
                OP("dve", "bn_aggr", [U("bst", gi)], ["mvall"], out=small[:, 2 * gi:2 * gi + 2], in_=st)

        def a_chain(j):
            c0, n = TILES[j]
            ngrp = n // 128
            mvall = small[:, 0:2 * ngrp].rearrange("p (g t) -> p g t", t=2)
            rsall = small[:, 16:16 + ngrp]
            OP("act", "activation", ["mvall", "eps"], ["rsall"], out=rsall, in_=mvall[:, :, 1], func=AF.Ln, bias=EPS_AP[0], scale=1.0)
            OP("act", "activation", ["rsall"], ["rsall"], out=rsall, in_=rsall, func=AF.Exp, scale=-0.5)
            for gi in range(ngrp):
                samp = (j == 0 and gi < 2)
                gel = A32(GEL[gi])
                gu = U("a32", GEL[gi])
                vb = A16(VB[gi])
                vu = U("a16", VB[gi])
                if not samp:
                    OP("dve", "tensor_scalar", [gu, "mvall", "rsall"], [vu], out=vb, in0=gel, scalar1=small[:, 2 * gi:2 * gi + 1],
                       scalar2=small[:, 16 + gi:17 + gi], op0=ALU.subtract, op1=ALU.mult)
                    continue
                OP("dve", "tensor_scalar", [gu, "mvall", "rsall"], [gu], out=gel, in0=gel, scalar1=small[:, 2 * gi:2 * gi + 1],
                   scalar2=small[:, 16 + gi:17 + gi], op0=ALU.subtract, op1=ALU.mult)
                OP(A_OFF, "tensor_tensor", [gu, "lntab"], [gu], out=gel, in0=gel, in1=lntab[:, 0:512], op=ALU.mult)
                OP(A_OFF, "tensor_tensor", [gu, "lntab"], [vu], out=vb, in0=gel, in1=lntab[:, 512:1024], op=ALU.add)
                if samp:
                    OP(A_OFF, "tensor_tensor", [gu, "lntab"], [U("a32", 7)], out=A32(7), in0=gel, in1=lntab[:, 512:1024],
                       op=ALU.add)
                    dstv = o_v.rearrange("p (g i f) -> p g i f", g=2, i=2)[:, gi, i, :]
                    DMA("sp", "ov", [U("a32", 7)], [], dstv, A32(7))

        def a_umm(j):
            c0, n = TILES[j]
            for c in range(4):
                b = newbank()
                for k in range(KC):
                    OP("pe", "matmul", [U("ring", sU), U("h", j, k)], [U("ps", b)], PS(b, n), lhsT=RA(sU, k, c),
                       rhs=hs(k, c0, n), start=(k == 0), stop=(k == KC - 1))
                OP("act", "activation", [U("ps", b)], [U("a32", c)], out=A32(c, n), in_=PS(b, n), func=AF.Gelu_apprx_tanh)

        def a_gating(j):
            c0, n = TILES[j]
            ngrp = n // 128
            for gi in range(ngrp):
                samp = (j == 0 and gi < 2)
                var = 1 if samp else 0
                vb = A16(VB[gi])
                for hh in range(8):
                    c = hh // 2
                    pp = (hh % 2) * 64
                    OP("pe", "matmul", [U("a16", VB[gi]), "wmt%d" % var], [U("ps", mixb[c])],
                       PS(mixb[c], 128, pp, pp + 64, gi * 128), lhsT=vb[:, hh * 64:(hh + 1) * 64],
                       rhs=wmt[:, var * 1024 + hh * 128:var * 1024 + (hh + 1) * 128], start=True, stop=True)

        def a_ya(j):
            c0, n = TILES[j]
            for c in range(4):
                tb = A32(4 + c, n)
                tu = U("a32", 4 + c)
                bt = btab[:, c * 128:(c + 1) * 128]
                t2 = ext[:, c * 128:(c + 1) * 128]
                if j == 0:
                    OP("dve", "tensor_tensor", [U("ps", mixb[c]), "btab"], [tu], out=v3(tb[:, 0:256], 4),
                       in0=v3(PS(mixb[c], 256), 4), in1=bt[:, 0:64].unsqueeze(1).broadcast_to([P, 4, 64]), op=ALU.add)
                    OP("dve", "scalar_tensor_tensor", [U("ps", mixb[c]), U("ext", 0), "ptab"], [tu], out=tb[:, 256:384],
                       in0=PS(mixb[c], 128, 0, P, 256), scalar=pt(C_ALG + i * 4 + c), in1=t2, op0=ALU.mult, op1=ALU.add)
                else:
                    OP("dve", "scalar_tensor_tensor", [U("ps", mixb[c]), U("ext", 0), "ptab"], [tu], out=v3(tb, 4),
                       in0=v3(PS(mixb[c], 512), 4), scalar=pt(C_ALG + i * 4 + c),
                       in1=t2.unsqueeze(1).broadcast_to([P, 4, 128]), op0=ALU.mult, op1=ALU.add)
                OP(A_OFF, "tensor_tensor", [tu, U("a32", c)], [U("a16", 2 + c)], out=A16(2 + c, n), in0=tb,
                   in1=A32(c, n), op=ALU.mult)

        bst_ = {}

        def zp_mm(j):
            c0, n = TILES[j]
            sPz_ = bst_["sPz"]
            for g in range(4):
                for k in range(KC):
                    OP("pe", "matmul", [U("ring", sPz_), U("h", j, k)], [U("ps", g)], PS(g, n), lhsT=RA(sPz_, k, g),
                       rhs=hs(k, c0, n), start=(k == 0), stop=(k == KC - 1))

        if not skip_norm0:
            norm_to_h(0, C_MIXG + l * 8)
        if l == 0:
            xload(2, [U("h", 0, KC - 1)])
        a_vmm(0)
        a_chain(0)
        a_umm(0)
        norm_to_h(1, C_MIXG + l * 8)
        if l == 0:
            xload(3, [U("h", 1, KC - 1)])
        for j, (c0, n) in enumerate(TILES):
            a_gating(j)
            a_ya(j)
            if j + 1 < NT:
                a_vmm(j + 1)
            else:
                dl = [U("h", 2, KC - 1)] if l == 0 else []
                bst_["sPz"] = load_slab(slab_in(w_in_even, i, 1024), 8, deps=dl)
                bst_["sOB"] = load_slab(slab_rows(w_out_even, i, 512), 4, deps=dl)
                zp_mm(0)
            wout_stage(j, sOA, [A16(2 + c, n) for c in range(4)], [U("a16", 2 + c) for c in range(4)])
            if j + 1 < NT:
                a_chain(j + 1)
                a_umm(j + 1)
                if j + 2 < NT:
                    norm_to_h(j + 2, C_MIXG + l * 8)
                    if l == 0 and j + 4 < NT:
                        xload(j + 4, [U("h", j + 2, KC - 1)])

        sPz = bst_["sPz"]
        sOB = bst_["sOB"]
        H = 15
        set_pool([4, 5, 6, 7])
        def b_evac(j):
            c0, n = TILES[j]
            hist_setup(j, H, stp_d, i)
            for g in range(4):
                ev, nseg, L = extv(g, j, H)
                eu = U("ext", g)
                OP("act", "activation", [U("ps", g)], [eu], out=ev[:, :, H:H + L], in_=v3(PS(g, n), nseg), func=AF.Copy)
                hist_mid(j, g, H)
                state_out(j, g, H, o_pool, i)

        def b_sums(j):
            c0, n = TILES[j]
            for g in range(4):
                ev, nseg, L = extv(g, j, H)
                eu = U("ext", g)
                wwin = 2 << g
                W = H + L
                bufs = [v3(A32(2 * (g % 2), nseg * W), nseg), v3(A32(2 * (g % 2) + 1, nseg * W), nseg)]
                bunits = [U("a32", 2 * (g % 2)), U("a32", 2 * (g % 2) + 1)]
                cur, cur_u = ev, eu
                sh = 1
                step = 0
                while sh < wwin:
                    lo = H - (wwin - 2 * sh)
                    dst, dst_u = bufs[step % 2], bunits[step % 2]
                    OP("dve", "tensor_tensor", [cur_u], [dst_u], out=dst[:, :, lo:W], in0=cur[:, :, lo:W],
                       in1=cur[:, :, lo - sh:W - sh], op=ALU.add)
                    cur, cur_u = dst, dst_u
                    sh *= 2
                    step += 1
                pooled = A16(g, n)
                OP("dve", "scalar_tensor_tensor", [cur_u, eu], [U("a16", g)], out=v3(pooled, nseg), in0=cur[:, :, H:H + L],
                   scalar=1.0 / wwin, in1=ev[:, :, H:H + L], op0=ALU.mult, op1=ALU.subtract)
                if j == 1:
                    tmp = small[:, 48:64]
                    OP("dve", "tensor_tensor", [cur_u, "ptab"], ["small32"], out=tmp, in0=cur[:, 0, H:H + 16],
                       in1=pt(C_ICNT + g * 16, 16), op=ALU.mult)
                    OP("dve", "tensor_tensor", ["small32", eu], [U("a16", g)], out=pooled[:, 0:16], in0=tmp,
                       in1=ev[:, 0, H:H + 16], op=ALU.subtract)

        def b_poolw(j):
            c0, n = TILES[j]
            for g in range(4):
                b2 = newbank()
                OP("pe", "matmul", ["poolw", U("a16", g)], [U("ps", b2)], PS(b2, n), lhsT=poolw[:, g * 128:(g + 1) * 128],
                   rhs=A16(g, n), start=True, stop=True)
                OP("act", "activation", [U("ps", b2), "ptab"], [U("a16", 4 + g)], out=A16(4 + g, n), in_=PS(b2, n),
                   func=AF.Copy, scale=pt(C_BSC + i * 4 + g))

        def b_wout(j):
            c0, n = TILES[j]
            wout_stage(j, sOB, [A16(4 + g, n) for g in range(4)], [U("a16", 4 + g) for g in range(4)])

        for j in range(NT):
            b_evac(j)
            if j + 1 < NT:
                zp_mm(j + 1)
            b_sums(j)
            if j >= 1:
                b_wout(j - 1)
            b_poolw(j)
        b_wout(NT - 1)

    dg_ctr = [0]

    def odd_layer(l, skip_norm0=False):
        i = l // 2
        sCv = load_slab(slab_in(w_in_odd, i, 0), 8)
        sCg = load_slab(slab_in(w_in_odd, i, 512), 8)
        sOA = load_slab(slab_rows(w_out_odd, i, 0), 4)
        H = 30
        sig_ctr = [0]
        set_pool([0, 1])
        bm, bq = 6, 7
        ext16 = ar32[:, 2 * A32W:2 * A32W + 2 * EXTW].bitcast(BF16)
        dgb = ar32[:, 5 * A32W:5 * A32W + 512].bitcast(BF16)
        dgb2 = lntab[:].bitcast(BF16)
        NDG = 16
        E16U = [U("a32", 2), U("a32", 3), U("a32", 4)]

        def e16v(g, j):
            nseg, L = (6, 64) if j == 0 else (1, 512)
            W = H + L
            return v3(ext16[:, g * EXTW:g * EXTW + nseg * W], nseg), nseg, L

        def vg(j, g, banks=None):
            c0, n = TILES[j]
            ev, nseg, L = extv(g, j, H)
            eu = U("ext", g)
            bv = banks[0] if banks else newbank()
            for k in range(KC):
                OP("pe", "matmul", [U("ring", sCv), U("h", j, k)], [U("ps", bv)], PS(bv, n), lhsT=RA(sCv, k, g),
                   rhs=hs(k, c0, n), start=(k == 0), stop=(k == KC - 1))
            bg = banks[1] if banks else newbank()
            for k in range(KC):
                OP("pe", "matmul", [U("ring", sCg), U("h", j, k)], [U("ps", bg)], PS(bg, n), lhsT=RA(sCg, k, g),
                   rhs=hs(k, c0, n), start=(k == 0), stop=(k == KC - 1))
            q = sig_ctr[0] % 2
            sig_ctr[0] += 1
            sg = A32(q, n)
            OP("act", "activation", [U("ps", bg)], [U("a32", q)], out=sg, in_=PS(bg, n), func=AF.Sigmoid)
            OP("dve", "tensor_tensor", [U("ps", bv), U("a32", q)], [eu], out=ev[:, :, H:H + L], in0=v3(PS(bv, n), nseg),
               in1=v3(sg, nseg), op=ALU.mult)
            hist_mid(j, g, H)
            state_out(j, g, H, o_c, i)
            e16, _, _ = e16v(g, j)
            OP("dve", "tensor_copy", [eu], (E16U if j == 0 else []) + [U("e16", g)], out=e16, in_=ev)

        def conv(j, g):
            c0, n = TILES[j]
            e16, nseg, L = e16v(g, j)
            wb = C_CCW + (i * 4 + g) * 31
            for k0 in range(0, 31, 4):
                nk = min(4, 31 - k0)
                gq = dg_ctr[0] % (NDG // 4)
                dg_ctr[0] += 1
                dgw = [U("dg", gq)] + (["lntab"] if first_dg[0] > 0 else [])
                first_dg[0] -= 4
                dg4 = dgb2[:, gq * 512:gq * 512 + nk * 128].rearrange("p (t m) -> p t m", t=nk)
                OP("pool", "tensor_tensor", ["ident", "ptab"], dgw, out=dg4,
                   in0=ident.unsqueeze(1).broadcast_to([P, nk, 128]),
                   in1=pt(wb + k0, nk).unsqueeze(2).broadcast_to([P, nk, 128]), op=ALU.mult)
                for t in range(nk):
                    k = k0 + t
                    OP("pe", "matmul", [U("dg", gq), U("e16", g)], [U("ps", 2 + g)], v3(PS(2 + g, n), nseg),
                       lhsT=dgb2[:, gq * 512 + t * 128:gq * 512 + (t + 1) * 128],
                       rhs=e16[:, :, k:k + L], start=(k == 0), stop=(k == 30))
            q = sig_ctr[0] % 2
            cb = A16(q, n)
            sq = A16(2 + q, n)
            OP("act", "activation", [U("ps", 2 + g), "ptab"], [U("a16", q)], out=cb, in_=PS(2 + g, n), func=AF.Identity,
               bias=pt(C_CCB + i * 4 + g), scale=1.0)
            OP("act", "activation", [U("ps", 2 + g), "ptab"], [U("a16", 2 + q)], out=sq, in_=PS(2 + g, n), func=AF.Square,
               bias=pt(C_CCB + i * 4 + g), scale=1.0)
            OP("pe", "matmul", [U("a16", q), "ones"], [U("ps", bm)], PS(bm, n), lhsT=ones[:], rhs=cb,
               start=(g == 0), stop=(g == 3))
            OP("pe", "matmul", [U("a16", 2 + q), "ones"], [U("ps", bq)], PS(bq, n), lhsT=ones[:], rhs=sq,
               start=(g == 0), stop=(g == 3))
            sig_ctr[0] += 1

        ABUF = [(btab[:, 0:512], "btab"), (wmt[:, 0:1024].bitcast(F32), "wmt0"), (wmt[:, 1024:2048].bitcast(F32), "wmt1"),
                (A32(5), U("a32", 5))]

        def ln_chain(j):
            c0, n = TILES[j]
            m2 = A32(6, n)
            mu = U("a32", 6)
            nm = A32(7, n)
            nu = U("a32", 7)
            OP("act", "activation", [U("ps", bm)], [nu], out=nm, in_=PS(bm, n), func=AF.Copy, scale=-1.0 / 512)
            OP("act", "activation", [U("ps", bm)], [mu], out=m2, in_=PS(bm, n), func=AF.Square, scale=1.0 / 512)
            for g in range(4):
                a, au = ABUF[g][0][:, 0:n], ABUF[g][1]
                OP("dve", "scalar_tensor_tensor", [U("ps", 2 + g), nu, "ptab"], [au], out=a, in0=PS(2 + g, n),
                   scalar=pt(C_CCB + i * 4 + g), in1=nm, op0=ALU.add, op1=ALU.add)
            OP("dve", "scalar_tensor_tensor", [U("ps", bq), mu], [mu], out=m2, in0=PS(bq, n), scalar=1.0 / 512, in1=m2,
               op0=ALU.mult, op1=ALU.subtract)
            OP("act", "activation", [mu, "eps"], [mu], out=m2, in_=m2, func=AF.Ln, bias=EPS_AP[0], scale=1.0)
            OP("act", "activation", [mu], [mu], out=m2, in_=m2, func=AF.Exp, scale=-0.5)

        def ln_post(j):
            c0, n = TILES[j]
            m2 = A32(6, n)
            mu = U("a32", 6)
            for g in range(4):
                a, au = ABUF[g][0][:, 0:n], ABUF[g][1]
                OP("dve", "tensor_tensor", [au, mu], [au], out=a, in0=a, in1=m2, op=ALU.mult)
                OP("act", "activation", [au, "ptab"], [U("a16", 4 + g)], out=A16(4 + g, n), in_=a, func=AF.Silu,
                   scale=pt(C_CLG + i * 4 + g), bias=pt(C_CLB + i * 4 + g))

        first_dg = [NDG]
        if not skip_norm0:
            norm_to_h(0, C_MIXG + l * 8)
        hist_setup(0, H, stc_d, i)
        vg(0, 0)
        vg(0, 1)
        conv(0, 0)
        for j, (c0, n) in enumerate(TILES):
            vg(j, 2)
            conv(j, 1)
            if j + 1 < NT:
                norm_to_h(j + 1, C_MIXG + l * 8)
            vg(j, 3)
            conv(j, 2)
            conv(j, 3)
            ln_chain(j)
            if j + 1 < NT:
                hist_setup(j + 1, H, stc_d, i)
                vg(j + 1, 0)
                vg(j + 1, 1, banks=(4, 5))
            ln_post(j)
            if j + 1 < NT:
                conv(j + 1, 0)
                wout_stage(j, sOA, [A16(4 + g, n) for g in range(4)], [U("a16", 4 + g) for g in range(4)])

        OP("dve", "memset", [U("e16", g) for g in range(4)] + [U("dg", q) for q in range(4)],
           E16U + ["lntab"], small[:, 20:21], 0.0)
        set_pool(range(8))
        sDc = load_slab(slab_in(w_in_odd, i, 1536), 8)
        sDx = load_slab(slab_in(w_in_odd, i, 2048), 8)
        sDb = load_slab(slab_in(w_in_odd, i, 1024), 8)
        sOB = load_slab(slab_rows(w_out_odd, i, 512), 4)
        H = 2
        ctr = [0]
        def d_chunk(j, g):
            c0, n = TILES[j]
            ev, nseg, L = extv(g, j, H)
            eu = U("ext", g)
            b1 = newbank()
            for k in range(KC):
                OP("pe", "matmul", [U("ring", sDc), U("h", j, k)], [U("ps", b1)], PS(b1, n), lhsT=RA(sDc, k, g),
                   rhs=hs(k, c0, n), start=(k == 0), stop=(k == KC - 1))
            b2 = newbank()
            for k in range(KC):
                OP("pe", "matmul", [U("ring", sDx), U("h", j, k)], [U("ps", b2)], PS(b2, n), lhsT=RA(sDx, k, g),
                   rhs=hs(k, c0, n), start=(k == 0), stop=(k == KC - 1))
            b3 = newbank()
            for k in range(KC):
                OP("pe", "matmul", [U("ring", sDb), U("h", j, k)], [U("ps", b3)], PS(b3, n), lhsT=RA(sDb, k, g),
                   rhs=hs(k, c0, n), start=(k == 0), stop=(k == KC - 1))
            q = ctr[0] % 2
            ctr[0] += 1
            gcs = A32(q, n)
            OP("act", "activation", [U("ps", b1)], [U("a32", q)], out=gcs, in_=PS(b1, n), func=AF.Copy)
            OP("dve", "tensor_tensor", [U("ps", b2), U("a32", q)], [eu], out=ev[:, :, H:H + L], in0=v3(PS(b2, n), nseg),
               in1=v3(gcs, nseg), op=ALU.mult)
            hist_mid(j, g, H)
            state_out(j, g, H, o_d, i)
            acc = v3(A32(2 + q, n), nseg)
            au = U("a32", 2 + q)
            wb = C_DCW + (i * 4 + g) * 3
            OP("dve", "tensor_scalar", [eu, "ptab"], [au], out=acc, in0=ev[:, :, 0:L], scalar1=pt(wb), scalar2=None,
               op0=ALU.mult)
            for k in range(1, 3):
                OP("dve", "scalar_tensor_tensor", [eu, "ptab", au], [au], out=acc, in0=ev[:, :, k:k + L],
                   scalar=pt(wb + k), in1=acc, op0=ALU.mult, op1=ALU.add)
            di = (j % 2) * 4 + g
            OP("dve", "tensor_tensor", [U("ps", b3), au], [U("a16", di)], out=A16(di, n), in0=PS(b3, n), in1=A32(2 + q, n),
               op=ALU.mult)

        def d_wout(j):
            c0, n = TILES[j]
            par = (j % 2) * 4
            wout_stage(j, sOB, [A16(par + g, n) for g in range(4)], [U("a16", par + g) for g in range(4)])

        for j in range(NT):
            hist_setup(j, H, std_d, i)
            for g in range(4):
                d_chunk(j, g)
                if g == 0 and j == 0:
                    nl_ = TILES[NT - 1][1]
                    wout_stage(NT - 1, sOA, [A16(4 + gg, nl_) for gg in range(4)], [U("a16", 4 + gg) for gg in range(4)])
                if g == 0 and j >= 1:
                    d_wout(j - 1)
                if g == 2 and j >= 1:
                    norm_to_h(j - 1, C_FFNG + l * 8)
        d_wout(NT - 1)
        norm_to_h(NT - 1, C_FFNG + l * 8)

    def ffn(l, after_last=None, after_tile0=None):
        set_pool(range(8))
        rctr = [0]
        items = [(p, j) for p in range(8) for j in range(NT)]
        slots = {}

        def stage1(t):
            p, j = items[t]
            c0, n = TILES[j]
            if j == 0:
                s1 = load_slab(slab_in(w_ff1, l, 512 * p), 8)
                s2 = load_slab(slab_rows(w_ff2, l, 512 * p), 4)
                slots[p] = (s1, s2)
            s1, s2 = slots[p]
            if l % 2 == 0 and p == 0 and j == 0:
                norm_to_h(0, C_FFNG + l * 8)
            aset = t % 2
            for c in range(4):
                b = newbank()
                for k in range(KC):
                    OP("pe", "matmul", [U("ring", s1), U("h", j, k)], [U("ps", b)], PS(b, n), lhsT=RA(s1, k, c),
                       rhs=hs(k, c0, n), start=(k == 0), stop=(k == KC - 1))
                q = rctr[0] % 4
                rctr[0] += 1
                r_ = A32(q, n)
                OP("act", "activation", [U("ps", b)], [U("a32", q)], out=r_, in_=PS(b, n), func=AF.Relu)
                OP("pool" if (l % 2 == 0 and p == 0 and c % 2 == 0) else "dve", "tensor_tensor", [U("a32", q)],
                   [U("a16", aset * 4 + c)], out=A16(aset * 4 + c, n), in0=r_, in1=r_, op=ALU.mult)
            if l % 2 == 0 and p == 0 and j + 1 < NT:
                norm_to_h(j + 1, C_FFNG + l * 8)

        def stage2(t):
            p, j = items[t]
            c0, n = TILES[j]
            s1, s2 = slots[p]
            aset = t % 2
            wout_stage(j, s2, [A16(aset * 4 + c, n) for c in range(4)], [U("a16", aset * 4 + c) for c in range(4)])

        for t in range(len(items) + 1):
            if t < len(items):
                stage1(t)
            if t >= 1:
                stage2(t - 1)
                pp_, jj_ = items[t - 1]
                if pp_ == 7 and after_last is not None:
                    after_last(jj_)
                if pp_ == 7 and jj_ == 0 and after_tile0 is not None:
                    after_tile0()

    epsb = sb("epsb", [P, 1], F32)
    OP("dve", "memset", [], ["eps"], epsb[:], EPS)
    EPS_AP[0] = epsb[:]
    yctr = [0]

    def final_norm(j):
        c0, n = TILES[j]

        def f(k, rstd):
            q = 4 + yctr[0] % 4
            yctr[0] += 1
            yb = A32(q, n)
            OP("dve", "scalar_tensor_tensor", [U("x", j, k), "nrs", "ptab"], [U("a32", q)], out=yb, in0=xs(k, c0, n),
               scalar=pt(C_FING + k), in1=rstd, op0=ALU.mult, op1=ALU.mult)
            DMA("sp", U("yo", q), [U("a32", q)], [], yT[:, k * T + c0:k * T + c0 + n], yb)
        rmsnorm_tile(j, f)

    for l in range(nlayers):
        if l % 2 == 0:
            even_layer(l, skip_norm0=(l > 0))
        else:
            odd_layer(l, skip_norm0=(l > 0))
        last = (l == nlayers - 1)
        ffn(l, after_last=(final_norm if last else None),
            after_tile0=(None if last else (lambda l=l: norm_to_h(0, C_MIXG + (l + 1) * 8))))

    S.emit(nc, stack)
    stack.close()
    return nc


_NC_CACHE = {}


def _prep_inputs(inp):
    f = np.float32
    xp = np.asarray(inp["x_prompt"], f)
    xsm = np.asarray(inp["x_sample"], f)
    ptab_common = np.zeros((P, NPT), f)

    def pp(v):
        v = np.asarray(v, f)
        ch = v.shape[-1] // P
        v = v.reshape(v.shape[:-1] + (ch, P))
        return np.moveaxis(v, -1, 0)

    ptab_common[:, C_MIXG:C_MIXG + 32] = pp(inp["norm_mix_g"]).reshape(P, 32)
    ptab_common[:, C_FFNG:C_FFNG + 32] = pp(inp["norm_ffn_g"]).reshape(P, 32)
    ptab_common[:, C_FING:C_FING + 8] = pp(inp["final_norm_g"]).reshape(P, 8)
    ptab_common[:, C_BSC:C_BSC + 8] = pp(inp["b_scale"]).reshape(P, 8)
    ptab_common[:, C_CCB:C_CCB + 8] = pp(inp["c_conv_b"]).reshape(P, 8)
    ptab_common[:, C_CLG:C_CLG + 8] = pp(inp["c_ln_g"]).reshape(P, 8)
    ptab_common[:, C_CLB:C_CLB + 8] = pp(inp["c_ln_b"]).reshape(P, 8)
    ccw = pp(inp["c_conv_w"])
    ptab_common[:, C_CCW:C_CCW + 248] = np.transpose(ccw, (0, 1, 3, 2)).reshape(P, 248)
    dcw = pp(inp["d_conv_w"])
    ptab_common[:, C_DCW:C_DCW + 24] = np.transpose(dcw, (0, 1, 3, 2)).reshape(P, 24)
    ptab_common[:, C_ALG:C_ALG + 8] = pp(inp["a_ln_g"]).reshape(P, 8)
    ptab_common[:, C_ALB:C_ALB + 8] = pp(inp["a_ln_b"]).reshape(P, 8)

    aws = np.asarray(inp["a_w_s"], f)
    wTm = np.transpose(aws, (3, 0, 1, 2)).reshape(P, 2 * 8 * 128)
    idx = np.arange(128) % 64
    aws_s = aws[:, :, idx][:, :, :, idx]
    wTs = np.transpose(aws_s, (3, 0, 1, 2)).reshape(P, 2 * 8 * 128)
    jj = np.arange(128)[:, None] // 64
    ii = np.arange(128)[None, :] // 64
    gmask = np.concatenate([(jj <= ii).astype(f), (jj == ii).astype(f), np.eye(128, dtype=f)], axis=1)
    abs_ = np.asarray(inp["a_b_s"], f)
    hp = (np.arange(P) // 64)
    btab = np.zeros((P, 2, 4, 128), f)
    for c in range(4):
        btab[:, :, c, :] = np.transpose(abs_[:, 2 * c + hp, :], (1, 0, 2))
    btab = btab.reshape(P, 2 * 4 * 128)
    lng = np.asarray(inp["a_ln_g"], f)
    lnb = np.asarray(inp["a_ln_b"], f)
    lntab = np.stack([lng, lnb], axis=1)[None].repeat(P, axis=0).reshape(P, 2 * 2 * 512)
    poolw = np.transpose(np.asarray(inp["b_pool_w"], f), (2, 0, 1, 3)).reshape(P, 2 * 4 * 128)

    def st(v, H):
        v = np.asarray(v, f).reshape(2, 8, 4, H, 4, P)
        v = np.transpose(v, (1, 5, 0, 2, 4, 3))
        return v.reshape(8, P, -1)

    stp = st(inp["state_pool"], 15)
    stc = st(inp["state_conv_c"], 30)
    std = st(inp["state_conv_d"], 2)

    shared = dict(wTm=np.ascontiguousarray(wTm), wTs=np.ascontiguousarray(wTs), gmask=np.ascontiguousarray(gmask),
                  btab=btab, lntab=np.ascontiguousarray(lntab), poolw=np.ascontiguousarray(poolw))
    for nme in ("w_in_even", "w_out_even", "w_in_odd", "w_out_odd", "w_ff1", "w_ff2"):
        shared[nme] = np.ascontiguousarray(np.asarray(inp[nme], f))
    in_maps = []
    for c in range(NCORE):
        b, q = c // 4, c % 4
        tok = np.zeros((T, D), f)
        tok[0:TS] = xsm[4 * c:4 * c + 4].reshape(TS, D)
        if q > 0:
            tok[TS:TS + TH] = xp[b, q * TM - TH:q * TM]
        tok[TS + TH:] = xp[b, q * TM:(q + 1) * TM]
        xin = np.ascontiguousarray(tok.reshape(T, KC, P).transpose(2, 1, 0)).reshape(P, KC * T)
        pt_ = ptab_common.copy()
        pt_[:, C_MASK] = 0.0 if q == 0 else 1.0
        for g, w in enumerate((2, 4, 8, 16)):
            if q == 0:
                cntv = np.minimum(w, np.arange(16) + 1).astype(f)
            else:
                cntv = np.full(16, w, f)
            pt_[:, C_ICNT + g * 16:C_ICNT + (g + 1) * 16] = (1.0 / cntv)[None, :]
        m = dict(shared)
        m.update(xin=xin, ptab=pt_, stp=np.ascontiguousarray(stp[c]), stc=np.ascontiguousarray(stc[c]),
                 std=np.ascontiguousarray(std[c]))
        in_maps.append(m)
    return in_maps


def kernel(**inputs):
    return _run(inputs, DEPTH)


def _run(inputs, nlayers, trace=False):
    if nlayers not in _NC_CACHE:
        _NC_CACHE[nlayers] = build_program(nlayers)
    nc = _NC_CACHE[nlayers]
    in_maps = _prep_inputs(inputs)
    if trace:
        res = run_bass_kernel_spmd(nc, in_maps, core_ids=list(range(NCORE)), trace=True)
        _NC_CACHE["last_res"] = res
    else:
        res = run_bass_kernel_spmd(nc, in_maps, core_ids=list(range(NCORE)))
    R = res.results
    f = np.float32
    y_prompt = np.zeros((2, 8192, D), f)
    y_sample = np.zeros((32, 64, D), f)
    new_pool_p = np.zeros((2, 2, 15, 512), f)
    new_c_p = np.zeros((2, 2, 30, 512), f)
    new_d_p = np.zeros((2, 2, 2, 512), f)
    new_v_s = np.zeros((2, 32, 64, 512), f)
    new_pool_s = np.zeros((2, 32, 15, 512), f)
    new_c_s = np.zeros((2, 32, 30, 512), f)
    new_d_s = np.zeros((2, 32, 2, 512), f)
    for c in range(NCORE):
        b, q = c // 4, c % 4
        y = np.asarray(R[c]["yT"]).reshape(P, KC, T).transpose(2, 1, 0).reshape(T, D)
        y_sample[4 * c:4 * c + 4] = y[0:TS].reshape(4, 64, D)
        y_prompt[b, q * TM:(q + 1) * TM] = y[TS + TH:]
        ov = np.asarray(R[c]["o_v"]).reshape(P, 2, 2, 512)
        vv = ov.transpose(2, 1, 0, 3).reshape(2, 256, 512).reshape(2, 4, 64, 512)
        new_v_s[:, 4 * c:4 * c + 4] = vv
        for arr_s, arr_p, nme, H in ((new_pool_s, new_pool_p, "o_pool", 15), (new_c_s, new_c_p, "o_c", 30),
                                     (new_d_s, new_d_p, "o_d", 2)):
            o = np.asarray(R[c][nme]).reshape(P, 2, 5, 4, H)
            o = o.transpose(1, 2, 4, 3, 0).reshape(2, 5, H, 512)
            arr_s[:, 4 * c:4 * c + 4] = o[:, 0:4]
            if q == 3:
                arr_p[:, b] = o[:, 4]
    return (y_prompt, y_sample, new_pool_p, new_c_p, new_d_p, new_v_s, new_pool_s, new_c_s, new_d_s)
```

```python
import numpy as np
from contextlib import ExitStack
import concourse.bass as bass
import concourse.mybir as mybir
from concourse.bass_utils import run_bass_kernel_spmd

F32 = mybir.dt.float32
BF16 = mybir.dt.bfloat16
AF = mybir.ActivationFunctionType
ALU = mybir.AluOpType

P = 128
D = 1024
KC = 8
NCORE = 8
TS, TH, TM = 256, 128, 2048
T = TS + TH + TM
TILES = [(0, 384)] + [(384 + 512 * i, 512) for i in range(4)]
NT = len(TILES)
DEPTH = 4
EPS = 1e-6
NSLOT = 5
DG_ENG = ("dve", "pool", "dve", "act")
A_OFF = "dve"
SLOTW = 4096
EXTW = 576

C_MIXG, C_FFNG, C_FING, C_BSC, C_CCB, C_CLG, C_CLB = 0, 32, 64, 72, 80, 88, 96
C_CCW, C_DCW, C_MASK, C_ICNT = 104, 352, 376, 377
C_ALG, C_ALB = 448, 456
NPT = 464


class Sched:
    ENG = ("pe", "act", "dve", "pool", "sp")

    def __init__(self):
        self.ops = []
        self.lastw = {}
        self.readers = {}
        self.lastdma = {}

    def _add(self, eng, fn, r, w, kind, key=None):
        i = len(self.ops)
        deps = set()
        for u in r:
            j = self.lastw.get(u)
            if j is not None:
                deps.add(j)
        for u in w:
            j = self.lastw.get(u)
            if j is not None:
                deps.add(j)
            deps.update(self.readers.get(u, ()))
        if kind == "d":
            j = self.lastdma.get(key)
            if j is not None:
                deps.add(j)
            self.lastdma[key] = i
        for u in r:
            self.readers.setdefault(u, []).append(i)
        for u in w:
            self.lastw[u] = i
            self.readers[u] = []
        self.ops.append(dict(eng=eng, fn=fn, deps=deps, kind=kind, key=key, ev=None))
        return i

    def op(self, eng, fn, r=(), w=()):
        return self._add(eng, fn, r, w, "c")

    def dma(self, eng, fn, key, r=(), w=()):
        return self._add(eng, fn, r, w, "d", key)

    def emit(self, nc, stack):
        ops = self.ops
        n = len(ops)
        need = [False] * n
        for op in ops:
            for d in op["deps"]:
                od = ops[d]
                if od["kind"] == "c":
                    if od["eng"] == "pe" and op["eng"] == "pe" and op["kind"] == "c":
                        continue
                    need[d] = True
        cnt = {e: 0 for e in self.ENG}
        dcnt = {}
        for i, op in enumerate(ops):
            if op["kind"] == "c":
                if need[i]:
                    cnt[op["eng"]] += 1
                    op["ev"] = (("eng", op["eng"]), cnt[op["eng"]])
            else:
                k = op["key"]
                dcnt[k] = dcnt.get(k, 0) + 16
                op["ev"] = (("dma", k), dcnt[k])
        sems = {}
        for e in self.ENG:
            sems[("eng", e)] = stack.enter_context(nc.semaphore("s_" + e))
        for idx, k in enumerate(dcnt):
            sems[("dma", k)] = stack.enter_context(nc.semaphore("d_%d" % idx))
        block = stack.enter_context(nc.Block())
        by_eng = {e: [] for e in self.ENG}
        for i, op in enumerate(ops):
            by_eng[op["eng"]].append(i)

        def run(eng_name, eh):
            waited = {}
            for i in by_eng[eng_name]:
                op = ops[i]
                w = {}
                for d in op["deps"]:
                    od = ops[d]
                    if od["kind"] == "c" and od["eng"] == "pe" and eng_name == "pe" and op["kind"] == "c":
                        continue
                    s, v = od["ev"]
                    if v > w.get(s, 0):
                        w[s] = v
                for s, v in w.items():
                    if waited.get(s, 0) < v:
                        eh.wait_ge(sems[s], v)
                        waited[s] = v
                nm, a_, kw_ = op["fn"]
                ins = getattr(eh, nm)(*a_, **kw_)
                if op["kind"] == "c":
                    if need[i]:
                        ins.then_inc(sems[("eng", eng_name)], 1)
                else:
                    ins.then_inc(sems[("dma", op["key"])], 16)
            if eng_name == "sp":
                for k, v in dcnt.items():
                    eh.wait_ge(sems[("dma", k)], v)

        @block.tensor
        def _(e):
            run("pe", e)

        @block.scalar
        def _(e):
            run("act", e)

        @block.vector
        def _(e):
            run("dve", e)

        @block.gpsimd
        def _(e):
            run("pool", e)

        @block.sync
        def _(e):
            run("sp", e)


def build_program(nlayers=DEPTH):
    nc = bass.Bass("TRN2", target_bir_lowering=False)
    stack = ExitStack()
    S = Sched()

    def OP(eng, name, r, w, *a, **kw):
        S._add(eng, (name, a, kw), r, w, "c")

    def DMA(eng, key, r, w, out, in_):
        S._add(eng, ("dma_start", (), dict(out=out, in_=in_)), r, w, "d", key)

    def din(name, shape, dt=F32):
        return nc.dram_tensor(name, shape, dt, kind="ExternalInput").ap()

    def dout(name, shape, dt=F32):
        return nc.dram_tensor(name, shape, dt, kind="ExternalOutput").ap()

    xin = din("xin", [P, KC * T])
    ptab_d = din("ptab", [P, NPT])
    stp_d = din("stp", [P, 2 * 4 * 4 * 15])
    stc_d = din("stc", [P, 2 * 4 * 4 * 30])
    std_d = din("std", [P, 2 * 4 * 4 * 2])
    wTm_d = din("wTm", [P, 2 * 8 * 128])
    wTs_d = din("wTs", [P, 2 * 8 * 128])
    gmask_d = din("gmask", [P, 3 * 128])
    btab_d = din("btab", [P, 2 * 4 * 128])
    lntab_d = din("lntab", [P, 2 * 2 * 512])
    poolw_d = din("poolw", [P, 2 * 4 * 128])
    w_in_even = din("w_in_even", [2, 1024, 1536])
    w_out_even = din("w_out_even", [2, 1024, 1024])
    w_in_odd = din("w_in_odd", [2, 1024, 2560])
    w_out_odd = din("w_out_odd", [2, 1024, 1024])
    w_ff1 = din("w_ff1", [4, 1024, 4096])
    w_ff2 = din("w_ff2", [4, 4096, 1024])
    yT = dout("yT", [P, KC * T])
    o_pool = dout("o_pool", [P, 2 * 5 * 4 * 15])
    o_c = dout("o_c", [P, 2 * 5 * 4 * 30])
    o_d = dout("o_d", [P, 2 * 5 * 4 * 2])
    o_v = dout("o_v", [P, 2 * 2 * 512])

    def sb(name, shape, dt):
        return stack.enter_context(nc.sbuf_tensor(name, shape, dt))

    x_sb = sb("x_sb", [P, KC * T], F32)
    h_sb = sb("h_sb", [P, KC * T], BF16)
    ring = sb("ring", [P, NSLOT * SLOTW], BF16)
    ext = sb("ext", [P, 4 * EXTW], F32)
    wmt = sb("wmt", [P, 2 * 8 * 128], BF16)
    gmask = sb("gmask_sb", [P, 3 * 128], BF16)
    ident = gmask[:, 256:384]
    btab = sb("btab_sb", [P, 4 * 128], F32)
    lntab = sb("lntab_sb", [P, 2 * 512], F32)
    poolw = sb("poolw_sb", [P, 4 * 128], BF16)
    ptab = sb("ptab_sb", [P, NPT], F32)
    ones = sb("ones", [P, P], BF16)
    small = sb("small", [P, 64], F32)
    nsq = sb("nsq", [P, 2 * 512], BF16)
    nrs = sb("nrs", [P, 512], F32)
    A32W = 544
    ar32 = sb("ar32", [P, 8 * A32W], F32)
    ar16 = sb("ar16", [P, 8 * 512], BF16)
    ps = stack.enter_context(nc.psum_tensor("ps", [P, 8 * 512], F32))

    def xs(k, c0, n):
        return x_sb[:, k * T + c0:k * T + c0 + n]

    def hs(k, c0, n):
        return h_sb[:, k * T + c0:k * T + c0 + n]

    def A32(i, n=512):
        return ar32[:, i * A32W:i * A32W + n]

    def A16(i, n=512):
        return ar16[:, i * 512:i * 512 + n]

    def pt(c, n=1):
        return ptab[:, c:c + n]

    bank_pool = [list(range(8))]
    bank_ctr = [0]

    def set_pool(lst):
        bank_pool[0] = list(lst)
        bank_ctr[0] = 0

    def newbank():
        pl = bank_pool[0]
        b = pl[bank_ctr[0] % len(pl)]
        bank_ctr[0] += 1
        return b

    def PS(b, n=512, p0=0, p1=P, c0=0):
        return ps[p0:p1, b * 512 + c0:b * 512 + c0 + n]

    def v3(ap, nseg):
        if nseg == 1:
            return ap.unsqueeze(1)
        return ap.rearrange("p (s l) -> p s l", s=nseg)

    def extv(g, j, H):
        nseg, L = (6, 64) if j == 0 else (1, 512)
        W = H + L
        a = ext[:, g * EXTW:g * EXTW + nseg * W]
        return v3(a, nseg), nseg, L

    U = lambda *a: tuple(a)

    slab_ctr = [0]

    def load_slab(src_ap, kk, deps=()):
        i = slab_ctr[0]
        slab_ctr[0] += 1
        s = i % NSLOT
        dst = ring[:, s * SLOTW:(s + 1) * SLOTW].rearrange("p (k n) -> p k n", k=kk)
        DMA("pool", U("ring", s), list(deps), [U("ring", s)], dst, src_ap)
        return s

    def slab_in(wd, l, n0):
        return wd[l].rearrange("(k p) n -> p k n", p=P)[:, :, n0:n0 + 512]

    def slab_rows(wd, l, r0):
        return wd[l, r0:r0 + 512, :].rearrange("(k p) n -> p k n", p=P)

    def RA(s, k, m):
        o = s * SLOTW + k * 512 + m * 128
        return ring[:, o:o + 128]

    def RAfull(s, k):
        o = s * SLOTW + k * 512
        return ring[:, o:o + 512]

    def RB(s, k, m):
        o = s * SLOTW + k * 1024 + m * 128
        return ring[:, o:o + 128]

    DMA("sp", "ptab", [], ["ptab"], ptab[:], ptab_d)
    def xload(j, deps=()):
        c0, n = TILES[j]
        src = xin.rearrange("p (k t) -> p k t", k=KC)[:, :, c0:c0 + n]
        dst = x_sb[:].rearrange("p (k t) -> p k t", k=KC)[:, :, c0:c0 + n]
        DMA("sp", U("xin", j), list(deps), [U("x", j, k) for k in range(KC)], dst, src)

    xload(0)
    xload(1)
    OP("pool", "memset", [], ["ones"], ones[:], 1.0)
    DMA("pool", "gmask", [], ["gmask", "ident"], gmask[:], gmask_d)

    nsq_ctr = [0]

    def rmsnorm_tile(j, out_fn):
        c0, n = TILES[j]
        b = newbank()
        for k in range(KC):
            q = nsq_ctr[0] % 2
            nsq_ctr[0] += 1
            sq = nsq[:, q * 512:q * 512 + n]
            OP("act", "activation", [U("x", j, k)], [U("nsq", q)], out=sq, in_=xs(k, c0, n), func=AF.Square)
            OP("pe", "matmul", [U("nsq", q), "ones"], [U("ps", b)], PS(b, n), lhsT=ones[:], rhs=sq,
               start=(k == 0), stop=(k == KC - 1))
        OP("act", "activation", [U("ps", b), "eps"], ["nrs"], out=nrs[:, 0:n], in_=PS(b, n), func=AF.Ln, bias=EPS_AP[0],
           scale=1.0 / D)
        OP("act", "activation", ["nrs"], ["nrs"], out=nrs[:, 0:n], in_=nrs[:, 0:n], func=AF.Exp, scale=-0.5)
        for k in range(KC):
            out_fn(k, nrs[:, 0:n])

    EPS_AP = [None]

    def norm_to_h(j, gbase):
        c0, n = TILES[j]

        def f(k, rstd):
            OP("dve", "scalar_tensor_tensor", [U("x", j, k), "nrs", "ptab"], [U("h", j, k)],
               out=hs(k, c0, n), in0=xs(k, c0, n), scalar=pt(gbase + k), in1=rstd, op0=ALU.mult, op1=ALU.mult)
        rmsnorm_tile(j, f)

    def add_to_x(j, o, b, nov=None):
        c0, n = TILES[j]
        n = nov or n
        OP("dve", "tensor_tensor", [U("ps", b), U("x", j, o)], [U("x", j, o)],
           out=xs(o, c0, n), in0=PS(b, n), in1=xs(o, c0, n), op=ALU.add)

    def wout_stage(j, slot, rhs_list, rhs_units, stage=None, nov=None):
        c0, n = TILES[j]
        n = nov or n
        for o in range(KC):
            b = newbank()
            for k in range(4):
                OP("pe", "matmul", [U("ring", slot), rhs_units[k]], [U("ps", b)], PS(b, n), lhsT=RB(slot, k, o),
                   rhs=rhs_list[k], start=(k == 0), stop=(k == 3))
            if stage is None or o % 2 == 1:
                add_to_x(j, o, b, nov)
            else:
                sidx = stage[(o // 2) % len(stage)]
                sa = A32(sidx, n)
                OP("act", "activation", [U("ps", b)], [U("a32", sidx)], out=sa, in_=PS(b, n), func=AF.Copy)
                OP("pool", "tensor_tensor", [U("a32", sidx), U("x", j, o)], [U("x", j, o)], out=xs(o, c0, n), in0=sa,
                   in1=xs(o, c0, n), op=ALU.add)

    def hist_setup(j, H, st_d, i):
        for g in range(4):
            ev, nseg, L = extv(g, j, H)
            if j == 0:
                src = st_d.rearrange("p (i s g r) -> p i s g r", i=2, s=4, g=4)[:, i, :, g, :]
                DMA("sp", U("hist", g), [], [U("ext", g)], ev[:, 0:4, 0:H], src)
                OP("dve", "memset", [], [U("ext", g)], ev[:, 4:5, 0:H], 0.0)
            elif j == 1:
                pv, _, _ = extv(g, 0, H)
                OP("dve", "tensor_scalar", [U("ext", g), "ptab"], [U("ext", g)], out=ev[:, 0:1, 0:H],
                   in0=pv[:, 5:6, 64:64 + H], scalar1=pt(C_MASK), scalar2=None, op0=ALU.mult)
            else:
                OP("dve", "tensor_copy", [U("ext", g)], [U("ext", g)], out=ev[:, 0:1, 0:H], in_=ev[:, 0:1, 512:512 + H])

    def hist_mid(j, g, H):
        if j == 0:
            ev, nseg, L = extv(g, 0, H)
            OP("dve", "tensor_copy", [U("ext", g)], [U("ext", g)], out=ev[:, 5:6, 0:H], in_=ev[:, 4:5, 64:64 + H])

    def state_out(j, g, H, o_ap, i):
        ov = o_ap.rearrange("p (i s g r) -> p i s g r", i=2, s=5, g=4)
        if j == 0:
            ev, _, _ = extv(g, 0, H)
            DMA("sp", U("so", g), [U("ext", g)], [], ov[:, i, 0:4, g, :], ev[:, 0:4, 64:64 + H])
        elif j == NT - 1:
            ev, _, _ = extv(g, j, H)
            DMA("sp", U("so", g), [U("ext", g)], [], ov[:, i, 4:5, g, :], ev[:, 0:1, 512:512 + H])

    def even_layer(l, skip_norm0=False):
        i = l // 2
        set_pool(range(8))
        DMA("pool", "wmt0", [], ["wmt0"], wmt[:, 0:1024], wTm_d[:, i * 1024:(i + 1) * 1024])
        DMA("pool", "wmt1", [], ["wmt1"], wmt[:, 1024:2048], wTs_d[:, i * 1024:(i + 1) * 1024])
        for v in range(2):
            wv = wmt[:, v * 1024:(v + 1) * 1024].rearrange("p (h i) -> p h i", h=8)
            mk = gmask[:, v * 128:(v + 1) * 128].unsqueeze(1).broadcast_to([P, 8, 128])
            OP("dve", "tensor_tensor", ["wmt%d" % v, "gmask"], ["wmt%d" % v], out=wv, in0=wv, in1=mk, op=ALU.mult)
        DMA("sp", "btab", [], ["btab"], btab[:], btab_d[:, i * 512:(i + 1) * 512])
        DMA("sp", "lntab", [], ["lntab"], lntab[:], lntab_d[:, i * 1024:(i + 1) * 1024])
        DMA("pool", "poolw", [], ["poolw"], poolw[:], poolw_d[:, i * 512:(i + 1) * 512])
        rsb = [newbank(), newbank()]
        for hb in range(2):
            OP("pe", "matmul", ["wmt0", "ones"], [U("ps", rsb[hb])], PS(rsb[hb], 512), lhsT=ones[:],
               rhs=wmt[:, hb * 512:(hb + 1) * 512], start=True, stop=True)
        tab2 = ext[:, 0:512]
        for c in range(4):
            for half in range(2):
                hh = 2 * c + half
                p0 = 64 * half
                OP("dve", "scalar_tensor_tensor", [U("ps", rsb[hh // 4]), "ptab", "btab"], [U("ext", 0)],
                   out=ext[p0:p0 + 64, c * 128:(c + 1) * 128],
                   in0=PS(rsb[hh // 4], 128, p0, p0 + 64, (hh % 4) * 128),
                   scalar=ptab[p0:p0 + 64, C_ALB + i * 4 + c:C_ALB + i * 4 + c + 1], in1=btab[p0:p0 + 64, c * 128:(c + 1) * 128],
                   op0=ALU.mult, op1=ALU.add)

        sV = load_slab(slab_in(w_in_even, i, 512), 8)
        sU = load_slab(slab_in(w_in_even, i, 0), 8)
        sOA = load_slab(slab_rows(w_out_even, i, 0), 4)
        gel_ctr = [0]
        mixb = [0, 1, 2, 3]
        set_pool([4, 5, 6, 7])
        VB = [0, 1, 6, 7]
        GEL = [4, 5, 6, 7]

        def a_vmm(j):
            c0, n = TILES[j]
            ngrp = n // 128
            for gi in range(ngrp):
                g0 = c0 + gi * 128
                b = newbank()
                for k in range(KC):
                    OP("pe", "matmul", [U("ring", sV), U("h", j, k)], [U("ps", b)], PS(b, 512), lhsT=hs(k, g0, 128),
                       rhs=RAfull(sV, k), start=(k == 0), stop=(k == KC - 1))
                gel = A32(GEL[gi])
                gu = U("a32", GEL[gi])
                OP("act", "activation", [U("ps", b)], [gu], out=gel, in_=PS(b, 512), func=AF.Gelu_apprx_tanh)
                st = small[:, 24 + gi * 6:24 + gi * 6 + 6]
                OP("dve", "bn_stats", [gu], [U("bst", gi)], out=st, in_=gel)
                OP("dve", "bn_aggr", [U("bst", gi)], ["mvall"], out=small[:, 2 * gi:2 * gi + 2], in_=st)

        def a_chain(j):
            c0, n = TILES[j]
            ngrp = n // 128
            mvall = small[:, 0:2 * ngrp].rearrange("p (g t) -> p g t", t=2)
            rsall = small[:, 16:16 + ngrp]
            OP("act", "activation", ["mvall", "eps"], ["rsall"], out=rsall, in_=mvall[:, :, 1], func=AF.Ln, bias=EPS_AP[0], scale=1.0)
            OP("act", "activation", ["rsall"], ["rsall"], out=rsall, in_=rsall, func=AF.Exp, scale=-0.5)
            for gi in range(ngrp):
                samp = (j == 0 and gi < 2)
                gel = A32(GEL[gi])
                gu = U("a32", GEL[gi])
                vb = A16(VB[gi])
                vu = U("a16", VB[gi])
                if not samp:
                    OP("dve", "tensor_scalar", [gu, "mvall", "rsall"], [vu], out=vb, in0=gel, scalar1=small[:, 2 * gi:2 * gi + 1],
                       scalar2=small[:, 16 + gi:17 + gi], op0=ALU.subtract, op1=ALU.mult)
                    continue
                OP("dve", "tensor_scalar", [gu, "mvall", "rsall"], [gu], out=gel, in0=gel, scalar1=small[:, 2 * gi:2 * gi + 1],
                   scalar2=small[:, 16 + gi:17 + gi], op0=ALU.subtract, op1=ALU.mult)
                OP(A_OFF, "tensor_tensor", [gu, "lntab"], [gu], out=gel, in0=gel, in1=lntab[:, 0:512], op=ALU.mult)
                OP(A_OFF, "tensor_tensor", [gu, "lntab"], [vu], out=vb, in0=gel, in1=lntab[:, 512:1024], op=ALU.add)
                if samp:
                    OP(A_OFF, "tensor_tensor", [gu, "lntab"], [U("a32", 7)], out=A32(7), in0=gel, in1=lntab[:, 512:1024],
                       op=ALU.add)
                    dstv = o_v.rearrange("p (g i f) -> p g i f", g=2, i=2)[:, gi, i, :]
                    DMA("sp", "ov", [U("a32", 7)], [], dstv, A32(7))

        def a_umm(j):
            c0, n = TILES[j]
            for c in range(4):
                b = newbank()
                for k in range(KC):
                    OP("pe", "matmul", [U("ring", sU), U("h", j, k)], [U("ps", b)], PS(b, n), lhsT=RA(sU, k, c),
                       rhs=hs(k, c0, n), start=(k == 0), stop=(k == KC - 1))
                OP("act", "activation", [U("ps", b)], [U("a32", c)], out=A32(c, n), in_=PS(b, n), func=AF.Gelu_apprx_tanh)

        def a_gating(j):
            c0, n = TILES[j]
            ngrp = n // 128
            for gi in range(ngrp):
                samp = (j == 0 and gi < 2)
                var = 1 if samp else 0
                vb = A16(VB[gi])
                for hh in range(8):
                    c = hh // 2
                    pp = (hh % 2) * 64
                    OP("pe", "matmul", [U("a16", VB[gi]), "wmt%d" % var], [U("ps", mixb[c])],
                       PS(mixb[c], 128, pp, pp + 64, gi * 128), lhsT=vb[:, hh * 64:(hh + 1) * 64],
                       rhs=wmt[:, var * 1024 + hh * 128:var * 1024 + (hh + 1) * 128], start=True, stop=True)

        def a_ya(j):
            c0, n = TILES[j]
            for c in range(4):
                tb = A32(4 + c, n)
                tu = U("a32", 4 + c)
                bt = btab[:, c * 128:(c + 1) * 128]
                t2 = ext[:, c * 128:(c + 1) * 128]
                if j == 0:
                    OP("dve", "tensor_tensor", [U("ps", mixb[c]), "btab"], [tu], out=v3(tb[:, 0:256], 4),
                       in0=v3(PS(mixb[c], 256), 4), in1=bt[:, 0:64].unsqueeze(1).broadcast_to([P, 4, 64]), op=ALU.add)
                    OP("dve", "scalar_tensor_tensor", [U("ps", mixb[c]), U("ext", 0), "ptab"], [tu], out=tb[:, 256:384],
                       in0=PS(mixb[c], 128, 0, P, 256), scalar=pt(C_ALG + i * 4 + c), in1=t2, op0=ALU.mult, op1=ALU.add)
                else:
                    OP("dve", "scalar_tensor_tensor", [U("ps", mixb[c]), U("ext", 0), "ptab"], [tu], out=v3(tb, 4),
                       in0=v3(PS(mixb[c], 512), 4), scalar=pt(C_ALG + i * 4 + c),
                       in1=t2.unsqueeze(1).broadcast_to([P, 4, 128]), op0=ALU.mult, op1=ALU.add)
                OP(A_OFF, "tensor_tensor", [tu, U("a32", c)], [U("a16", 2 + c)], out=A16(2 + c, n), in0=tb,
                   in1=A32(c, n), op=ALU.mult)

        bst_ = {}

        def zp_mm(j):
            c0, n = TILES[j]
            sPz_ = bst_["sPz"]
            for g in range(4):
                for k in range(KC):
                    OP("pe", "matmul", [U("ring", sPz_), U("h", j, k)], [U("ps", g)], PS(g, n), lhsT=RA(sPz_, k, g),
                       rhs=hs(k, c0, n), start=(k == 0), stop=(k == KC - 1))

        if not skip_norm0:
            norm_to_h(0, C_MIXG + l * 8)
        if l == 0:
            xload(2, [U("h", 0, KC - 1)])
        a_vmm(0)
        a_chain(0)
        a_umm(0)
        norm_to_h(1, C_MIXG + l * 8)
        if l == 0:
            xload(3, [U("h", 1, KC - 1)])
        for j, (c0, n) in enumerate(TILES):
            a_gating(j)
            a_ya(j)
            if j + 1 < NT:
                a_vmm(j + 1)
            else:
                dl = [U("h", 2, KC - 1)] if l == 0 else []
                bst_["sPz"] = load_slab(slab_in(w_in_even, i, 1024), 8, deps=dl)
                bst_["sOB"] = load_slab(slab_rows(w_out_even, i, 512), 4, deps=dl)
                zp_mm(0)
            wout_stage(j, sOA, [A16(2 + c, n) for c in range(4)], [U("a16", 2 + c) for c in range(4)])
            if j + 1 < NT:
                a_chain(j + 1)
                a_umm(j + 1)
                if j + 2 < NT:
                    norm_to_h(j + 2, C_MIXG + l * 8)
                    if l == 0 and j + 4 < NT:
                        xload(j + 4, [U("h", j + 2, KC - 1)])

        sPz = bst_["sPz"]
        sOB = bst_["sOB"]
        H = 15
        set_pool([4, 5, 6, 7])
        def b_evac(j):
            c0, n = TILES[j]
            hist_setup(j, H, stp_d, i)
            for g in range(4):
                ev, nseg, L = extv(g, j, H)
                eu = U("ext", g)
                OP("act", "activation", [U("ps", g)], [eu], out=ev[:, :, H:H + L], in_=v3(PS(g, n), nseg), func=AF.Copy)
                hist_mid(j, g, H)
                state_out(j, g, H, o_pool, i)

        def b_sums(j):
            c0, n = TILES[j]
            for g in range(4):
                ev, nseg, L = extv(g, j, H)
                eu = U("ext", g)
                wwin = 2 << g
                W = H + L
                bufs = [v3(A32(2 * (g % 2), nseg * W), nseg), v3(A32(2 * (g % 2) + 1, nseg * W), nseg)]
                bunits = [U("a32", 2 * (g % 2)), U("a32", 2 * (g % 2) + 1)]
                cur, cur_u = ev, eu
                sh = 1
                step = 0
                while sh < wwin:
                    lo = H - (wwin - 2 * sh)
                    dst, dst_u = bufs[step % 2], bunits[step % 2]
                    OP("dve", "tensor_tensor", [cur_u], [dst_u], out=dst[:, :, lo:W], in0=cur[:, :, lo:W],
                       in1=cur[:, :, lo - sh:W - sh], op=ALU.add)
                    cur, cur_u = dst, dst_u
                    sh *= 2
                    step += 1
                pooled = A16(g, n)
                OP("dve", "scalar_tensor_tensor", [cur_u, eu], [U("a16", g)], out=v3(pooled, nseg), in0=cur[:, :, H:H + L],
                   scalar=1.0 / wwin, in1=ev[:, :, H:H + L], op0=ALU.mult, op1=ALU.subtract)
                if j == 1:
                    tmp = small[:, 48:64]
                    OP("dve", "tensor_tensor", [cur_u, "ptab"], ["small32"], out=tmp, in0=cur[:, 0, H:H + 16],
                       in1=pt(C_ICNT + g * 16, 16), op=ALU.mult)
                    OP("dve", "tensor_tensor", ["small32", eu], [U("a16", g)], out=pooled[:, 0:16], in0=tmp,
                       in1=ev[:, 0, H:H + 16], op=ALU.subtract)

        def b_poolw(j):
            c0, n = TILES[j]
            for g in range(4):
                b2 = newbank()
                OP("pe", "matmul", ["poolw", U("a16", g)], [U("ps", b2)], PS(b2, n), lhsT=poolw[:, g * 128:(g + 1) * 128],
                   rhs=A16(g, n), start=True, stop=True)
                OP("act", "activation", [U("ps", b2), "ptab"], [U("a16", 4 + g)], out=A16(4 + g, n), in_=PS(b2, n),
                   func=AF.Copy, scale=pt(C_BSC + i * 4 + g))

        def b_wout(j):
            c0, n = TILES[j]
            wout_stage(j, sOB, [A16(4 + g, n) for g in range(4)], [U("a16", 4 + g) for g in range(4)])

        for j in range(NT):
            b_evac(j)
            if j + 1 < NT:
                zp_mm(j + 1)
            b_sums(j)
            if j >= 1:
                b_wout(j - 1)
            b_poolw(j)
        b_wout(NT - 1)

    dg_ctr = [0]

    def odd_layer(l, skip_norm0=False):
        i = l // 2
        sCv = load_slab(slab_in(w_in_odd, i, 0), 8)
        sCg = load_slab(slab_in(w_in_odd, i, 512), 8)
        sOA = load_slab(slab_rows(w_out_odd, i, 0), 4)
        H = 30
        sig_ctr = [0]
        set_pool([0, 1])
        bm, bq = 6, 7
        ext16 = ar32[:, 2 * A32W:2 * A32W + 2 * EXTW].bitcast(BF16)
        dgb = ar32[:, 5 * A32W:5 * A32W + 512].bitcast(BF16)
        dgb2 = lntab[:].bitcast(BF16)
        NDG = 16
        E16U = [U("a32", 2), U("a32", 3), U("a32", 4)]

        def e16v(g, j):
            nseg, L = (6, 64) if j == 0 else (1, 512)
            W = H + L
            return v3(ext16[:, g * EXTW:g * EXTW + nseg * W], nseg), nseg, L

        def vg(j, g, banks=None):
            c0, n = TILES[j]
            ev, nseg, L = extv(g, j, H)
            eu = U("ext", g)
            bv = banks[0] if banks else newbank()
            for k in range(KC):
                OP("pe", "matmul", [U("ring", sCv), U("h", j, k)], [U("ps", bv)], PS(bv, n), lhsT=RA(sCv, k, g),
                   rhs=hs(k, c0, n), start=(k == 0), stop=(k == KC - 1))
            bg = banks[1] if banks else newbank()
            for k in range(KC):
                OP("pe", "matmul", [U("ring", sCg), U("h", j, k)], [U("ps", bg)], PS(bg, n), lhsT=RA(sCg, k, g),
                   rhs=hs(k, c0, n), start=(k == 0), stop=(k == KC - 1))
            q = sig_ctr[0] % 2
            sig_ctr[0] += 1
            sg = A32(q, n)
            OP("act", "activation", [U("ps", bg)], [U("a32", q)], out=sg, in_=PS(bg, n), func=AF.Sigmoid)
            OP("dve", "tensor_tensor", [U("ps", bv), U("a32", q)], [eu], out=ev[:, :, H:H + L], in0=v3(PS(bv, n), nseg),
               in1=v3(sg, nseg), op=ALU.mult)
            hist_mid(j, g, H)
            state_out(j, g, H, o_c, i)
            e16, _, _ = e16v(g, j)
            OP("dve", "tensor_copy", [eu], (E16U if j == 0 else []) + [U("e16", g)], out=e16, in_=ev)

        def conv(j, g):
            c0, n = TILES[j]
            e16, nseg, L = e16v(g, j)
            wb = C_CCW + (i * 4 + g) * 31
            for k0 in range(0, 31, 4):
                nk = min(4, 31 - k0)
                gq = dg_ctr[0] % (NDG // 4)
                dg_ctr[0] += 1
                dgw = [U("dg", gq)] + (["lntab"] if first_dg[0] > 0 else [])
                first_dg[0] -= 4
                dg4 = dgb2[:, gq * 512:gq * 512 + nk * 128].rearrange("p (t m) -> p t m", t=nk)
                OP("pool", "tensor_tensor", ["ident", "ptab"], dgw, out=dg4,
                   in0=ident.unsqueeze(1).broadcast_to([P, nk, 128]),
                   in1=pt(wb + k0, nk).unsqueeze(2).broadcast_to([P, nk, 128]), op=ALU.mult)
                for t in range(nk):
                    k = k0 + t
                    OP("pe", "matmul", [U("dg", gq), U("e16", g)], [U("ps", 2 + g)], v3(PS(2 + g, n), nseg),
                       lhsT=dgb2[:, gq * 512 + t * 128:gq * 512 + (t + 1) * 128],
                       rhs=e16[:, :, k:k + L], start=(k == 0), stop=(k == 30))
            q = sig_ctr[0] % 2
            cb = A16(q, n)
            sq = A16(2 + q, n)
            OP("act", "activation", [U("ps", 2 + g), "ptab"], [U("a16", q)], out=cb, in_=PS(2 + g, n), func=AF.Identity,
               bias=pt(C_CCB + i * 4 + g), scale=1.0)
            OP("act", "activation", [U("ps", 2 + g), "ptab"], [U("a16", 2 + q)], out=sq, in_=PS(2 + g, n), func=AF.Square,
               bias=pt(C_CCB + i * 4 + g), scale=1.0)
            OP("pe", "matmul", [U("a16", q), "ones"], [U("ps", bm)], PS(bm, n), lhsT=ones[:], rhs=cb,
               start=(g == 0), stop=(g == 3))
            OP("pe", "matmul", [U("a16", 2 + q), "ones"], [U("ps", bq)], PS(bq, n), lhsT=ones[:], rhs=sq,
               start=(g == 0), stop=(g == 3))
            sig_ctr[0] += 1

        ABUF = [(btab[:, 0:512], "btab"), (wmt[:, 0:1024].bitcast(F32), "wmt0"), (wmt[:, 1024:2048].bitcast(F32), "wmt1"),
                (A32(5), U("a32", 5))]

        def ln_chain(j):
            c0, n = TILES[j]
            m2 = A32(6, n)
            mu = U("a32", 6)
            nm = A32(7, n)
            nu = U("a32", 7)
            OP("act", "activation", [U("ps", bm)], [nu], out=nm, in_=PS(bm, n), func=AF.Copy, scale=-1.0 / 512)
            OP("act", "activation", [U("ps", bm)], [mu], out=m2, in_=PS(bm, n), func=AF.Square, scale=1.0 / 512)
            for g in range(4):
                a, au = ABUF[g][0][:, 0:n], ABUF[g][1]
                OP("dve", "scalar_tensor_tensor", [U("ps", 2 + g), nu, "ptab"], [au], out=a, in0=PS(2 + g, n),
                   scalar=pt(C_CCB + i * 4 + g), in1=nm, op0=ALU.add, op1=ALU.add)
            OP("dve", "scalar_tensor_tensor", [U("ps", bq), mu], [mu], out=m2, in0=PS(bq, n), scalar=1.0 / 512, in1=m2,
               op0=ALU.mult, op1=ALU.subtract)
            OP("act", "activation", [mu, "eps"], [mu], out=m2, in_=m2, func=AF.Ln, bias=EPS_AP[0], scale=1.0)
            OP("act", "activation", [mu], [mu], out=m2, in_=m2, func=AF.Exp, scale=-0.5)

        def ln_post(j):
            c0, n = TILES[j]
            m2 = A32(6, n)
            mu = U("a32", 6)
            for g in range(4):
                a, au = ABUF[g][0][:, 0:n], ABUF[g][1]
                OP("dve", "tensor_tensor", [au, mu], [au], out=a, in0=a, in1=m2, op=ALU.mult)
                OP("act", "activation", [au, "ptab"], [U("a16", 4 + g)], out=A16(4 + g, n), in_=a, func=AF.Silu,
                   scale=pt(C_CLG + i * 4 + g), bias=pt(C_CLB + i * 4 + g))

        first_dg = [NDG]
        if not skip_norm0:
            norm_to_h(0, C_MIXG + l * 8)
        hist_setup(0, H, stc_d, i)
        vg(0, 0)
        vg(0, 1)
        conv(0, 0)
        for j, (c0, n) in enumerate(TILES):
            vg(j, 2)
            conv(j, 1)
            if j + 1 < NT:
                norm_to_h(j + 1, C_MIXG + l * 8)
            vg(j, 3)
            conv(j, 2)
            conv(j, 3)
            ln_chain(j)
            if j + 1 < NT:
                hist_setup(j + 1, H, stc_d, i)
                vg(j + 1, 0)
                vg(j + 1, 1, banks=(4, 5))
            ln_post(j)
            if j + 1 < NT:
                conv(j + 1, 0)
            wout_stage(j, sOA, [A16(4 + g, n) for g in range(4)], [U("a16", 4 + g) for g in range(4)])

        OP("dve", "memset", [U("e16", g) for g in range(4)] + [U("dg", q) for q in range(4)],
           E16U + ["lntab"], small[:, 20:21], 0.0)
        set_pool(range(8))
        sDc = load_slab(slab_in(w_in_odd, i, 1536), 8)
        sDx = load_slab(slab_in(w_in_odd, i, 2048), 8)
        sDb = load_slab(slab_in(w_in_odd, i, 1024), 8)
        sOB = load_slab(slab_rows(w_out_odd, i, 512), 4)
        H = 2
        ctr = [0]
        def d_chunk(j, g):
            c0, n = TILES[j]
            ev, nseg, L = extv(g, j, H)
            eu = U("ext", g)
            b1 = newbank()
            for k in range(KC):
                OP("pe", "matmul", [U("ring", sDc), U("h", j, k)], [U("ps", b1)], PS(b1, n), lhsT=RA(sDc, k, g),
                   rhs=hs(k, c0, n), start=(k == 0), stop=(k == KC - 1))
            b2 = newbank()
            for k in range(KC):
                OP("pe", "matmul", [U("ring", sDx), U("h", j, k)], [U("ps", b2)], PS(b2, n), lhsT=RA(sDx, k, g),
                   rhs=hs(k, c0, n), start=(k == 0), stop=(k == KC - 1))
            b3 = newbank()
            for k in range(KC):
                OP("pe", "matmul", [U("ring", sDb), U("h", j, k)], [U("ps", b3)], PS(b3, n), lhsT=RA(sDb, k, g),
                   rhs=hs(k, c0, n), start=(k == 0), stop=(k == KC - 1))
            q = ctr[0] % 2
            ctr[0] += 1
            gcs = A32(q, n)
            OP("act", "activation", [U("ps", b1)], [U("a32", q)], out=gcs, in_=PS(b1, n), func=AF.Copy)
            OP("dve", "tensor_tensor", [U("ps", b2), U("a32", q)], [eu], out=ev[:, :, H:H + L], in0=v3(PS(b2, n), nseg),
               in1=v3(gcs, nseg), op=ALU.mult)
            hist_mid(j, g, H)
            state_out(j, g, H, o_d, i)
            acc = v3(A32(2 + q, n), nseg)
            au = U("a32", 2 + q)
            wb = C_DCW + (i * 4 + g) * 3
            OP("dve", "tensor_scalar", [eu, "ptab"], [au], out=acc, in0=ev[:, :, 0:L], scalar1=pt(wb), scalar2=None,
               op0=ALU.mult)
            for k in range(1, 3):
                OP("dve", "scalar_tensor_tensor", [eu, "ptab", au], [au], out=acc, in0=ev[:, :, k:k + L],
                   scalar=pt(wb + k), in1=acc, op0=ALU.mult, op1=ALU.add)
            di = (j % 2) * 4 + g
            OP("dve", "tensor_tensor", [U("ps", b3), au], [U("a16", di)], out=A16(di, n), in0=PS(b3, n), in1=A32(2 + q, n),
               op=ALU.mult)

        def d_wout(j):
            c0, n = TILES[j]
            par = (j % 2) * 4
            wout_stage(j, sOB, [A16(par + g, n) for g in range(4)], [U("a16", par + g) for g in range(4)])

        for j in range(NT):
            hist_setup(j, H, std_d, i)
            for g in range(4):
                d_chunk(j, g)
                if g == 0 and j >= 1:
                    d_wout(j - 1)
                if g == 2 and j >= 1:
                    norm_to_h(j - 1, C_FFNG + l * 8)
        d_wout(NT - 1)
        norm_to_h(NT - 1, C_FFNG + l * 8)

    def ffn(l, after_last=None, after_tile0=None):
        set_pool(range(8))
        rctr = [0]
        items = [(p, j) for p in range(8) for j in range(NT)]
        slots = {}

        halo_dead = (l == nlayers - 1)

        def stage1(t):
            p, j = items[t]
            c0, n = TILES[j]
            if halo_dead and j == 0:
                n = TS
            if j == 0:
                s1 = load_slab(slab_in(w_ff1, l, 512 * p), 8)
                s2 = load_slab(slab_rows(w_ff2, l, 512 * p), 4)
                slots[p] = (s1, s2)
            s1, s2 = slots[p]
            if l % 2 == 0 and p == 0 and j == 0:
                norm_to_h(0, C_FFNG + l * 8)
            aset = t % 2
            for c in range(4):
                b = newbank()
                for k in range(KC):
                    OP("pe", "matmul", [U("ring", s1), U("h", j, k)], [U("ps", b)], PS(b, n), lhsT=RA(s1, k, c),
                       rhs=hs(k, c0, n), start=(k == 0), stop=(k == KC - 1))
                q = rctr[0] % 4
                rctr[0] += 1
                r_ = A32(q, n)
                OP("act", "activation", [U("ps", b)], [U("a32", q)], out=r_, in_=PS(b, n), func=AF.Relu)
                OP("pool" if (l % 2 == 0 and p == 0 and c % 2 == 0) else "dve", "tensor_tensor", [U("a32", q)],
                   [U("a16", aset * 4 + c)], out=A16(aset * 4 + c, n), in0=r_, in1=r_, op=ALU.mult)
            if l % 2 == 0 and p == 0 and j + 1 < NT:
                norm_to_h(j + 1, C_FFNG + l * 8)

        def stage2(t):
            p, j = items[t]
            c0, n = TILES[j]
            nov = None
            if halo_dead and j == 0:
                n = TS
                nov = TS
            s1, s2 = slots[p]
            aset = t % 2
            wout_stage(j, s2, [A16(aset * 4 + c, n) for c in range(4)], [U("a16", aset * 4 + c) for c in range(4)], nov=nov)

        for t in range(len(items) + 1):
            if t < len(items):
                stage1(t)
            if t >= 1:
                stage2(t - 1)
                pp_, jj_ = items[t - 1]
                if pp_ == 7 and after_last is not None:
                    after_last(jj_)
                if pp_ == 7 and jj_ == 0 and after_tile0 is not None:
                    after_tile0()

    epsb = sb("epsb", [P, 1], F32)
    OP("dve", "memset", [], ["eps"], epsb[:], EPS)
    EPS_AP[0] = epsb[:]
    yctr = [0]

    def final_norm(j):
        c0, n = TILES[j]

        def f(k, rstd):
            q = 4 + yctr[0] % 4
            yctr[0] += 1
            yb = A32(q, n)
            OP("dve", "scalar_tensor_tensor", [U("x", j, k), "nrs", "ptab"], [U("a32", q)], out=yb, in0=xs(k, c0, n),
               scalar=pt(C_FING + k), in1=rstd, op0=ALU.mult, op1=ALU.mult)
            DMA("sp", U("yo", q), [U("a32", q)], [], yT[:, k * T + c0:k * T + c0 + n], yb)
        rmsnorm_tile(j, f)

    for l in range(nlayers):
        if l % 2 == 0:
            even_layer(l, skip_norm0=(l > 0))
        else:
            odd_layer(l, skip_norm0=(l > 0))
        last = (l == nlayers - 1)
        ffn(l, after_last=(final_norm if last else None),
            after_tile0=(None if last else (lambda l=l: norm_to_h(0, C_MIXG + (l + 1) * 8))))

    S.emit(nc, stack)
    stack.close()
    return nc


_NC_CACHE = {}


def _prep_inputs(inp):
    f = np.float32
    xp = np.asarray(inp["x_prompt"], f)
    xsm = np.asarray(inp["x_sample"], f)
    ptab_common = np.zeros((P, NPT), f)

    def pp(v):
        v = np.asarray(v, f)
        ch = v.shape[-1] // P
        v = v.reshape(v.shape[:-1] + (ch, P))
        return np.moveaxis(v, -1, 0)

    ptab_common[:, C_MIXG:C_MIXG + 32] = pp(inp["norm_mix_g"]).reshape(P, 32)
    ptab_common[:, C_FFNG:C_FFNG + 32] = pp(inp["norm_ffn_g"]).reshape(P, 32)
    ptab_common[:, C_FING:C_FING + 8] = pp(inp["final_norm_g"]).reshape(P, 8)
    ptab_common[:, C_BSC:C_BSC + 8] = pp(inp["b_scale"]).reshape(P, 8)
    ptab_common[:, C_CCB:C_CCB + 8] = pp(inp["c_conv_b"]).reshape(P, 8)
    ptab_common[:, C_CLG:C_CLG + 8] = pp(inp["c_ln_g"]).reshape(P, 8)
    ptab_common[:, C_CLB:C_CLB + 8] = pp(inp["c_ln_b"]).reshape(P, 8)
    ccw = pp(inp["c_conv_w"])
    ptab_common[:, C_CCW:C_CCW + 248] = np.transpose(ccw, (0, 1, 3, 2)).reshape(P, 248)
    dcw = pp(inp["d_conv_w"])
    ptab_common[:, C_DCW:C_DCW + 24] = np.transpose(dcw, (0, 1, 3, 2)).reshape(P, 24)
    ptab_common[:, C_ALG:C_ALG + 8] = pp(inp["a_ln_g"]).reshape(P, 8)
    ptab_common[:, C_ALB:C_ALB + 8] = pp(inp["a_ln_b"]).reshape(P, 8)

    aws = np.asarray(inp["a_w_s"], f)
    wTm = np.transpose(aws, (3, 0, 1, 2)).reshape(P, 2 * 8 * 128)
    idx = np.arange(128) % 64
    aws_s = aws[:, :, idx][:, :, :, idx]
    wTs = np.transpose(aws_s, (3, 0, 1, 2)).reshape(P, 2 * 8 * 128)
    jj = np.arange(128)[:, None] // 64
    ii = np.arange(128)[None, :] // 64
    gmask = np.concatenate([(jj <= ii).astype(f), (jj == ii).astype(f), np.eye(128, dtype=f)], axis=1)
    abs_ = np.asarray(inp["a_b_s"], f)
    hp = (np.arange(P) // 64)
    btab = np.zeros((P, 2, 4, 128), f)
    for c in range(4):
        btab[:, :, c, :] = np.transpose(abs_[:, 2 * c + hp, :], (1, 0, 2))
    btab = btab.reshape(P, 2 * 4 * 128)
    lng = np.asarray(inp["a_ln_g"], f)
    lnb = np.asarray(inp["a_ln_b"], f)
    lntab = np.stack([lng, lnb], axis=1)[None].repeat(P, axis=0).reshape(P, 2 * 2 * 512)
    poolw = np.transpose(np.asarray(inp["b_pool_w"], f), (2, 0, 1, 3)).reshape(P, 2 * 4 * 128)

    def st(v, H):
        v = np.asarray(v, f).reshape(2, 8, 4, H, 4, P)
        v = np.transpose(v, (1, 5, 0, 2, 4, 3))
        return v.reshape(8, P, -1)

    stp = st(inp["state_pool"], 15)
    stc = st(inp["state_conv_c"], 30)
    std = st(inp["state_conv_d"], 2)

    shared = dict(wTm=np.ascontiguousarray(wTm), wTs=np.ascontiguousarray(wTs), gmask=np.ascontiguousarray(gmask),
                  btab=btab, lntab=np.ascontiguousarray(lntab), poolw=np.ascontiguousarray(poolw))
    for nme in ("w_in_even", "w_out_even", "w_in_odd", "w_out_odd", "w_ff1", "w_ff2"):
        shared[nme] = np.ascontiguousarray(np.asarray(inp[nme], f))
    in_maps = []
    for c in range(NCORE):
        b, q = c // 4, c % 4
        tok = np.zeros((T, D), f)
        tok[0:TS] = xsm[4 * c:4 * c + 4].reshape(TS, D)
        if q > 0:
            tok[TS:TS + TH] = xp[b, q * TM - TH:q * TM]
        tok[TS + TH:] = xp[b, q * TM:(q + 1) * TM]
        xin = np.ascontiguousarray(tok.reshape(T, KC, P).transpose(2, 1, 0)).reshape(P, KC * T)
        pt_ = ptab_common.copy()
        pt_[:, C_MASK] = 0.0 if q == 0 else 1.0
        for g, w in enumerate((2, 4, 8, 16)):
            if q == 0:
                cntv = np.minimum(w, np.arange(16) + 1).astype(f)
            else:
                cntv = np.full(16, w, f)
            pt_[:, C_ICNT + g * 16:C_ICNT + (g + 1) * 16] = (1.0 / cntv)[None, :]
        m = dict(shared)
        m.update(xin=xin, ptab=pt_, stp=np.ascontiguousarray(stp[c]), stc=np.ascontiguousarray(stc[c]),
                 std=np.ascontiguousarray(std[c]))
        in_maps.append(m)
    return in_maps


def kernel(**inputs):
    return _run(inputs, DEPTH)


def _run(inputs, nlayers, trace=False):
    if nlayers not in _NC_CACHE:
        _NC_CACHE[nlayers] = build_program(nlayers)
    nc = _NC_CACHE[nlayers]
    in_maps = _prep_inputs(inputs)
    if trace:
        res = run_bass_kernel_spmd(nc, in_maps, core_ids=list(range(NCORE)), trace=True)
        _NC_CACHE["last_res"] = res
    else:
        res = run_bass_kernel_spmd(nc, in_maps, core_ids=list(range(NCORE)))
    R = res.results
    f = np.float32
    y_prompt = np.zeros((2, 8192, D), f)
    y_sample = np.zeros((32, 64, D), f)
    new_pool_p = np.zeros((2, 2, 15, 512), f)
    new_c_p = np.zeros((2, 2, 30, 512), f)
    new_d_p = np.zeros((2, 2, 2, 512), f)
    new_v_s = np.zeros((2, 32, 64, 512), f)
    new_pool_s = np.zeros((2, 32, 15, 512), f)
    new_c_s = np.zeros((2, 32, 30, 512), f)
    new_d_s = np.zeros((2, 32, 2, 512), f)
    for c in range(NCORE):
        b, q = c // 4, c % 4
        y = np.asarray(R[c]["yT"]).reshape(P, KC, T).transpose(2, 1, 0).reshape(T, D)
        y_sample[4 * c:4 * c + 4] = y[0:TS].reshape(4, 64, D)
        y_prompt[b, q * TM:(q + 1) * TM] = y[TS + TH:]
        ov = np.asarray(R[c]["o_v"]).reshape(P, 2, 2, 512)
        vv = ov.transpose(2, 1, 0, 3).reshape(2, 256, 512).reshape(2, 4, 64, 512)
        new_v_s[:, 4 * c:4 * c + 4] = vv
        for arr_s, arr_p, nme, H in ((new_pool_s, new_pool_p, "o_pool", 15), (new_c_s, new_c_p, "o_c", 30),
                                     (new_d_s, new_d_p, "o_d", 2)):
            o = np.asarray(R[c][nme]).reshape(P, 2, 5, 4, H)
            o = o.transpose(1, 2, 4, 3, 0).reshape(2, 5, H, 512)
            arr_s[:, 4 * c:4 * c + 4] = o[:, 0:4]
            if q == 3:
                arr_p[:, b] = o[:, 4]
    return (y_prompt, y_sample, new_pool_p, new_c_p, new_d_p, new_v_s, new_pool_s, new_c_s, new_d_s)
```

```python
import numpy as np
from contextlib import ExitStack
import concourse.bass as bass
import concourse.mybir as mybir
from concourse.bass_utils import run_bass_kernel_spmd

F32 = mybir.dt.float32
BF16 = mybir.dt.bfloat16
AF = mybir.ActivationFunctionType
ALU = mybir.AluOpType

P = 128
D = 1024
KC = 8
NCORE = 8
TS, TH, TM = 256, 128, 2048
T = TS + TH + TM
TILES = [(0, 384)] + [(384 + 512 * i, 512) for i in range(4)]
NT = len(TILES)
DEPTH = 4
EPS = 1e-6
NSLOT = 5
DG_ENG = ("dve", "pool", "dve", "act")
A_OFF = "dve"
SLOTW = 4096
EXTW = 576

C_MIXG, C_FFNG, C_FING, C_BSC, C_CCB, C_CLG, C_CLB = 0, 32, 64, 72, 80, 88, 96
C_CCW, C_DCW, C_MASK, C_ICNT = 104, 352, 376, 377
C_ALG, C_ALB = 448, 456
NPT = 464


class Sched:
    ENG = ("pe", "act", "dve", "pool", "sp")

    def __init__(self):
        self.ops = []
        self.lastw = {}
        self.readers = {}
        self.lastdma = {}

    def _add(self, eng, fn, r, w, kind, key=None):
        i = len(self.ops)
        deps = set()
        for u in r:
            j = self.lastw.get(u)
            if j is not None:
                deps.add(j)
        for u in w:
            j = self.lastw.get(u)
            if j is not None:
                deps.add(j)
            deps.update(self.readers.get(u, ()))
        if kind == "d":
            j = self.lastdma.get(key)
            if j is not None:
                deps.add(j)
            self.lastdma[key] = i
        for u in r:
            self.readers.setdefault(u, []).append(i)
        for u in w:
            self.lastw[u] = i
            self.readers[u] = []
        self.ops.append(dict(eng=eng, fn=fn, deps=deps, kind=kind, key=key, ev=None))
        return i

    def op(self, eng, fn, r=(), w=()):
        return self._add(eng, fn, r, w, "c")

    def dma(self, eng, fn, key, r=(), w=()):
        return self._add(eng, fn, r, w, "d", key)

    def emit(self, nc, stack):
        ops = self.ops
        n = len(ops)
        need = [False] * n
        for op in ops:
            for d in op["deps"]:
                od = ops[d]
                if od["kind"] == "c":
                    if od["eng"] == "pe" and op["eng"] == "pe" and op["kind"] == "c":
                        continue
                    need[d] = True
        cnt = {e: 0 for e in self.ENG}
        dcnt = {}
        for i, op in enumerate(ops):
            if op["kind"] == "c":
                if need[i]:
                    cnt[op["eng"]] += 1
                    op["ev"] = (("eng", op["eng"]), cnt[op["eng"]])
            else:
                k = op["key"]
                dcnt[k] = dcnt.get(k, 0) + 16
                op["ev"] = (("dma", k), dcnt[k])
        sems = {}
        for e in self.ENG:
            sems[("eng", e)] = stack.enter_context(nc.semaphore("s_" + e))
        for idx, k in enumerate(dcnt):
            sems[("dma", k)] = stack.enter_context(nc.semaphore("d_%d" % idx))
        block = stack.enter_context(nc.Block())
        by_eng = {e: [] for e in self.ENG}
        for i, op in enumerate(ops):
            by_eng[op["eng"]].append(i)

        def run(eng_name, eh):
            waited = {}
            for i in by_eng[eng_name]:
                op = ops[i]
                w = {}
                for d in op["deps"]:
                    od = ops[d]
                    if od["kind"] == "c" and od["eng"] == "pe" and eng_name == "pe" and op["kind"] == "c":
                        continue
                    s, v = od["ev"]
                    if v > w.get(s, 0):
                        w[s] = v
                for s, v in w.items():
                    if waited.get(s, 0) < v:
                        eh.wait_ge(sems[s], v)
                        waited[s] = v
                nm, a_, kw_ = op["fn"]
                ins = getattr(eh, nm)(*a_, **kw_)
                if op["kind"] == "c":
                    if need[i]:
                        ins.then_inc(sems[("eng", eng_name)], 1)
                else:
                    ins.then_inc(sems[("dma", op["key"])], 16)
            if eng_name == "sp":
                for k, v in dcnt.items():
                    eh.wait_ge(sems[("dma", k)], v)

        @block.tensor
        def _(e):
            run("pe", e)

        @block.scalar
        def _(e):
            run("act", e)

        @block.vector
        def _(e):
            run("dve", e)

        @block.gpsimd
        def _(e):
            run("pool", e)

        @block.sync
        def _(e):
            run("sp", e)


def build_program(nlayers=DEPTH):
    nc = bass.Bass("TRN2", target_bir_lowering=False)
    stack = ExitStack()
    S = Sched()

    def OP(eng, name, r, w, *a, **kw):
        S._add(eng, (name, a, kw), r, w, "c")

    def DMA(eng, key, r, w, out, in_):
        S._add(eng, ("dma_start", (), dict(out=out, in_=in_)), r, w, "d", key)

    def din(name, shape, dt=F32):
        return nc.dram_tensor(name, shape, dt, kind="ExternalInput").ap()

    def dout(name, shape, dt=F32):
        return nc.dram_tensor(name, shape, dt, kind="ExternalOutput").ap()

    xin = din("xin", [P, KC * T])
    ptab_d = din("ptab", [P, NPT])
    stp_d = din("stp", [P, 2 * 4 * 4 * 15])
    stc_d = din("stc", [P, 2 * 4 * 4 * 30])
    std_d = din("std", [P, 2 * 4 * 4 * 2])
    wTm_d = din("wTm", [P, 2 * 8 * 128])
    wTs_d = din("wTs", [P, 2 * 8 * 128])
    gmask_d = din("gmask", [P, 3 * 128])
    btab_d = din("btab", [P, 2 * 4 * 128])
    lntab_d = din("lntab", [P, 2 * 2 * 512])
    poolw_d = din("poolw", [P, 2 * 4 * 128])
    w_in_even = din("w_in_even", [2, 1024, 1536])
    w_out_even = din("w_out_even", [2, 1024, 1024])
    w_in_odd = din("w_in_odd", [2, 1024, 2560])
    w_out_odd = din("w_out_odd", [2, 1024, 1024])
    w_ff1 = din("w_ff1", [4, 1024, 4096])
    w_ff2 = din("w_ff2", [4, 4096, 1024])
    yT = dout("yT", [P, KC * T])
    o_pool = dout("o_pool", [P, 2 * 5 * 4 * 15])
    o_c = dout("o_c", [P, 2 * 5 * 4 * 30])
    o_d = dout("o_d", [P, 2 * 5 * 4 * 2])
    o_v = dout("o_v", [P, 2 * 2 * 512])

    def sb(name, shape, dt):
        return stack.enter_context(nc.sbuf_tensor(name, shape, dt))

    x_sb = sb("x_sb", [P, KC * T], F32)
    h_sb = sb("h_sb", [P, KC * T], BF16)
    ring = sb("ring", [P, NSLOT * SLOTW], BF16)
    ext = sb("ext", [P, 4 * EXTW], F32)
    wmt = sb("wmt", [P, 2 * 8 * 128], BF16)
    gmask = sb("gmask_sb", [P, 3 * 128], BF16)
    ident = gmask[:, 256:384]
    btab = sb("btab_sb", [P, 4 * 128], F32)
    lntab = sb("lntab_sb", [P, 2 * 512], F32)
    poolw = sb("poolw_sb", [P, 4 * 128], BF16)
    ptab = sb("ptab_sb", [P, NPT], F32)
    ones = sb("ones", [P, P], BF16)
    small = sb("small", [P, 64], F32)
    nsq = sb("nsq", [P, 2 * 512], BF16)
    nrs = sb("nrs", [P, 512], F32)
    A32W = 544
    ar32 = sb("ar32", [P, 8 * A32W], F32)
    ar16 = sb("ar16", [P, 8 * 512], BF16)
    ps = stack.enter_context(nc.psum_tensor("ps", [P, 8 * 512], F32))

    def xs(k, c0, n):
        return x_sb[:, k * T + c0:k * T + c0 + n]

    def hs(k, c0, n):
        return h_sb[:, k * T + c0:k * T + c0 + n]

    def A32(i, n=512):
        return ar32[:, i * A32W:i * A32W + n]

    def A16(i, n=512):
        return ar16[:, i * 512:i * 512 + n]

    def pt(c, n=1):
        return ptab[:, c:c + n]

    bank_pool = [list(range(8))]
    bank_ctr = [0]

    def set_pool(lst):
        bank_pool[0] = list(lst)
        bank_ctr[0] = 0

    def newbank():
        pl = bank_pool[0]
        b = pl[bank_ctr[0] % len(pl)]
        bank_ctr[0] += 1
        return b

    def PS(b, n=512, p0=0, p1=P, c0=0):
        return ps[p0:p1, b * 512 + c0:b * 512 + c0 + n]

    def v3(ap, nseg):
        if nseg == 1:
            return ap.unsqueeze(1)
        return ap.rearrange("p (s l) -> p s l", s=nseg)

    def extv(g, j, H):
        nseg, L = (6, 64) if j == 0 else (1, 512)
        W = H + L
        a = ext[:, g * EXTW:g * EXTW + nseg * W]
        return v3(a, nseg), nseg, L

    U = lambda *a: tuple(a)

    slab_ctr = [0]

    def load_slab(src_ap, kk, deps=()):
        i = slab_ctr[0]
        slab_ctr[0] += 1
        s = i % NSLOT
        dst = ring[:, s * SLOTW:(s + 1) * SLOTW].rearrange("p (k n) -> p k n", k=kk)
        DMA("pool", U("ring", s), list(deps), [U("ring", s)], dst, src_ap)
        return s

    def slab_in(wd, l, n0):
        return wd[l].rearrange("(k p) n -> p k n", p=P)[:, :, n0:n0 + 512]

    def slab_rows(wd, l, r0):
        return wd[l, r0:r0 + 512, :].rearrange("(k p) n -> p k n", p=P)

    def RA(s, k, m):
        o = s * SLOTW + k * 512 + m * 128
        return ring[:, o:o + 128]

    def RAfull(s, k):
        o = s * SLOTW + k * 512
        return ring[:, o:o + 512]

    def RB(s, k, m):
        o = s * SLOTW + k * 1024 + m * 128
        return ring[:, o:o + 128]

    DMA("sp", "ptab", [], ["ptab"], ptab[:], ptab_d)
    def xload(j, deps=()):
        c0, n = TILES[j]
        src = xin.rearrange("p (k t) -> p k t", k=KC)[:, :, c0:c0 + n]
        dst = x_sb[:].rearrange("p (k t) -> p k t", k=KC)[:, :, c0:c0 + n]
        DMA("sp", U("xin", j), list(deps), [U("x", j, k) for k in range(KC)], dst, src)

    xload(0)
    xload(1)
    OP("pool", "memset", [], ["ones"], ones[:], 1.0)
    DMA("pool", "gmask", [], ["gmask", "ident"], gmask[:], gmask_d)

    nsq_ctr = [0]

    def rmsnorm_tile(j, out_fn):
        c0, n = TILES[j]
        b = newbank()
        for k in range(KC):
            q = nsq_ctr[0] % 2
            nsq_ctr[0] += 1
            sq = nsq[:, q * 512:q * 512 + n]
            OP("act", "activation", [U("x", j, k)], [U("nsq", q)], out=sq, in_=xs(k, c0, n), func=AF.Square)
            OP("pe", "matmul", [U("nsq", q), "ones"], [U("ps", b)], PS(b, n), lhsT=ones[:], rhs=sq,
               start=(k == 0), stop=(k == KC - 1))
        OP("act", "activation", [U("ps", b), "eps"], ["nrs"], out=nrs[:, 0:n], in_=PS(b, n), func=AF.Ln, bias=EPS_AP[0],
           scale=1.0 / D)
        OP("act", "activation", ["nrs"], ["nrs"], out=nrs[:, 0:n], in_=nrs[:, 0:n], func=AF.Exp, scale=-0.5)
        for k in range(KC):
            out_fn(k, nrs[:, 0:n])

    EPS_AP = [None]

    def norm_to_h(j, gbase):
        c0, n = TILES[j]

        def f(k, rstd):
            OP("dve", "scalar_tensor_tensor", [U("x", j, k), "nrs", "ptab"], [U("h", j, k)],
               out=hs(k, c0, n), in0=xs(k, c0, n), scalar=pt(gbase + k), in1=rstd, op0=ALU.mult, op1=ALU.mult)
        rmsnorm_tile(j, f)

    def add_to_x(j, o, b, nov=None):
        c0, n = TILES[j]
        n = nov or n
        OP("dve", "tensor_tensor", [U("ps", b), U("x", j, o)], [U("x", j, o)],
           out=xs(o, c0, n), in0=PS(b, n), in1=xs(o, c0, n), op=ALU.add)

    def wout_stage(j, slot, rhs_list, rhs_units, stage=None, nov=None):
        c0, n = TILES[j]
        n = nov or n
        for o in range(KC):
            b = newbank()
            for k in range(4):
                OP("pe", "matmul", [U("ring", slot), rhs_units[k]], [U("ps", b)], PS(b, n), lhsT=RB(slot, k, o),
                   rhs=rhs_list[k], start=(k == 0), stop=(k == 3))
            if stage is None or o % 2 == 1:
                add_to_x(j, o, b, nov)
            else:
                sidx = stage[(o // 2) % len(stage)]
                sa = A32(sidx, n)
                OP("act", "activation", [U("ps", b)], [U("a32", sidx)], out=sa, in_=PS(b, n), func=AF.Copy)
                OP("pool", "tensor_tensor", [U("a32", sidx), U("x", j, o)], [U("x", j, o)], out=xs(o, c0, n), in0=sa,
                   in1=xs(o, c0, n), op=ALU.add)

    def hist_setup(j, H, st_d, i):
        for g in range(4):
            ev, nseg, L = extv(g, j, H)
            if j == 0:
                src = st_d.rearrange("p (i s g r) -> p i s g r", i=2, s=4, g=4)[:, i, :, g, :]
                DMA("sp", U("hist", g), [], [U("ext", g)], ev[:, 0:4, 0:H], src)
                OP("dve", "memset", [], [U("ext", g)], ev[:, 4:5, 0:H], 0.0)
            elif j == 1:
                pv, _, _ = extv(g, 0, H)
                OP("dve", "tensor_scalar", [U("ext", g), "ptab"], [U("ext", g)], out=ev[:, 0:1, 0:H],
                   in0=pv[:, 5:6, 64:64 + H], scalar1=pt(C_MASK), scalar2=None, op0=ALU.mult)
            else:
                OP("dve", "tensor_copy", [U("ext", g)], [U("ext", g)], out=ev[:, 0:1, 0:H], in_=ev[:, 0:1, 512:512 + H])

    def hist_mid(j, g, H):
        if j == 0:
            ev, nseg, L = extv(g, 0, H)
            OP("dve", "tensor_copy", [U("ext", g)], [U("ext", g)], out=ev[:, 5:6, 0:H], in_=ev[:, 4:5, 64:64 + H])

    def state_out(j, g, H, o_ap, i):
        ov = o_ap.rearrange("p (i s g r) -> p i s g r", i=2, s=5, g=4)
        if j == 0:
            ev, _, _ = extv(g, 0, H)
            DMA("sp", U("so", g), [U("ext", g)], [], ov[:, i, 0:4, g, :], ev[:, 0:4, 64:64 + H])
        elif j == NT - 1:
            ev, _, _ = extv(g, j, H)
            DMA("sp", U("so", g), [U("ext", g)], [], ov[:, i, 4:5, g, :], ev[:, 0:1, 512:512 + H])

    def even_layer(l, skip_norm0=False):
        i = l // 2
        set_pool(range(8))
        DMA("pool", "wmt0", [], ["wmt0"], wmt[:, 0:1024], wTm_d[:, i * 1024:(i + 1) * 1024])
        DMA("pool", "wmt1", [], ["wmt1"], wmt[:, 1024:2048], wTs_d[:, i * 1024:(i + 1) * 1024])
        for v in range(2):
            wv = wmt[:, v * 1024:(v + 1) * 1024].rearrange("p (h i) -> p h i", h=8)
            mk = gmask[:, v * 128:(v + 1) * 128].unsqueeze(1).broadcast_to([P, 8, 128])
            OP("dve", "tensor_tensor", ["wmt%d" % v, "gmask"], ["wmt%d" % v], out=wv, in0=wv, in1=mk, op=ALU.mult)
        DMA("sp", "btab", [], ["btab"], btab[:], btab_d[:, i * 512:(i + 1) * 512])
        DMA("sp", "lntab", [], ["lntab"], lntab[:], lntab_d[:, i * 1024:(i + 1) * 1024])
        DMA("pool", "poolw", [], ["poolw"], poolw[:], poolw_d[:, i * 512:(i + 1) * 512])
        rsb = [newbank(), newbank()]
        for hb in range(2):
            OP("pe", "matmul", ["wmt0", "ones"], [U("ps", rsb[hb])], PS(rsb[hb], 512), lhsT=ones[:],
               rhs=wmt[:, hb * 512:(hb + 1) * 512], start=True, stop=True)
        tab2 = ext[:, 0:512]
        for c in range(4):
            for half in range(2):
                hh = 2 * c + half
                p0 = 64 * half
                OP("dve", "scalar_tensor_tensor", [U("ps", rsb[hh // 4]), "ptab", "btab"], [U("ext", 0)],
                   out=ext[p0:p0 + 64, c * 128:(c + 1) * 128],
                   in0=PS(rsb[hh // 4], 128, p0, p0 + 64, (hh % 4) * 128),
                   scalar=ptab[p0:p0 + 64, C_ALB + i * 4 + c:C_ALB + i * 4 + c + 1], in1=btab[p0:p0 + 64, c * 128:(c + 1) * 128],
                   op0=ALU.mult, op1=ALU.add)

        sV = load_slab(slab_in(w_in_even, i, 512), 8)
        sU = load_slab(slab_in(w_in_even, i, 0), 8)
        sOA = load_slab(slab_rows(w_out_even, i, 0), 4)
        gel_ctr = [0]
        mixb = [0, 1, 2, 3]
        set_pool([4, 5, 6, 7])
        VB = [0, 1, 6, 7]
        GEL = [4, 5, 6, 7]

        def a_vmm(j):
            c0, n = TILES[j]
            ngrp = n // 128
            for gi in range(ngrp):
                g0 = c0 + gi * 128
                b = newbank()
                for k in range(KC):
                    OP("pe", "matmul", [U("ring", sV), U("h", j, k)], [U("ps", b)], PS(b, 512), lhsT=hs(k, g0, 128),
                       rhs=RAfull(sV, k), start=(k == 0), stop=(k == KC - 1))
                gel = A32(GEL[gi])
                gu = U("a32", GEL[gi])
                OP("act", "activation", [U("ps", b)], [gu], out=gel, in_=PS(b, 512), func=AF.Gelu_apprx_tanh)
                st = small[:, 24 + gi * 6:24 + gi * 6 + 6]
                OP("dve", "bn_stats", [gu], [U("bst", gi)], out=st, in_=gel)
                OP("dve", "bn_aggr", [U("bst", gi)], ["mvall"], out=small[:, 2 * gi:2 * gi + 2], in_=st)

        def a_chain(j):
            c0, n = TILES[j]
            ngrp = n // 128
            mvall = small[:, 0:2 * ngrp].rearrange("p (g t) -> p g t", t=2)
            rsall = small[:, 16:16 + ngrp]
            OP("act", "activation", ["mvall", "eps"], ["rsall"], out=rsall, in_=mvall[:, :, 1], func=AF.Ln, bias=EPS_AP[0], scale=1.0)
            OP("act", "activation", ["rsall"], ["rsall"], out=rsall, in_=rsall, func=AF.Exp, scale=-0.5)
            for gi in range(ngrp):
                samp = (j == 0 and gi < 2)
                gel = A32(GEL[gi])
                gu = U("a32", GEL[gi])
                vb = A16(VB[gi])
                vu = U("a16", VB[gi])
                if not samp:
                    OP("dve", "tensor_scalar", [gu, "mvall", "rsall"], [vu], out=vb, in0=gel, scalar1=small[:, 2 * gi:2 * gi + 1],
                       scalar2=small[:, 16 + gi:17 + gi], op0=ALU.subtract, op1=ALU.mult)
                    continue
                OP("dve", "tensor_scalar", [gu, "mvall", "rsall"], [gu], out=gel, in0=gel, scalar1=small[:, 2 * gi:2 * gi + 1],
                   scalar2=small[:, 16 + gi:17 + gi], op0=ALU.subtract, op1=ALU.mult)
                OP(A_OFF, "tensor_tensor", [gu, "lntab"], [gu], out=gel, in0=gel, in1=lntab[:, 0:512], op=ALU.mult)
                OP(A_OFF, "tensor_tensor", [gu, "lntab"], [vu], out=vb, in0=gel, in1=lntab[:, 512:1024], op=ALU.add)
                if samp:
                    OP(A_OFF, "tensor_tensor", [gu, "lntab"], [U("a32", 7)], out=A32(7), in0=gel, in1=lntab[:, 512:1024],
                       op=ALU.add)
                    dstv = o_v.rearrange("p (g i f) -> p g i f", g=2, i=2)[:, gi, i, :]
                    DMA("sp", "ov", [U("a32", 7)], [], dstv, A32(7))

        def a_umm(j):
            c0, n = TILES[j]
            for c in range(4):
                b = newbank()
                for k in range(KC):
                    OP("pe", "matmul", [U("ring", sU), U("h", j, k)], [U("ps", b)], PS(b, n), lhsT=RA(sU, k, c),
                       rhs=hs(k, c0, n), start=(k == 0), stop=(k == KC - 1))
                OP("act", "activation", [U("ps", b)], [U("a32", c)], out=A32(c, n), in_=PS(b, n), func=AF.Gelu_apprx_tanh)

        def a_gating(j):
            c0, n = TILES[j]
            ngrp = n // 128
            for gi in range(ngrp):
                samp = (j == 0 and gi < 2)
                var = 1 if samp else 0
                vb = A16(VB[gi])
                for hh in range(8):
                    c = hh // 2
                    pp = (hh % 2) * 64
                    OP("pe", "matmul", [U("a16", VB[gi]), "wmt%d" % var], [U("ps", mixb[c])],
                       PS(mixb[c], 128, pp, pp + 64, gi * 128), lhsT=vb[:, hh * 64:(hh + 1) * 64],
                       rhs=wmt[:, var * 1024 + hh * 128:var * 1024 + (hh + 1) * 128], start=True, stop=True)

        def a_ya(j):
            c0, n = TILES[j]
            for c in range(4):
                tb = A32(4 + c, n)
                tu = U("a32", 4 + c)
                bt = btab[:, c * 128:(c + 1) * 128]
                t2 = ext[:, c * 128:(c + 1) * 128]
                if j == 0:
                    OP("dve", "tensor_tensor", [U("ps", mixb[c]), "btab"], [tu], out=v3(tb[:, 0:256], 4),
                       in0=v3(PS(mixb[c], 256), 4), in1=bt[:, 0:64].unsqueeze(1).broadcast_to([P, 4, 64]), op=ALU.add)
                    OP("dve", "scalar_tensor_tensor", [U("ps", mixb[c]), U("ext", 0), "ptab"], [tu], out=tb[:, 256:384],
                       in0=PS(mixb[c], 128, 0, P, 256), scalar=pt(C_ALG + i * 4 + c), in1=t2, op0=ALU.mult, op1=ALU.add)
                else:
                    OP("dve", "scalar_tensor_tensor", [U("ps", mixb[c]), U("ext", 0), "ptab"], [tu], out=v3(tb, 4),
                       in0=v3(PS(mixb[c], 512), 4), scalar=pt(C_ALG + i * 4 + c),
                       in1=t2.unsqueeze(1).broadcast_to([P, 4, 128]), op0=ALU.mult, op1=ALU.add)
                OP(A_OFF, "tensor_tensor", [tu, U("a32", c)], [U("a16", 2 + c)], out=A16(2 + c, n), in0=tb,
                   in1=A32(c, n), op=ALU.mult)

        bst_ = {}

        def zp_mm(j):
            c0, n = TILES[j]
            sPz_ = bst_["sPz"]
            for g in range(4):
                for k in range(KC):
                    OP("pe", "matmul", [U("ring", sPz_), U("h", j, k)], [U("ps", g)], PS(g, n), lhsT=RA(sPz_, k, g),
                       rhs=hs(k, c0, n), start=(k == 0), stop=(k == KC - 1))

        if not skip_norm0:
            norm_to_h(0, C_MIXG + l * 8)
        if l == 0:
            xload(2, [U("h", 0, KC - 1)])
        a_vmm(0)
        a_chain(0)
        a_umm(0)
        norm_to_h(1, C_MIXG + l * 8)
        if l == 0:
            xload(3, [U("h", 1, KC - 1)])
        for j, (c0, n) in enumerate(TILES):
            a_gating(j)
            a_ya(j)
            if j + 1 < NT:
                a_vmm(j + 1)
            else:
                dl = [U("h", 2, KC - 1)] if l == 0 else []
                bst_["sPz"] = load_slab(slab_in(w_in_even, i, 1024), 8, deps=dl)
                bst_["sOB"] = load_slab(slab_rows(w_out_even, i, 512), 4, deps=dl)
                zp_mm(0)
            wout_stage(j, sOA, [A16(2 + c, n) for c in range(4)], [U("a16", 2 + c) for c in range(4)])
            if j + 1 < NT:
                a_chain(j + 1)
                a_umm(j + 1)
                if j + 2 < NT:
                    norm_to_h(j + 2, C_MIXG + l * 8)
                    if l == 0 and j + 4 < NT:
                        xload(j + 4, [U("h", j + 2, KC - 1)])

        sPz = bst_["sPz"]
        sOB = bst_["sOB"]
        H = 15
        set_pool([4, 5, 6, 7])
        def b_evac(j):
            c0, n = TILES[j]
            hist_setup(j, H, stp_d, i)
            for g in range(4):
                ev, nseg, L = extv(g, j, H)
                eu = U("ext", g)
                OP("act", "activation", [U("ps", g)], [eu], out=ev[:, :, H:H + L], in_=v3(PS(g, n), nseg), func=AF.Copy)
                hist_mid(j, g, H)
                state_out(j, g, H, o_pool, i)

        def b_sums(j):
            c0, n = TILES[j]
            for g in range(4):
                ev, nseg, L = extv(g, j, H)
                eu = U("ext", g)
                wwin = 2 << g
                W = H + L
                bufs = [v3(A32(2 * (g % 2), nseg * W), nseg), v3(A32(2 * (g % 2) + 1, nseg * W), nseg)]
                bunits = [U("a32", 2 * (g % 2)), U("a32", 2 * (g % 2) + 1)]
                cur, cur_u = ev, eu
                sh = 1
                step = 0
                while sh < wwin:
                    lo = H - (wwin - 2 * sh)
                    dst, dst_u = bufs[step % 2], bunits[step % 2]
                    OP("dve", "tensor_tensor", [cur_u], [dst_u], out=dst[:, :, lo:W], in0=cur[:, :, lo:W],
                       in1=cur[:, :, lo - sh:W - sh], op=ALU.add)
                    cur, cur_u = dst, dst_u
                    sh *= 2
                    step += 1
                pooled = A16(g, n)
                OP("dve", "scalar_tensor_tensor", [cur_u, eu], [U("a16", g)], out=v3(pooled, nseg), in0=cur[:, :, H:H + L],
                   scalar=1.0 / wwin, in1=ev[:, :, H:H + L], op0=ALU.mult, op1=ALU.subtract)
                if j == 1:
                    tmp = small[:, 48:64]
                    OP("dve", "tensor_tensor", [cur_u, "ptab"], ["small32"], out=tmp, in0=cur[:, 0, H:H + 16],
                       in1=pt(C_ICNT + g * 16, 16), op=ALU.mult)
                    OP("dve", "tensor_tensor", ["small32", eu], [U("a16", g)], out=pooled[:, 0:16], in0=tmp,
                       in1=ev[:, 0, H:H + 16], op=ALU.subtract)

        def b_poolw(j):
            c0, n = TILES[j]
            for g in range(4):
                b2 = newbank()
                OP("pe", "matmul", ["poolw", U("a16", g)], [U("ps", b2)], PS(b2, n), lhsT=poolw[:, g * 128:(g + 1) * 128],
                   rhs=A16(g, n), start=True, stop=True)
                OP("act", "activation", [U("ps", b2), "ptab"], [U("a16", 4 + g)], out=A16(4 + g, n), in_=PS(b2, n),
                   func=AF.Copy, scale=pt(C_BSC + i * 4 + g))

        def b_wout(j):
            c0, n = TILES[j]
            wout_stage(j, sOB, [A16(4 + g, n) for g in range(4)], [U("a16", 4 + g) for g in range(4)])

        for j in range(NT):
            b_evac(j)
            if j + 1 < NT:
                zp_mm(j + 1)
            b_sums(j)
            if j >= 1:
                b_wout(j - 1)
            b_poolw(j)
        b_wout(NT - 1)

    dg_ctr = [0]

    def odd_layer(l, skip_norm0=False):
        i = l // 2
        sCv = load_slab(slab_in(w_in_odd, i, 0), 8)
        sCg = load_slab(slab_in(w_in_odd, i, 512), 8)
        sOA = load_slab(slab_rows(w_out_odd, i, 0), 4)
        H = 30
        sig_ctr = [0]
        set_pool([0, 1])
        bm, bq = 6, 7
        ext16 = ar32[:, 2 * A32W:2 * A32W + 2 * EXTW].bitcast(BF16)
        dgb = ar32[:, 5 * A32W:5 * A32W + 512].bitcast(BF16)
        dgb2 = lntab[:].bitcast(BF16)
        NDG = 16
        E16U = [U("a32", 2), U("a32", 3), U("a32", 4)]

        def e16v(g, j):
            nseg, L = (6, 64) if j == 0 else (1, 512)
            W = H + L
            return v3(ext16[:, g * EXTW:g * EXTW + nseg * W], nseg), nseg, L

        def vg(j, g, banks=None):
            c0, n = TILES[j]
            ev, nseg, L = extv(g, j, H)
            eu = U("ext", g)
            bv = banks[0] if banks else newbank()
            for k in range(KC):
                OP("pe", "matmul", [U("ring", sCv), U("h", j, k)], [U("ps", bv)], PS(bv, n), lhsT=RA(sCv, k, g),
                   rhs=hs(k, c0, n), start=(k == 0), stop=(k == KC - 1))
            bg = banks[1] if banks else newbank()
            for k in range(KC):
                OP("pe", "matmul", [U("ring", sCg), U("h", j, k)], [U("ps", bg)], PS(bg, n), lhsT=RA(sCg, k, g),
                   rhs=hs(k, c0, n), start=(k == 0), stop=(k == KC - 1))
            q = sig_ctr[0] % 2
            sig_ctr[0] += 1
            sg = A32(q, n)
            OP("act", "activation", [U("ps", bg)], [U("a32", q)], out=sg, in_=PS(bg, n), func=AF.Sigmoid)
            OP("dve", "tensor_tensor", [U("ps", bv), U("a32", q)], [eu], out=ev[:, :, H:H + L], in0=v3(PS(bv, n), nseg),
               in1=v3(sg, nseg), op=ALU.mult)
            hist_mid(j, g, H)
            state_out(j, g, H, o_c, i)
            e16, _, _ = e16v(g, j)
            OP("dve", "tensor_copy", [eu], (E16U if j == 0 else []) + [U("e16", g)], out=e16, in_=ev)

        def conv(j, g):
            c0, n = TILES[j]
            e16, nseg, L = e16v(g, j)
            wb = C_CCW + (i * 4 + g) * 31
            for k0 in range(0, 31, 4):
                nk = min(4, 31 - k0)
                gq = dg_ctr[0] % (NDG // 4)
                dg_ctr[0] += 1
                dgw = [U("dg", gq)] + (["lntab"] if first_dg[0] > 0 else [])
                first_dg[0] -= 4
                dg4 = dgb2[:, gq * 512:gq * 512 + nk * 128].rearrange("p (t m) -> p t m", t=nk)
                OP("pool", "tensor_tensor", ["ident", "ptab"], dgw, out=dg4,
                   in0=ident.unsqueeze(1).broadcast_to([P, nk, 128]),
                   in1=pt(wb + k0, nk).unsqueeze(2).broadcast_to([P, nk, 128]), op=ALU.mult)
                for t in range(nk):
                    k = k0 + t
                    OP("pe", "matmul", [U("dg", gq), U("e16", g)], [U("ps", 2 + g)], v3(PS(2 + g, n), nseg),
                       lhsT=dgb2[:, gq * 512 + t * 128:gq * 512 + (t + 1) * 128],
                       rhs=e16[:, :, k:k + L], start=(k == 0), stop=(k == 30))
            q = sig_ctr[0] % 2
            cb = A16(q, n)
            sq = A16(2 + q, n)
            OP("act", "activation", [U("ps", 2 + g), "ptab"], [U("a16", q)], out=cb, in_=PS(2 + g, n), func=AF.Identity,
               bias=pt(C_CCB + i * 4 + g), scale=1.0)
            OP("act", "activation", [U("ps", 2 + g), "ptab"], [U("a16", 2 + q)], out=sq, in_=PS(2 + g, n), func=AF.Square,
               bias=pt(C_CCB + i * 4 + g), scale=1.0)
            OP("pe", "matmul", [U("a16", q), "ones"], [U("ps", bm)], PS(bm, n), lhsT=ones[:], rhs=cb,
               start=(g == 0), stop=(g == 3))
            OP("pe", "matmul", [U("a16", 2 + q), "ones"], [U("ps", bq)], PS(bq, n), lhsT=ones[:], rhs=sq,
               start=(g == 0), stop=(g == 3))
            sig_ctr[0] += 1

        ABUF = [(btab[:, 0:512], "btab"), (wmt[:, 0:1024].bitcast(F32), "wmt0"), (wmt[:, 1024:2048].bitcast(F32), "wmt1"),
                (A32(5), U("a32", 5))]

        def ln_chain(j):
            c0, n = TILES[j]
            m2 = A32(6, n)
            mu = U("a32", 6)
            nm = A32(7, n)
            nu = U("a32", 7)
            OP("act", "activation", [U("ps", bm)], [nu], out=nm, in_=PS(bm, n), func=AF.Copy, scale=-1.0 / 512)
            OP("act", "activation", [U("ps", bm)], [mu], out=m2, in_=PS(bm, n), func=AF.Square, scale=1.0 / 512)
            for g in range(4):
                a, au = ABUF[g][0][:, 0:n], ABUF[g][1]
                OP("dve", "scalar_tensor_tensor", [U("ps", 2 + g), nu, "ptab"], [au], out=a, in0=PS(2 + g, n),
                   scalar=pt(C_CCB + i * 4 + g), in1=nm, op0=ALU.add, op1=ALU.add)
            OP("dve", "scalar_tensor_tensor", [U("ps", bq), mu], [mu], out=m2, in0=PS(bq, n), scalar=1.0 / 512, in1=m2,
               op0=ALU.mult, op1=ALU.subtract)
            OP("act", "activation", [mu, "eps"], [mu], out=m2, in_=m2, func=AF.Ln, bias=EPS_AP[0], scale=1.0)
            OP("act", "activation", [mu], [mu], out=m2, in_=m2, func=AF.Exp, scale=-0.5)

        def ln_post(j):
            c0, n = TILES[j]
            m2 = A32(6, n)
            mu = U("a32", 6)
            for g in range(4):
                a, au = ABUF[g][0][:, 0:n], ABUF[g][1]
                OP("dve", "tensor_tensor", [au, mu], [au], out=a, in0=a, in1=m2, op=ALU.mult)
                OP("act", "activation", [au, "ptab"], [U("a16", 4 + g)], out=A16(4 + g, n), in_=a, func=AF.Silu,
                   scale=pt(C_CLG + i * 4 + g), bias=pt(C_CLB + i * 4 + g))

        first_dg = [NDG]
        if not skip_norm0:
            norm_to_h(0, C_MIXG + l * 8)
        hist_setup(0, H, stc_d, i)
        vg(0, 0)
        vg(0, 1)
        conv(0, 0)
        for j, (c0, n) in enumerate(TILES):
            vg(j, 2)
            conv(j, 1)
            if j + 1 < NT:
                norm_to_h(j + 1, C_MIXG + l * 8)
            vg(j, 3)
            conv(j, 2)
            conv(j, 3)
            ln_chain(j)
            if j + 1 < NT:
                hist_setup(j + 1, H, stc_d, i)
                vg(j + 1, 0)
                vg(j + 1, 1, banks=(4, 5))
            ln_post(j)
            if j + 1 < NT:
                conv(j + 1, 0)
            nw = TS if (l == nlayers - 1 and j == 0) else n
            wout_stage(j, sOA, [A16(4 + g, nw) for g in range(4)], [U("a16", 4 + g) for g in range(4)],
                       nov=(nw if nw != n else None))

        OP("dve", "memset", [U("e16", g) for g in range(4)] + [U("dg", q) for q in range(4)],
           E16U + ["lntab"], small[:, 20:21], 0.0)
        set_pool(range(8))
        sDc = load_slab(slab_in(w_in_odd, i, 1536), 8)
        sDx = load_slab(slab_in(w_in_odd, i, 2048), 8)
        sDb = load_slab(slab_in(w_in_odd, i, 1024), 8)
        sOB = load_slab(slab_rows(w_out_odd, i, 512), 4)
        H = 2
        ctr = [0]
        def d_chunk(j, g):
            c0, n = TILES[j]
            ev, nseg, L = extv(g, j, H)
            eu = U("ext", g)
            b1 = newbank()
            for k in range(KC):
                OP("pe", "matmul", [U("ring", sDc), U("h", j, k)], [U("ps", b1)], PS(b1, n), lhsT=RA(sDc, k, g),
                   rhs=hs(k, c0, n), start=(k == 0), stop=(k == KC - 1))
            b2 = newbank()
            for k in range(KC):
                OP("pe", "matmul", [U("ring", sDx), U("h", j, k)], [U("ps", b2)], PS(b2, n), lhsT=RA(sDx, k, g),
                   rhs=hs(k, c0, n), start=(k == 0), stop=(k == KC - 1))
            b3 = newbank()
            for k in range(KC):
                OP("pe", "matmul", [U("ring", sDb), U("h", j, k)], [U("ps", b3)], PS(b3, n), lhsT=RA(sDb, k, g),
                   rhs=hs(k, c0, n), start=(k == 0), stop=(k == KC - 1))
            q = ctr[0] % 2
            ctr[0] += 1
            gcs = A32(q, n)
            OP("act", "activation", [U("ps", b1)], [U("a32", q)], out=gcs, in_=PS(b1, n), func=AF.Copy)
            OP("dve", "tensor_tensor", [U("ps", b2), U("a32", q)], [eu], out=ev[:, :, H:H + L], in0=v3(PS(b2, n), nseg),
               in1=v3(gcs, nseg), op=ALU.mult)
            hist_mid(j, g, H)
            state_out(j, g, H, o_d, i)
            acc = v3(A32(2 + q, n), nseg)
            au = U("a32", 2 + q)
            wb = C_DCW + (i * 4 + g) * 3
            OP("dve", "tensor_scalar", [eu, "ptab"], [au], out=acc, in0=ev[:, :, 0:L], scalar1=pt(wb), scalar2=None,
               op0=ALU.mult)
            for k in range(1, 3):
                OP("dve", "scalar_tensor_tensor", [eu, "ptab", au], [au], out=acc, in0=ev[:, :, k:k + L],
                   scalar=pt(wb + k), in1=acc, op0=ALU.mult, op1=ALU.add)
            di = (j % 2) * 4 + g
            OP("dve", "tensor_tensor", [U("ps", b3), au], [U("a16", di)], out=A16(di, n), in0=PS(b3, n), in1=A32(2 + q, n),
               op=ALU.mult)

        def d_wout(j):
            c0, n = TILES[j]
            par = (j % 2) * 4
            nw = TS if (l == nlayers - 1 and j == 0) else n
            wout_stage(j, sOB, [A16(par + g, nw) for g in range(4)], [U("a16", par + g) for g in range(4)],
                       nov=(nw if nw != n else None))

        for j in range(NT):
            hist_setup(j, H, std_d, i)
            for g in range(4):
                d_chunk(j, g)
                if g == 0 and j >= 1:
                    d_wout(j - 1)
                if g == 2 and j >= 1:
                    norm_to_h(j - 1, C_FFNG + l * 8)
        d_wout(NT - 1)
        norm_to_h(NT - 1, C_FFNG + l * 8)

    def ffn(l, after_last=None, after_tile0=None):
        set_pool(range(8))
        rctr = [0]
        items = [(p, j) for p in range(8) for j in range(NT)]
        slots = {}

        halo_dead = (l == nlayers - 1)

        def stage1(t):
            p, j = items[t]
            c0, n = TILES[j]
            if halo_dead and j == 0:
                n = TS
            if j == 0:
                s1 = load_slab(slab_in(w_ff1, l, 512 * p), 8)
                s2 = load_slab(slab_rows(w_ff2, l, 512 * p), 4)
                slots[p] = (s1, s2)
            s1, s2 = slots[p]
            if l % 2 == 0 and p == 0 and j == 0:
                norm_to_h(0, C_FFNG + l * 8)
            aset = t % 2
            for c in range(4):
                b = newbank()
                for k in range(KC):
                    OP("pe", "matmul", [U("ring", s1), U("h", j, k)], [U("ps", b)], PS(b, n), lhsT=RA(s1, k, c),
                       rhs=hs(k, c0, n), start=(k == 0), stop=(k == KC - 1))
                q = rctr[0] % 4
                rctr[0] += 1
                r_ = A32(q, n)
                OP("act", "activation", [U("ps", b)], [U("a32", q)], out=r_, in_=PS(b, n), func=AF.Relu)
                OP("pool" if (l % 2 == 0 and p == 0 and c % 2 == 0) else "dve", "tensor_tensor", [U("a32", q)],
                   [U("a16", aset * 4 + c)], out=A16(aset * 4 + c, n), in0=r_, in1=r_, op=ALU.mult)
            if l % 2 == 0 and p == 0 and j + 1 < NT:
                norm_to_h(j + 1, C_FFNG + l * 8)

        def stage2(t):
            p, j = items[t]
            c0, n = TILES[j]
            nov = None
            if halo_dead and j == 0:
                n = TS
                nov = TS
            s1, s2 = slots[p]
            aset = t % 2
            wout_stage(j, s2, [A16(aset * 4 + c, n) for c in range(4)], [U("a16", aset * 4 + c) for c in range(4)], nov=nov)

        for t in range(len(items) + 1):
            if t < len(items):
                stage1(t)
            if t >= 1:
                stage2(t - 1)
                pp_, jj_ = items[t - 1]
                if pp_ == 7 and after_last is not None:
                    after_last(jj_)
                if pp_ == 7 and jj_ == 0 and after_tile0 is not None:
                    after_tile0()

    epsb = sb("epsb", [P, 1], F32)
    OP("dve", "memset", [], ["eps"], epsb[:], EPS)
    EPS_AP[0] = epsb[:]
    yctr = [0]

    def final_norm(j):
        c0, n = TILES[j]

        def f(k, rstd):
            q = 4 + yctr[0] % 4
            yctr[0] += 1
            yb = A32(q, n)
            OP("dve", "scalar_tensor_tensor", [U("x", j, k), "nrs", "ptab"], [U("a32", q)], out=yb, in0=xs(k, c0, n),
               scalar=pt(C_FING + k), in1=rstd, op0=ALU.mult, op1=ALU.mult)
            DMA("sp", U("yo", q), [U("a32", q)], [], yT[:, k * T + c0:k * T + c0 + n], yb)
        rmsnorm_tile(j, f)

    for l in range(nlayers):
        if l % 2 == 0:
            even_layer(l, skip_norm0=(l > 0))
        else:
            odd_layer(l, skip_norm0=(l > 0))
        last = (l == nlayers - 1)
        ffn(l, after_last=(final_norm if last else None),
            after_tile0=(None if last else (lambda l=l: norm_to_h(0, C_MIXG + (l + 1) * 8))))

    S.emit(nc, stack)
    stack.close()
    return nc


_NC_CACHE = {}


def _prep_inputs(inp):
    f = np.float32
    xp = np.asarray(inp["x_prompt"], f)
    xsm = np.asarray(inp["x_sample"], f)
    ptab_common = np.zeros((P, NPT), f)

    def pp(v):
        v = np.asarray(v, f)
        ch = v.shape[-1] // P
        v = v.reshape(v.shape[:-1] + (ch, P))
        return np.moveaxis(v, -1, 0)

    ptab_common[:, C_MIXG:C_MIXG + 32] = pp(inp["norm_mix_g"]).reshape(P, 32)
    ptab_common[:, C_FFNG:C_FFNG + 32] = pp(inp["norm_ffn_g"]).reshape(P, 32)
    ptab_common[:, C_FING:C_FING + 8] = pp(inp["final_norm_g"]).reshape(P, 8)
    ptab_common[:, C_BSC:C_BSC + 8] = pp(inp["b_scale"]).reshape(P, 8)
    ptab_common[:, C_CCB:C_CCB + 8] = pp(inp["c_conv_b"]).reshape(P, 8)
    ptab_common[:, C_CLG:C_CLG + 8] = pp(inp["c_ln_g"]).reshape(P, 8)
    ptab_common[:, C_CLB:C_CLB + 8] = pp(inp["c_ln_b"]).reshape(P, 8)
    ccw = pp(inp["c_conv_w"])
    ptab_common[:, C_CCW:C_CCW + 248] = np.transpose(ccw, (0, 1, 3, 2)).reshape(P, 248)
    dcw = pp(inp["d_conv_w"])
    ptab_common[:, C_DCW:C_DCW + 24] = np.transpose(dcw, (0, 1, 3, 2)).reshape(P, 24)
    ptab_common[:, C_ALG:C_ALG + 8] = pp(inp["a_ln_g"]).reshape(P, 8)
    ptab_common[:, C_ALB:C_ALB + 8] = pp(inp["a_ln_b"]).reshape(P, 8)

    aws = np.asarray(inp["a_w_s"], f)
    wTm = np.transpose(aws, (3, 0, 1, 2)).reshape(P, 2 * 8 * 128)
    idx = np.arange(128) % 64
    aws_s = aws[:, :, idx][:, :, :, idx]
    wTs = np.transpose(aws_s, (3, 0, 1, 2)).reshape(P, 2 * 8 * 128)
    jj = np.arange(128)[:, None] // 64
    ii = np.arange(128)[None, :] // 64
    gmask = np.concatenate([(jj <= ii).astype(f), (jj == ii).astype(f), np.eye(128, dtype=f)], axis=1)
    abs_ = np.asarray(inp["a_b_s"], f)
    hp = (np.arange(P) // 64)
    btab = np.zeros((P, 2, 4, 128), f)
    for c in range(4):
        btab[:, :, c, :] = np.transpose(abs_[:, 2 * c + hp, :], (1, 0, 2))
    btab = btab.reshape(P, 2 * 4 * 128)
    lng = np.asarray(inp["a_ln_g"], f)
    lnb = np.asarray(inp["a_ln_b"], f)
    lntab = np.stack([lng, lnb], axis=1)[None].repeat(P, axis=0).reshape(P, 2 * 2 * 512)
    poolw = np.transpose(np.asarray(inp["b_pool_w"], f), (2, 0, 1, 3)).reshape(P, 2 * 4 * 128)

    def st(v, H):
        v = np.asarray(v, f).reshape(2, 8, 4, H, 4, P)
        v = np.transpose(v, (1, 5, 0, 2, 4, 3))
        return v.reshape(8, P, -1)

    stp = st(inp["state_pool"], 15)
    stc = st(inp["state_conv_c"], 30)
    std = st(inp["state_conv_d"], 2)

    shared = dict(wTm=np.ascontiguousarray(wTm), wTs=np.ascontiguousarray(wTs), gmask=np.ascontiguousarray(gmask),
                  btab=btab, lntab=np.ascontiguousarray(lntab), poolw=np.ascontiguousarray(poolw))
    for nme in ("w_in_even", "w_out_even", "w_in_odd", "w_out_odd", "w_ff1", "w_ff2"):
        shared[nme] = np.ascontiguousarray(np.asarray(inp[nme], f))
    in_maps = []
    for c in range(NCORE):
        b, q = c // 4, c % 4
        tok = np.zeros((T, D), f)
        tok[0:TS] = xsm[4 * c:4 * c + 4].reshape(TS, D)
        if q > 0:
            tok[TS:TS + TH] = xp[b, q * TM - TH:q * TM]
        tok[TS + TH:] = xp[b, q * TM:(q + 1) * TM]
        xin = np.ascontiguousarray(tok.reshape(T, KC, P).transpose(2, 1, 0)).reshape(P, KC * T)
        pt_ = ptab_common.copy()
        pt_[:, C_MASK] = 0.0 if q == 0 else 1.0
        for g, w in enumerate((2, 4, 8, 16)):
            if q == 0:
                cntv = np.minimum(w, np.arange(16) + 1).astype(f)
            else:
                cntv = np.full(16, w, f)
            pt_[:, C_ICNT + g * 16:C_ICNT + (g + 1) * 16] = (1.0 / cntv)[None, :]
        m = dict(shared)
        m.update(xin=xin, ptab=pt_, stp=np.ascontiguousarray(stp[c]), stc=np.ascontiguousarray(stc[c]),
                 std=np.ascontiguousarray(std[c]))
        in_maps.append(m)
    return in_maps


def kernel(**inputs):
    return _run(inputs, DEPTH)


def _run(inputs, nlayers, trace=False):
    if nlayers not in _NC_CACHE:
        _NC_CACHE[nlayers] = build_program(nlayers)
    nc = _NC_CACHE[nlayers]
    in_maps = _prep_inputs(inputs)
    if trace:
        res = run_bass_kernel_spmd(nc, in_maps, core_ids=list(range(NCORE)), trace=True)
        _NC_CACHE["last_res"] = res
    else:
        res = run_bass_kernel_spmd(nc, in_maps, core_ids=list(range(NCORE)))
    R = res.results
    f = np.float32
    y_prompt = np.zeros((2, 8192, D), f)
    y_sample = np.zeros((32, 64, D), f)
    new_pool_p = np.zeros((2, 2, 15, 512), f)
    new_c_p = np.zeros((2, 2, 30, 512), f)
    new_d_p = np.zeros((2, 2, 2, 512), f)
    new_v_s = np.zeros((2, 32, 64, 512), f)
    new_pool_s = np.zeros((2, 32, 15, 512), f)
    new_c_s = np.zeros((2, 32, 30, 512), f)
    new_d_s = np.zeros((2, 32, 2, 512), f)
    for c in range(NCORE):
        b, q = c // 4, c % 4
        y = np.asarray(R[c]["yT"]).reshape(P, KC, T).transpose(2, 1, 0).reshape(T, D)
        y_sample[4 * c:4 * c + 4] = y[0:TS].reshape(4, 64, D)
        y_prompt[b, q * TM:(q + 1) * TM] = y[TS + TH:]
        ov = np.asarray(R[c]["o_v"]).reshape(P, 2, 2, 512)
        vv = ov.transpose(2, 1, 0, 3).reshape(2, 256, 512).reshape(2, 4, 64, 512)
        new_v_s[:, 4 * c:4 * c + 4] = vv
        for arr_s, arr_p, nme, H in ((new_pool_s, new_pool_p, "o_pool", 15), (new_c_s, new_c_p, "o_c", 30),
                                     (new_d_s, new_d_p, "o_d", 2)):
            o = np.asarray(R[c][nme]).reshape(P, 2, 5, 4, H)
            o = o.transpose(1, 2, 4, 3, 0).reshape(2, 5, H, 512)
            arr_s[:, 4 * c:4 * c + 4] = o[:, 0:4]
            if q == 3:
                arr_p[:, b] = o[:, 4]
    return (y_prompt, y_sample, new_pool_p, new_c_p, new_d_p, new_v_s, new_pool_s, new_c_s, new_d_s)
```
